# Optimizing a Trainium2 kernel written in Bass

```python
import math
import jax, jax.numpy as jnp
from jax import lax
import numpy as np

D_MODEL = 1024
BATCH = 8
SEQ = 8192
DEPTH = 1

CHUNK = 64
N_META = 16
Q_BLOCK = 128
DA_HEADS = 4
DA_QK_DIM = 64
DA_V_DIM = 2 * DA_QK_DIM
DA_WIDTH = DA_HEADS * DA_V_DIM
GLA_HEADS = 4
GLA_K_DIM = 64
GLA_V_DIM = 128
GLA_WIDTH = GLA_HEADS * GLA_V_DIM
GLA_GATE_RANK = 16
GLA_GATE_TAU = 16.0
MIX_WIDTH = DA_WIDTH + GLA_WIDTH
IN_SIZES = (
    DA_HEADS * 2 * DA_QK_DIM,
    DA_HEADS * 2 * DA_QK_DIM,
    DA_HEADS * DA_V_DIM,
    GLA_HEADS * GLA_K_DIM,
    GLA_HEADS * GLA_K_DIM,
    GLA_HEADS * GLA_V_DIM,
    GLA_HEADS * GLA_V_DIM,
    GLA_GATE_RANK,
)
IN_COLS = sum(IN_SIZES)
D_FF = 2816
CONV_WIDTH = 3
RMS_EPS = 1e-6

kernel_name = "hymba_diffattn_gla_convglu_stream"


def _rmsnorm(x, w):
    xf = x.astype(jnp.float32)
    y = xf * lax.rsqrt(jnp.mean(xf * xf, axis=-1, keepdims=True) + RMS_EPS)
    return (y * w.astype(jnp.float32)).astype(x.dtype)


def _chunk_id(pos):
    return (pos - N_META) // CHUNK + 1


def _diff_attention(q, k, v, lam, subln_w, lam_init):
    f32 = jnp.float32
    B_, L, _ = q.shape
    Lp = -(-L // Q_BLOCK) * Q_BLOCK
    pad = Lp - L
    nb = Lp // Q_BLOCK

    def two_maps(t):
        t = jnp.pad(t.astype(f32), ((0, 0), (0, pad), (0, 0)))
        return t.reshape(B_, Lp, DA_HEADS, 2, DA_QK_DIM).transpose(3, 0, 2, 1, 4)

    qh = two_maps(q) * (DA_QK_DIM ** -0.5)
    kh = two_maps(k)
    k1, k2 = kh[0], kh[1]
    vh = jnp.pad(v.astype(f32), ((0, 0), (0, pad), (0, 0))).reshape(
        B_, Lp, DA_HEADS, DA_V_DIM).transpose(0, 2, 1, 3)

    pos = jnp.arange(Lp)
    kchunk = _chunk_id(pos)
    slopes = 2.0 ** (-8.0 * jnp.arange(1, DA_HEADS + 1, dtype=f32) / DA_HEADS)

    def blocks(t):
        return t.reshape(B_, DA_HEADS, nb, Q_BLOCK, -1).transpose(2, 0, 1, 3, 4)

    q1b, q2b = blocks(qh[0]), blocks(qh[1])
    starts = jnp.arange(nb) * Q_BLOCK

    def one_block(args):
        q1i, q2i, start = args
        qpos = start + jnp.arange(Q_BLOCK)
        visible = kchunk[None, :] <= _chunk_id(qpos)[:, None]
        dist = jnp.abs(qpos[:, None] - pos[None, :]).astype(f32)
        bias = -slopes[:, None, None] * dist

        def probs(qi, kk):
            s = jnp.einsum('bhqd,bhkd->bhqk', qi, kk) + bias
            s = jnp.where(visible, s, -jnp.inf)
            return jax.nn.softmax(s, axis=-1)

        p = probs(q1i, k1) - lam * probs(q2i, k2)
        return jnp.einsum('bhqk,bhkv->bhqv', p, vh)

    o = lax.map(one_block, (q1b, q2b, starts))
    o = o.transpose(1, 2, 0, 3, 4).reshape(B_, DA_HEADS, Lp, DA_V_DIM)[:, :, :L]
    o = _rmsnorm(o, subln_w) * (1.0 - lam_init)
    return o.transpose(0, 2, 1, 3).reshape(B_, L, DA_WIDTH).astype(q.dtype)


def _gla(q, k, v, r, g_lr, gate_w, gate_b, norm_w):
    f32 = jnp.float32
    B_, L, _ = q.shape
    log_a = jax.nn.log_sigmoid(g_lr.astype(f32) @ gate_w.astype(f32) + gate_b.astype(f32)) / GLA_GATE_TAU
    front = CHUNK - N_META
    back = (-(L + front)) % CHUNK
    Lg = L + front + back
    nc = Lg // CHUNK

    def heads(t, d):
        t = jnp.pad(t.astype(f32), ((0, 0), (front, back), (0, 0)))
        return t.reshape(B_, nc, CHUNK, GLA_HEADS, d).transpose(0, 3, 1, 2, 4)

    qh = heads(q, GLA_K_DIM) * (GLA_K_DIM ** -0.5)
    kh = heads(k, GLA_K_DIM)
    vh = heads(v, GLA_V_DIM)
    b = jnp.cumsum(heads(log_a, GLA_K_DIM), axis=3)
    q_dec = qh * jnp.exp(b)
    causal = jnp.tril(jnp.ones((CHUNK, CHUNK), dtype=bool))
    a = jnp.einsum('bhncd,bhnsd->bhncs', q_dec, kh * jnp.exp(-b))
    o_intra = jnp.einsum('bhncs,bhnsv->bhncv', jnp.where(causal, a, 0.0), vh)

    b_last = b[:, :, :, -1:, :]
    delta = jnp.einsum('bhncd,bhncv->bhndv', kh * jnp.exp(b_last - b), vh)
    chunk_decay = jnp.exp(b_last[:, :, :, 0, :])

    def step(state, inp):
        dec, dlt = inp
        return dec[..., None] * state + dlt, state

    s0 = jnp.zeros((B_, GLA_HEADS, GLA_K_DIM, GLA_V_DIM), f32)
    _, s_start = lax.scan(step, s0, (chunk_decay.transpose(2, 0, 1, 3),
                                     delta.transpose(2, 0, 1, 3, 4)))
    s_start = s_start.transpose(1, 2, 0, 3, 4)
    o_inter = jnp.einsum('bhncd,bhndv->bhncv', q_dec, s_start)

    o = (o_intra + o_inter).transpose(0, 2, 3, 1, 4).reshape(B_, Lg, GLA_HEADS, GLA_V_DIM)
    o = _rmsnorm(o[:, front:front + L], norm_w).reshape(B_, L, GLA_WIDTH)
    return (o * jax.nn.silu(r.astype(f32))).astype(q.dtype)


def _hybrid_mixer(h, norm_w, w_in, lq1, lk1, lq2, lk2, subln_w, gate_w, gate_b, gla_norm_w, w_out, layer):
    u = _rmsnorm(h, norm_w)
    proj = u @ w_in
    split_points = np.cumsum(IN_SIZES)[:-1].tolist()
    da_q, da_k, da_v, g_q, g_k, g_v, g_r, g_lr = jnp.split(proj, split_points, axis=-1)
    f32 = jnp.float32
    lam_init = 0.8 - 0.6 * math.exp(-0.3 * layer)
    lam = (jnp.exp(jnp.sum(lq1.astype(f32) * lk1.astype(f32)))
           - jnp.exp(jnp.sum(lq2.astype(f32) * lk2.astype(f32))) + lam_init)
    o_da = _diff_attention(da_q, da_k, da_v, lam, subln_w, lam_init)
    o_gla = _gla(g_q, g_k, g_v, g_r, g_lr, gate_w, gate_b, gla_norm_w)
    return jnp.concatenate([o_da, o_gla], axis=-1) @ w_out


def _conv_glu(h, norm_w, w_up, conv_w, conv_b, w_down):
    u = _rmsnorm(h, norm_w) @ w_up
    u = lax.conv_general_dilated(
        u, conv_w[:, None, :].astype(u.dtype), window_strides=(1,),
        padding=[(CONV_WIDTH - 1, 0)], dimension_numbers=('NWC', 'WIO', 'NWC'),
        feature_group_count=2 * D_FF) + conv_b
    val, gate = jnp.split(u, 2, axis=-1)
    return (val * jax.nn.silu(gate)) @ w_down


def setup_inputs(seed: int = 0) -> dict:
    key = jax.random.key(seed)
    ks = jax.random.split(key, 20)
    f32 = jnp.float32
    nrm = lambda k, shape, s: jax.random.normal(k, shape, f32) * s
    return {
        "x": nrm(ks[0], (BATCH, SEQ, D_MODEL), 1.0),
        "meta_tokens": nrm(ks[1], (N_META, D_MODEL), 1.0),
        "norm1_w": 1.0 + nrm(ks[2], (DEPTH, D_MODEL), 0.02),
        "w_in": nrm(ks[3], (DEPTH, D_MODEL, IN_COLS), D_MODEL ** -0.5),
        "lambda_q1": nrm(ks[4], (DEPTH, DA_QK_DIM), 0.1),
        "lambda_k1": nrm(ks[5], (DEPTH, DA_QK_DIM), 0.1),
        "lambda_q2": nrm(ks[6], (DEPTH, DA_QK_DIM), 0.1),
        "lambda_k2": nrm(ks[7], (DEPTH, DA_QK_DIM), 0.1),
        "da_subln_w": 1.0 + nrm(ks[8], (DEPTH, DA_V_DIM), 0.02),
        "gla_gate_w": nrm(ks[9], (DEPTH, GLA_GATE_RANK, GLA_HEADS * GLA_K_DIM), GLA_GATE_RANK ** -0.5),
        "gla_gate_b": nrm(ks[10], (DEPTH, GLA_HEADS * GLA_K_DIM), 0.01),
        "gla_norm_w": 1.0 + nrm(ks[11], (DEPTH, GLA_V_DIM), 0.02),
        "w_out": nrm(ks[12], (DEPTH, MIX_WIDTH, D_MODEL), MIX_WIDTH ** -0.5),
        "norm2_w": 1.0 + nrm(ks[13], (DEPTH, D_MODEL), 0.02),
        "w_up": nrm(ks[14], (DEPTH, D_MODEL, 2 * D_FF), D_MODEL ** -0.5),
        "conv_w": nrm(ks[15], (DEPTH, CONV_WIDTH, 2 * D_FF), CONV_WIDTH ** -0.5),
        "conv_b": nrm(ks[16], (DEPTH, 2 * D_FF), 0.01),
        "w_down": nrm(ks[17], (DEPTH, D_FF, D_MODEL), D_FF ** -0.5),
        "final_norm_w": 1.0 + nrm(ks[18], (D_MODEL,), 0.02),
    }


def reference(x, meta_tokens, norm1_w, w_in, lambda_q1, lambda_k1, lambda_q2, lambda_k2,
              da_subln_w, gla_gate_w, gla_gate_b, gla_norm_w, w_out, norm2_w, w_up,
              conv_w, conv_b, w_down, final_norm_w):
    b_ = x.shape[0]
    meta = jnp.broadcast_to(meta_tokens.astype(x.dtype)[None], (b_, N_META, D_MODEL))
    h = jnp.concatenate([meta, x], axis=1)
    for l in range(DEPTH):
        h = h + _hybrid_mixer(h, norm1_w[l], w_in[l], lambda_q1[l], lambda_k1[l],
                              lambda_q2[l], lambda_k2[l], da_subln_w[l], gla_gate_w[l],
                              gla_gate_b[l], gla_norm_w[l], w_out[l], l)
        h = h + _conv_glu(h, norm2_w[l], w_up[l], conv_w[l], conv_b[l], w_down[l])
    return _rmsnorm(h, final_norm_w)[:, N_META:]
```

```python
import math
from contextlib import ExitStack, suppress

import numpy as np
import concourse.bass as bass
import concourse.mybir as mybir
from concourse.bass_utils import run_bass_kernel_spmd

F32 = mybir.dt.float32
BF16 = mybir.dt.bfloat16
AF = mybir.ActivationFunctionType
ALU = mybir.AluOpType

D = 1024
NMETA = 16
INC = 3088
DFF = 2816
NCH = 44
EPS = 1e-6
SLOPES = [2.0 ** (-2.0 * (h + 1)) for h in range(4)]
LAM_INIT = 0.8 - 0.6 * math.exp(0.0)
NEG = -30000.0
SKIP_C = -100.0


class _Skip(Exception):
    pass


class DSem:
    def __init__(self, sem):
        self.sem = sem
        self.cnt = 0


class Sched:
    ENG = ("pe", "act", "dve", "pool", "sp")

    def __init__(self, nc, es):
        self.nc, self.es = nc, es
        self.q = {e: [] for e in self.ENG}
        self.sem, self.cnt = {}, {}
        self.nsem = 0
        for e in self.ENG:
            self.new_sem(e)
        self.waited = {e: {} for e in self.ENG}
        self.last_w = {}
        self.readers = {}
        self.sems = {}
        self.out_events = []
        self._grp = None
        self.dead = False
        self.kind = {}

    def group_begin(self):
        self._grp = []

    def group_end(self):
        for ev, ds in self._grp:
            ev[1] = ds.cnt
        self._grp = None

    def new_sem(self, e):
        self.nsem += 1
        self.sem[e] = self.es.enter_context(self.nc.semaphore(f"s{self.nsem}_{e}"))
        self.cnt[e] = 0

    def dsem(self, name):
        self.nsem += 1
        return DSem(self.es.enter_context(self.nc.semaphore(f"d{self.nsem}_{name}")))

    def _wait(self, eng, evs):
        need = {}
        for ev in evs:
            if ev is None:
                continue
            s, v, e = ev
            if e == "pe" and eng == "pe":
                continue
            k = id(s)
            if k not in need or need[k][1] < v:
                need[k] = (s, v)
        for k, (s, v) in need.items():
            if self.waited[eng].get(k, 0) >= v:
                continue
            self.waited[eng][k] = v
            self.q[eng].append(lambda E, s=s, v=v: E.wait_ge(s, v))

    def op(self, eng, fn, reads=(), writes=(), sig=True, dsem=None):
        if self.dead:
            return None
        preads = [k for k in reads if isinstance(k, str) and k.startswith("p_")]
        reads = [k for k in reads if not (isinstance(k, str) and k.startswith("p_"))]
        deps = []
        for k in reads:
            deps.append(self.last_w.get(k))
        for k in writes:
            deps.append(self.last_w.get(k))
            deps.extend(self.readers.get(k, {}).values())
        for k in preads:
            ev0 = self.last_w.get(k)
            if ev0 is not None and not (self.kind.get(k) == "r" and ev0[2] == eng):
                deps.append(ev0)
        self._wait(eng, deps)
        if dsem is not None:
            dsem.cnt += 16
            ev = [dsem.sem, dsem.cnt, "dma"]
            if self._grp is not None:
                self._grp.append((ev, dsem))
            self.q[eng].append(lambda E, fn=fn, s=dsem.sem: getattr(E, fn[0])(**fn[1]).then_inc(s, 16))
        elif sig:
            self.cnt[eng] += 1
            ev = [self.sem[eng], self.cnt[eng], eng]
            self.q[eng].append(lambda E, fn=fn, s=self.sem[eng]: getattr(E, fn[0])(**fn[1]).then_inc(s, 1))
        else:
            ev = [self.sem[eng], self.cnt[eng] + 1, eng]
            self.q[eng].append(lambda E, fn=fn: getattr(E, fn[0])(**fn[1]))
        for k in reads:
            self.readers.setdefault(k, {})[id(ev[0])] = ev
        for k in writes:
            self.last_w[k] = ev
            self.readers[k] = {}
            self.kind[k] = "w"
        for k in preads:
            self.last_w[k] = ev
            self.readers[k] = {}
            self.kind[k] = "r"
        return ev

    def barrier(self):
        self.dead = False
        evs = []
        for k, ev in self.last_w.items():
            evs.append(ev)
        for k, d in self.readers.items():
            evs.extend(d.values())
        for e in self.ENG:
            self._wait(e, evs)
        self.last_w = {}
        self.readers = {}

    def finish(self):
        self._wait("sp", self.out_events)

    def run(self):
        block = self.es.enter_context(self.nc.Block())
        q = self.q

        @block.tensor
        def _(E):
            for f in q["pe"]:
                f(E)

        @block.scalar
        def _(E):
            for f in q["act"]:
                f(E)

        @block.vector
        def _(E):
            for f in q["dve"]:
                f(E)

        @block.gpsimd
        def _(E):
            for f in q["pool"]:
                f(E)

        @block.sync
        def _(E):
            for f in q["sp"]:
                f(E)


def host_tables():
    t = {}
    t["c_ident"] = np.eye(128, dtype=np.float32)
    s = np.arange(128)[:, None]
    c = np.arange(128)[None, :]
    same = (s // 128) == (c // 128)
    t["c_tri"] = ((s <= c) & same).astype(np.float32)
    t["c_tri3"] = ((s > c) & same).astype(np.float32)
    db = np.zeros((128, 4, 2, 256), np.float32)
    k = np.arange(128)[:, None]
    q = np.arange(256)[None, :]
    for h in range(4):
        a = SLOPES[h]
        for dt_ in range(2):
            kb = 128 * dt_ + k
            vis = (kb // 64) <= (q // 64)
            val = -a * np.abs(q - kb) + a * q + a * (127 - k)
            db[:, h, dt_, :] = np.where(vis, val, NEG)
    t["c_dbias"] = db.reshape(128, 4 * 2 * 256)
    mb = np.full((128, 4, 16), NEG, np.float32)
    k16 = np.arange(16)[:, None]
    q16 = np.arange(16)[None, :]
    for h in range(4):
        a = SLOPES[h]
        mb[:16, h, :] = -a * np.abs(q16 - k16) + a * (15 - k16)
    t["c_mbias"] = mb.reshape(128, 64)
    dt = np.zeros((128, 8), np.float32)
    j = np.arange(128)
    for h in range(4):
        a = SLOPES[h]
        dt[:, h] = np.exp(-a * (127 - j))
        dt[:16, 4 + h] = np.exp(-a * (15 - j[:16]))
    t["c_dtab"] = dt
    return t


def build_nc(NT, debug=False, phases=3):
    TT = NT + 1
    LT = TT * 128
    NB = NT // 2
    nc = bass.Bass("TRN2", target_bir_lowering=False)

    def din(name, shape, dt=F32):
        return nc.dram_tensor(name, list(shape), dt, kind="ExternalInput").ap()

    x = din("x", [NT * 128, D])
    meta = din("meta", [NMETA, D])
    w_in = din("w_in", [D, INC])
    w_out = din("w_out", [D, D])
    w_up = din("w_up", [D, 2 * DFF])
    w_down = din("w_down", [DFF, D])
    nw1 = din("nw1", [128, D])
    nw2 = din("nw2", [128, 8])
    fnw = din("fnw", [128, D])
    lamv = din("lamv", [128, 256])
    subw = din("subw", [128, 128])
    gnw = din("gnw", [128, 128])
    gatew = din("gatew", [17, 256])
    convt = din("convt", [128, NCH * 4])
    c_ident = din("c_ident", [128, 128])
    c_tri = din("c_tri", [128, 128])
    c_tri3 = din("c_tri3", [128, 128])
    c_dbias = din("c_dbias", [128, 2048])
    c_mbias = din("c_mbias", [128, 64])
    c_dtab = din("c_dtab", [128, 8])
    out = nc.dram_tensor("out", [NT * 128, D], F32, kind="ExternalOutput").ap()

    okind = "ExternalOutput" if debug else "Internal"
    QT = nc.dram_tensor("QT", [4, 2, 128, LT], BF16, kind=okind).ap()
    KT = nc.dram_tensor("KT", [4, 128, LT], BF16, kind=okind).ap()
    VP = nc.dram_tensor("VP", [4, 128, TT, 129], BF16, kind=okind).ap()
    OG = nc.dram_tensor("OG", [LT, 512], BF16, kind=okind).ap()
    OD = nc.dram_tensor("OD", [LT, 512], BF16, kind=okind).ap()

    with ExitStack() as es:
        S = Sched(nc, es)

        def sb(name, shape, dt, st=es):
            return st.enter_context(nc.sbuf_tensor(name, list(shape), dt))

        def pst(name, shape, dt, st):
            return st.enter_context(nc.psum_tensor(name, list(shape), dt))

        ident = sb("ident", [128, 128], BF16)
        lam_t = sb("lam_t", [128, 8], F32)
        subw_t = sb("subw_t", [128, 128], F32)
        eps_t = sb("eps_t", [128, 1], F32)
        cds = S.dsem("const")

        def ld(eng, dst, src, key, ds):
            return S.op(eng, ("dma_start", dict(out=dst, in_=src)), writes=[key], dsem=ds)

        ld("pool", ident[:], c_ident, "ident", S.dsem("ident"))
        S.group_begin()
        ld("sp", subw_t[:], subw, "subw", cds)
        with ExitStack() as st0:
            lv = sb("lv", [128, 256], F32, st0)
            lj = sb("lj", [128, 128], F32, st0)
            ld("sp", lv[:], lamv, "lv", cds)
            S.group_end()
            S.op("dve", ("memset", dict(ap=eps_t[:], constant=EPS)), writes=["eps"])
            S.op("dve", ("memset", dict(ap=lam_t[:], constant=0.0)), writes=["lam"])
            S.op("dve", ("tensor_tensor", dict(out=lj[:, 0:64], in0=lv[:, 0:64], in1=lv[:, 64:128], op=ALU.mult)),
                 reads=["lv"], writes=["lj"])
            S.op("dve", ("tensor_tensor", dict(out=lj[:, 64:128], in0=lv[:, 128:192], in1=lv[:, 192:256], op=ALU.mult)),
                 reads=["lv"], writes=["lj2"])
            S.op("dve", ("reduce_sum", dict(out=lam_t[:, 0:1], in_=lj[:, 0:64], axis=mybir.AxisListType.X)),
                 reads=["lj"], writes=["lam"])
            S.op("dve", ("reduce_sum", dict(out=lam_t[:, 1:2], in_=lj[:, 64:128], axis=mybir.AxisListType.X)),
                 reads=["lj2"], writes=["lam"])
            S.op("act", ("activation", dict(out=lam_t[:, 2:4], in_=lam_t[:, 0:2], func=AF.Exp)), reads=["lam"], writes=["lam"])
            S.op("dve", ("tensor_tensor", dict(out=lam_t[:, 5:6], in0=lam_t[:, 3:4], in1=lam_t[:, 2:3], op=ALU.subtract)),
                 reads=["lam"], writes=["lam"])
            S.op("dve", ("tensor_scalar", dict(out=lam_t[:, 4:5], in0=lam_t[:, 5:6], scalar1=-LAM_INIT, scalar2=None, op0=ALU.add)),
                 reads=["lam"], writes=["lam"])
            S.op("dve", ("tensor_scalar", dict(out=subw_t[:], in0=subw_t[:], scalar1=1.0 - LAM_INIT, scalar2=None, op0=ALU.mult)),
                 reads=["subw"], writes=["subw"])
            S.barrier()
        nlam = lam_t[:, 4:5]

        def rstd_ops(ss_ap, n, key):
            S.op("act", ("activation", dict(out=ss_ap, in_=ss_ap, func=AF.Ln, bias=eps_t[0:ss_ap.shape[0], 0:1], scale=1.0 / n)),
                 reads=[key, "eps"], writes=[key])
            S.op("act", ("activation", dict(out=ss_ap, in_=ss_ap, func=AF.Exp, scale=-0.5)), reads=[key], writes=[key])

        import os as _os
        _stop = int(_os.environ.get("K_STOP", "0"))

        def ck(n):
            if _stop == n:
                S.dead = True

        with ExitStack() as st:
            st.enter_context(suppress(_Skip))
            win = sb("win", [128, 8, INC], BF16, st)
            nw1_t = sb("nw1_t", [128, D], F32, st)
            gnw_t = sb("gnw_t", [128, 128], F32, st)
            gw_t = sb("gw_t", [32, 256], F32, st)
            tri_t = sb("tri_t", [128, 128], F32, st)
            tri3_t = sb("tri3_t", [128, 128], F32, st)
            dtab = sb("dtab", [128, 8], F32, st)
            xs = [sb(f"xs{i}", [128, D], F32, st) for i in range(2)]
            junk = sb("junk", [128, D], BF16, st)
            ssn = sb("ssn", [128, 2], F32, st)
            xn = sb("xn", [128, D], BF16, st)
            xnTs = [sb(f"xnT{j}", [128, 8, 128], BF16, st) for j in range(3)]
            gjunk = sb("gjunk", [128, 512], BF16, st)
            qst = [sb(f"qst{i}", [128, 4, 2, 128], BF16, st) for i in range(2)]
            kst = [sb(f"kst{i}", [128, 4, 128], BF16, st) for i in range(2)]
            vst = [sb(f"vst{i}", [128, 4, 129], BF16, st) for i in range(2)]
            ogs = [sb(f"ogs{i}", [128, 512], BF16, st) for i in range(2)]
            glrT = sb("glrT", [32, 128], F32, st)
            la_l = [sb(f"la{j}", [128, 256], F32, st) for j in range(2)]
            e1T_l = [sb(f"e1T{j}", [128, 2, 128], F32, st) for j in range(2)]
            e2T_l = [sb(f"e2T{j}", [128, 2, 128], F32, st) for j in range(2)]
            ke_l = [sb(f"ke{j}", [128, 256], F32, st) for j in range(2)]
            kdT_l = [sb(f"kdT{j}", [128, 2, 128], BF16, st) for j in range(2)]
            qdh_l = [sb(f"qdh{j}", [128, 4, 128], BF16, st) for j in range(2)]
            qdc_l = [sb(f"qdc{j}", [128, 2, 2, 128], BF16, st) for j in range(2)]
            kec_l = [sb(f"kec{j}", [128, 2, 256], BF16, st) for j in range(2)]
            gvb_l = [sb(f"gvb{j}", [128, 512], BF16, st) for j in range(2)]
            aT_l = [sb(f"aT{j}", [128, 4, 128], BF16, st) for j in range(2)]
            Sf = sb("Sf", [128, 2, 128], F32, st)
            Sb = sb("Sb", [128, 2, 256], BF16, st)
            gss = sb("gss", [128, 4], F32, st)
            gon = sb("gon", [128, 512], F32, st)
            ger = sb("ger", [128, 512], F32, st)

            p_tr = pst("p_tr", [128, 8, 128], BF16, st)
            p_fm = [pst(f"p_fm{i}", [128, 4, 128], F32, st) for i in range(3)]
            p_tm = [pst(f"p_tm{i}", [128, 512], F32, st) for i in range(2)]
            p_g = [pst(f"p_g{i}", [128, 512], F32, st) for i in range(2)]

            wds = S.dsem("win")
            S.group_begin()
            WG = [(0, 512), (512, 1024), (1024, 1536), (1536, 2048), (2048, 2560), (2560, INC)]
            w_in_v = w_in.rearrange("(kc p) n -> p kc n", p=128)
            for gi, (c0_, c1_) in enumerate(WG):
                ld("pool", win[:, :, c0_:c1_], w_in_v[:, :, c0_:c1_], f"win{gi}", S.dsem(f"win{gi}"))

            def wkey(col0):
                for gi, (c0_, c1_) in enumerate(WG):
                    if c0_ <= col0 < c1_:
                        return [f"win{gi}"]
                raise ValueError(col0)
            ld("sp", nw1_t[:], nw1, "nw1", cds)
            ld("sp", gnw_t[:], gnw, "gnw", cds)
            ld("sp", gw_t[0:17, :], gatew, "gw", cds)
            ld("sp", tri_t[:], c_tri, "tri", cds)
            ld("sp", tri3_t[:], c_tri3, "tri3", cds)
            ld("sp", dtab[:], c_dtab, "dtab", cds)
            S.group_end()
            S.op("dve", ("memset", dict(ap=glrT[:], constant=1.0)), writes=["glrT"])
            for i in range(2):
                S.op("pool", ("memset", dict(ap=qst[i][:], constant=0.0)), writes=[f"qst{i}"])
            for j in range(2):
                S.op("pool", ("memset", dict(ap=qdh_l[j][:], constant=0.0)), writes=[f"qdh{j}"])
                S.op("pool", ("memset", dict(ap=qdc_l[j][:], constant=0.0)), writes=[f"qdc{j}"])
                S.op("pool", ("memset", dict(ap=kec_l[j][:], constant=0.0)), writes=[f"kec{j}"])
            S.op("dve", ("memset", dict(ap=Sf[:], constant=0.0)), writes=["Sf"])
            S.op("dve", ("memset", dict(ap=Sb[:], constant=0.0)), writes=["Sb"])
            S.op("pool", ("memset", dict(ap=xs[0][:], constant=0.0)), writes=["xs0"])

            xds = [S.dsem("xs0"), S.dsem("xs1")]
            sdq = [S.dsem("sq0"), S.dsem("sq1")]
            sdk = [S.dsem("sk0"), S.dsem("sk1")]
            sdv = [S.dsem("sv0"), S.dsem("sv1")]
            sdg = [S.dsem("sg0"), S.dsem("sg1")]

            def load_x(T):
                i = T % 2
                if T == 0:
                    ld("sp", xs[0][0:16, :], meta, "xs0", xds[0])
                else:
                    ld("sp", xs[i][:], x[(T - 1) * 128:T * 128, :], f"xs{i}", xds[i])

            load_x(0)
            def mk_groups(xnT, XT):
                def fm_group(pt, pk, col0, nchunk, m=128):
                    for c in range(nchunk):
                        for kc in range(8):
                            S.op("pe", ("matmul", dict(
                                out=pt[0:m, c, :], lhsT=win[:, kc, col0 + c * 128: col0 + c * 128 + m], rhs=xnT[:, kc, :],
                                start=(kc == 0), stop=(kc == 7))),
                                reads=XT + wkey(col0), writes=[pk], sig=(kc == 7 and c == nchunk - 1))
                def tm_group(pt, pk, col0, n, off=0):
                    for kc in range(8):
                        S.op("pe", ("matmul", dict(out=pt[:, off:off + n], lhsT=xnT[:, kc, :], rhs=win[:, kc, col0:col0 + n],
                                                            start=(kc == 0), stop=(kc == 7))),
                             reads=XT + wkey(col0), writes=[pk], sig=(kc == 7))
                return fm_group, tm_group

            def seg_F1(T, part="ab"):
                i = T % 2
                xk = f"xs{i}"
                xnT = xnTs[T % 3]
                XT = [f"xnTa{T % 3}", f"xnTb{T % 3}"]
                fm_group, tm_group = mk_groups(xnT, XT)
                if "a" in part:
                    if T + 1 < TT:
                        load_x(T + 1)
                    S.op("act", ("activation", dict(out=junk[:], in_=xs[i][:], func=AF.Square, accum_out=ssn[:, 0:1])),
                         reads=[xk], writes=["junk", "ssn"])
                    rstd_ops(ssn[:, 0:1], D, "ssn")
                    S.op("dve", ("scalar_tensor_tensor", dict(out=xn[:], in0=xs[i][:], scalar=ssn[:, 0:1], in1=nw1_t[:],
                                                                     op0=ALU.mult, op1=ALU.mult)),
                         reads=[xk, "ssn", "nw1"], writes=["xn"])
                if "b" not in part:
                    return
                for kc in range(8):
                    S.op("pe", ("transpose", dict(out=p_tr[:, kc, :], in_=xn[:, kc * 128:(kc + 1) * 128], identity=ident[:])),
                         reads=["xn", "ident"], writes=["p_tr"], sig=(kc == 7))
                S.op("act", ("copy", dict(out=xnT[:], in_=p_tr[:])), reads=["p_tr"], writes=XT)

            def seg_F2(T, part="qk"):
                i = T % 2
                xk = f"xs{i}"
                xnT = xnTs[T % 3]
                XT = [f"xnTa{T % 3}", f"xnTb{T % 3}"]
                fm_group, tm_group = mk_groups(xnT, XT)

                if "q" in part:
                    fm_group(p_fm[0], "p_fm0", 0, 4)
                    for m in range(2):
                        S.op("act" if m == 0 else "dve",
                             (("copy", dict(out=qst[i][m * 64:(m + 1) * 64, :, m, :], in_=p_fm[0][m * 64:(m + 1) * 64, :, :]))) if m == 0 else
                             (("tensor_copy", dict(out=qst[i][m * 64:(m + 1) * 64, :, m, :], in_=p_fm[0][m * 64:(m + 1) * 64, :, :]))),
                             reads=["p_fm0"], writes=[f"qst{i}"])
                    S.op("sp", ("dma_start", dict(out=QT[:, :, :, T * 128:(T + 1) * 128].rearrange("h m p t -> p h m t"), in_=qst[i][:])),
                         reads=[f"qst{i}"], writes=[("QT", T)], dsem=sdq[i])
                if "k" not in part:
                    return
                fm_group(p_fm[0], "p_fm0", 512, 4)
                S.op("act", ("copy", dict(out=kst[i][:], in_=p_fm[0][:])), reads=["p_fm0"], writes=[f"kst{i}"])
                S.op("sp", ("dma_start", dict(out=KT[:, :, T * 128:(T + 1) * 128].rearrange("h p t -> p h t"), in_=kst[i][:])),
                     reads=[f"kst{i}"], writes=[("KT", T)], dsem=sdk[i])

            def seg_F3(T):
                i = T % 2
                xk = f"xs{i}"
                xnT = xnTs[T % 3]
                XT = [f"xnTa{T % 3}", f"xnTb{T % 3}"]
                fm_group, tm_group = mk_groups(xnT, XT)
                pv0 = p_fm[0][:].rearrange("p a t -> p (a t)")
                tm_group(pv0, "p_fm0", 1024, 512)
                dcol = 4 if T == 0 else 0
                for h in range(4):
                    S.op("dve", ("tensor_scalar", dict(out=vst[i][:, h, 0:128], in0=pv0[:, h * 128:(h + 1) * 128],
                                                              scalar1=dtab[:, dcol + h:dcol + h + 1], scalar2=None, op0=ALU.mult)),
                         reads=["p_fm0", "dtab"], writes=[f"vst{i}"])
                S.op("dve", ("tensor_copy", dict(out=vst[i][:, :, 128], in_=dtab[:, dcol:dcol + 4])),
                     reads=["dtab"], writes=[f"vst{i}"])
                S.op("sp", ("dma_start", dict(out=VP[:, :, T, :].rearrange("h p c -> p h c"), in_=vst[i][:])),
                     reads=[f"vst{i}"], writes=[("VP", T)], dsem=sdv[i])


            def seg_G1(T, part="all"):
                i = T % 2
                xk = f"xs{i}"
                xnT = xnTs[T % 3]
                XT = [f"xnTa{T % 3}", f"xnTb{T % 3}"]
                fm_group, tm_group = mk_groups(xnT, XT)
                pd = p_fm[1][:].rearrange("p a t -> p (a t)")
                la = la_l[i]
                e1T = e1T_l[i]
                e2T = e2T_l[i]
                ke = ke_l[i]
                kdT = kdT_l[i]
                qdh = qdh_l[i]
                qdc = qdc_l[i]
                kec = kec_l[i]
                gvb = gvb_l[i]
                aT = aT_l[i]
                if part != "rest":
                    for kc in range(8):
                        S.op("pe", ("matmul", dict(out=p_g[0][0:16, 0:128], lhsT=win[:, kc, 3072:3088], rhs=xnT[:, kc, :],
                                                            start=(kc == 0), stop=(kc == 7))),
                             reads=XT + wkey(3072), writes=["p_g0"], sig=(kc == 7))
                    S.op("act", ("copy", dict(out=glrT[0:16, :], in_=p_g[0][0:16, 0:128])), reads=["p_g0"], writes=["glrT"])
                if part == "glr":
                    return
                fm_group(p_fm[2], "p_fm2", 1536, 4)
                S.op("pe", ("matmul", dict(out=p_g[1][:, 0:256], lhsT=glrT[0:17, :], rhs=gw_t[0:17, :], start=True, stop=True)),
                     reads=["glrT", "gw"], writes=["p_g1"])
                S.op("act", ("activation", dict(out=la[:], in_=p_g[1][:, 0:256], func=AF.Exp, scale=-1.0)),
                     reads=["p_g1"], writes=[f"la{i}"])
                S.op("act", ("activation", dict(out=la[:], in_=la[:], func=AF.Ln, bias=1.0)), reads=[f"la{i}"], writes=[f"la{i}"])

            def seg_G2(T, part="ab"):
                i = T % 2
                xk = f"xs{i}"
                xnT = xnTs[T % 3]
                XT = [f"xnTa{T % 3}", f"xnTb{T % 3}"]
                fm_group, tm_group = mk_groups(xnT, XT)
                pd = p_fm[1][:].rearrange("p a t -> p (a t)")
                la = la_l[i]
                e1T = e1T_l[i]
                e2T = e2T_l[i]
                ke = ke_l[i]
                kdT = kdT_l[i]
                qdh = qdh_l[i]
                qdc = qdc_l[i]
                kec = kec_l[i]
                gvb = gvb_l[i]
                aT = aT_l[i]
                if "a" in part:
                    tm_group(p_tm[0], "p_tm0", 2048, 512)
                    S.op("act", ("copy", dict(out=gvb[:], in_=p_tm[0][:])), reads=["p_tm0"], writes=[f"gvb{i}"])
                    kr = 16 if T == 0 else 128
                    for hp in range(2):
                        S.op("pe", ("matmul", dict(out=p_g[0][:, hp * 128:(hp + 1) * 128], lhsT=la[0:kr, hp * 128:(hp + 1) * 128],
                                                            rhs=tri_t[0:kr, :], start=True, stop=True)),
                             reads=[f"la{i}", "tri"], writes=["p_g0"], sig=(hp == 1))
                    S.op("pe", ("matmul", dict(out=p_g[1][:, 256:512], lhsT=tri3_t[0:kr, :], rhs=la[0:kr, :], start=True, stop=True)),
                         reads=[f"la{i}", "tri3"], writes=["p_g1"])
                    p_g0v = p_g[0][:, 0:256].rearrange("p (a t) -> p a t", a=2)
                    S.op("act", ("activation", dict(out=e1T[:], in_=p_g0v, func=AF.Exp, scale=-1.0 / 16)), reads=["p_g0"], writes=[f"e1T{i}"])
                    S.op("act", ("activation", dict(out=e2T[:], in_=p_g0v, func=AF.Exp, scale=1.0 / 16)), reads=["p_g0"], writes=[f"e2T{i}"])
                    S.op("act", ("activation", dict(out=ke[:], in_=p_g[1][:, 256:512], func=AF.Exp, scale=-1.0 / 16)),
                         reads=["p_g1"], writes=[f"ke{i}"])
                    tm_group(p_tm[0], "p_tm0", 1792, 256)
                    S.op("dve", ("tensor_tensor", dict(out=kec[:, 0, :], in0=p_tm[0][:, 0:256], in1=ke[:], op=ALU.mult)),
                         reads=["p_tm0", f"ke{i}"], writes=[f"kec{i}"])
                    S.op("dve", ("tensor_tensor", dict(out=kdT[:], in0=p_fm[2][:, 2:4, :], in1=e2T[:], op=ALU.mult)),
                         reads=["p_fm2", f"e2T{i}"], writes=[f"kdT{i}"])
                    for h in range(4):
                        hp, hl = h // 2, h % 2
                        S.op("dve", ("scalar_tensor_tensor", dict(
                            out=qdh[hl * 64:(hl + 1) * 64, h, :], in0=p_fm[2][hl * 64:(hl + 1) * 64, hp, :], scalar=0.125,
                            in1=e1T[hl * 64:(hl + 1) * 64, hp, :], op0=ALU.mult, op1=ALU.mult)),
                            reads=["p_fm2", f"e1T{i}"], writes=[f"qdh{i}"])
                    S.op("dve", ("scalar_tensor_tensor", dict(
                        out=qdc[:, 0, :, :], in0=p_fm[2][:, 0:2, :], scalar=0.125, in1=e1T[:], op0=ALU.mult, op1=ALU.mult)),
                        reads=["p_fm2", f"e1T{i}"], writes=[f"qdc{i}"])
                if "b" not in part:
                    return
                p_g1v = p_g[1][:].rearrange("p (a t) -> p a t", a=4)
                for h in range(4):
                    S.op("pe", ("matmul", dict(out=p_g1v[:, h, :], lhsT=kdT[:, h // 2, :], rhs=qdh[:, h, :], start=True, stop=True)),
                         reads=[f"kdT{i}", f"qdh{i}"], writes=["p_g1"], sig=(h == 3))
                for h in range(4):
                    S.op("dve", ("tensor_tensor", dict(out=aT[:, h, :], in0=p_g1v[:, h, :], in1=tri_t[:], op=ALU.mult)),
                         reads=["p_g1", "tri"], writes=[f"aT{i}"])

            def seg_G3(T):
                i = T % 2
                xk = f"xs{i}"
                xnT = xnTs[T % 3]
                XT = [f"xnTa{T % 3}", f"xnTb{T % 3}"]
                fm_group, tm_group = mk_groups(xnT, XT)
                pd = p_fm[1][:].rearrange("p a t -> p (a t)")
                la = la_l[i]
                e1T = e1T_l[i]
                e2T = e2T_l[i]
                ke = ke_l[i]
                kdT = kdT_l[i]
                qdh = qdh_l[i]
                qdc = qdc_l[i]
                kec = kec_l[i]
                gvb = gvb_l[i]
                aT = aT_l[i]
                nch = 1
                for hp in range(2):
                    for hl in range(2):
                        h = 2 * hp + hl
                        S.op("pe", ("matmul", dict(out=p_tm[1][:, h * 128:(h + 1) * 128], lhsT=aT[:, h, :], rhs=gvb[:, h * 128:(h + 1) * 128],
                                                          start=(h == 0), stop=(T == 0), skip_group_check=True)),
                             reads=[f"aT{i}", f"gvb{i}"], writes=["p_tm1"], sig=False)
                for cc in range(nch):
                    if T > 0:
                        for hp in range(2):
                            S.op("pe", ("matmul", dict(out=p_tm[1][:, hp * 256:(hp + 1) * 256], lhsT=qdc[:, cc, hp, :], rhs=Sb[:, hp, :],
                                                                       start=False, stop=True, skip_group_check=True)),
                                 reads=[f"qdc{i}", "Sb"], writes=["p_tm1"], sig=(hp == 1))
                    for hp in range(2):
                        S.op("pe", ("matmul", dict(out=pd[:, hp * 256:(hp + 1) * 256], lhsT=kec[:, cc, hp * 128:(hp + 1) * 128],
                                                                   rhs=gvb[:, hp * 256:(hp + 1) * 256], start=True, stop=True)),
                             reads=[f"kec{i}", f"gvb{i}"], writes=["p_fm1"], sig=(hp == 1))
                    dc = 15 if T == 0 else 127
                    for hp in range(2):
                        for hl in range(2):
                            S.op("dve", ("scalar_tensor_tensor", dict(
                                out=Sb[hl * 64:(hl + 1) * 64, hp, hl * 128:(hl + 1) * 128], in0=Sf[hl * 64:(hl + 1) * 64, hp, :],
                                scalar=e1T[hl * 64:(hl + 1) * 64, hp, dc:dc + 1],
                                in1=pd[hl * 64:(hl + 1) * 64, hp * 256 + hl * 128: hp * 256 + (hl + 1) * 128],
                                op0=ALU.mult, op1=ALU.add)),
                                reads=["Sf", f"e1T{i}", "p_fm1"], writes=["Sb"])
                            S.op("dve", ("scalar_tensor_tensor", dict(
                                out=Sf[hl * 64:(hl + 1) * 64, hp, :], in0=Sf[hl * 64:(hl + 1) * 64, hp, :],
                                scalar=e1T[hl * 64:(hl + 1) * 64, hp, dc:dc + 1],
                                in1=pd[hl * 64:(hl + 1) * 64, hp * 256 + hl * 128: hp * 256 + (hl + 1) * 128],
                                op0=ALU.mult, op1=ALU.add)),
                                reads=["Sf", f"e1T{i}", "p_fm1"], writes=["Sf"])

            def seg_G4(T):
                i = T % 2
                xk = f"xs{i}"
                xnT = xnTs[T % 3]
                XT = [f"xnTa{T % 3}", f"xnTb{T % 3}"]
                fm_group, tm_group = mk_groups(xnT, XT)
                pd = p_fm[1][:].rearrange("p a t -> p (a t)")
                la = la_l[i]
                e1T = e1T_l[i]
                e2T = e2T_l[i]
                ke = ke_l[i]
                kdT = kdT_l[i]
                qdh = qdh_l[i]
                qdc = qdc_l[i]
                kec = kec_l[i]
                gvb = gvb_l[i]
                aT = aT_l[i]
                for h in range(4):
                    S.op("act", ("activation", dict(out=gjunk[:, h * 128:(h + 1) * 128], in_=p_tm[1][:, h * 128:(h + 1) * 128],
                                                           func=AF.Square, accum_out=gss[:, h:h + 1])),
                         reads=["p_tm1"], writes=["gjunk", "gss"])
                rstd_ops(gss[:, 0:4], 128, "gss")
                for h in range(4):
                    S.op("dve", ("scalar_tensor_tensor", dict(out=gon[:, h * 128:(h + 1) * 128], in0=p_tm[1][:, h * 128:(h + 1) * 128],
                                                                     scalar=gss[:, h:h + 1], in1=gnw_t[:], op0=ALU.mult, op1=ALU.mult)),
                         reads=["p_tm1", "gss", "gnw"], writes=["gon"])
                tm_group(pd, "p_fm1", 2560, 512)
                S.op("act", ("activation", dict(out=ger[:], in_=pd, func=AF.Exp, scale=-1.0)), reads=["p_fm1"], writes=["ger"])
                S.op("act", ("activation", dict(out=ger[:], in_=ger[:], func=AF.Ln, bias=1.0)), reads=["ger"], writes=["ger"])
                S.op("act", ("activation", dict(out=ger[:], in_=ger[:], func=AF.Exp, scale=-1.0)), reads=["ger"], writes=["ger"])
                S.op("dve", ("tensor_tensor", dict(out=ger[:], in0=ger[:], in1=pd, op=ALU.mult)),
                     reads=["ger", "p_fm1"], writes=["ger"])
                S.op("pool", ("tensor_tensor", dict(out=ogs[i][:], in0=gon[:], in1=ger[:], op=ALU.mult)),
                     reads=["gon", "ger"], writes=[f"ogs{i}"])
                S.op("sp", ("dma_start", dict(out=OG[T * 128:(T + 1) * 128, :], in_=ogs[i][:])),
                     reads=[f"ogs{i}"], writes=[("OG", T)], dsem=sdg[i])

            seg_F1(0)
            seg_F2(0)
            seg_F3(0)
            seg_G1(0)
            seg_G2(0)
            if TT > 1:
                seg_F1(1)
                seg_F2(1)
                seg_F3(1)
                seg_G1(1, "glr")
            for T in range(TT):
                if T + 2 < TT:
                    seg_F1(T + 2, "a")
                if T + 1 < TT:
                    seg_G1(T + 1, "rest")
                seg_G3(T)
                if T + 2 < TT:
                    seg_F1(T + 2, "b")
                if T + 1 < TT:
                    seg_G2(T + 1, "a")
                if T + 2 < TT:
                    seg_F2(T + 2, "q")
                if T + 1 < TT:
                    seg_G2(T + 1, "b")
                if T + 2 < TT:
                    seg_F2(T + 2, "k")
                seg_G4(T)
                if T + 2 < TT:
                    seg_F3(T + 2)
                    seg_G1(T + 2, "glr")
            S.barrier()

        wout = sb("wout", [128, 8, D], BF16)
        wup = sb("wup", [128, 8, 2 * DFF], BF16)
        nw2c = sb("nw2c", [128, 8], F32)
        if phases >= 3:
            w2s = S.dsem("w2")
            S.group_begin()
            ld("sp", nw2c[:], nw2, "nw2c", S.dsem("nw2c"))
            for kc in range(8):
                ld("pool", wout[:, kc, :], w_out[kc * 128:(kc + 1) * 128, :], f"wout{kc}", w2s)
            for kc in range(8):
                for hh in range(2):
                    ld("pool", wup[:, kc, hh * DFF:(hh + 1) * DFF], w_up[kc * 128:(kc + 1) * 128, hh * DFF:(hh + 1) * DFF], f"wup{kc}_{hh}", w2s)
            S.group_end()

        with ExitStack() as st:
            st.enter_context(suppress(_Skip))
            if phases < 2:
                raise _Skip()
            KTh_l = [sb(f"KTh{j}", [128, LT], BF16, st) for j in range(2)]
            VPh_l = [sb(f"VPh{j}", [128, TT, 129], BF16, st) for j in range(2)]
            dbias = sb("dbias", [128, 4, 2, 256], F32, st)
            mbias = sb("mbias", [128, 4, 16], F32, st)
            qb = [sb(f"qb{i}", [128, 2, 256], BF16, st) for i in range(2)]
            NPT = 4
            pt = [sb(f"pt{i}", [128, 512], BF16, st) for i in range(NPT)]
            sbias = [sb(f"sbias{i}", [128, 512], F32, st) for i in range(2)]
            otmp = sb("otmp", [128, 128], F32, st)
            ofin = sb("ofin", [128, 2, 128], F32, st)
            zr = sb("zr", [128, 2, 4], F32, st)
            oss = sb("oss", [128, 2], F32, st)
            ods = [sb(f"ods{i}", [128, 128], BF16, st) for i in range(2)]
            p_s = [pst(f"p_s{i}", [128, 512], F32, st) for i in range(4)]
            p_o_full = [pst(f"p_o{i}", [128, 512], F32, st) for i in range(4)]
            p_o = [t[:, 0:258].rearrange("p (a c) -> p a c", a=2) for t in p_o_full]

            S.group_begin()
            ld("sp", dbias[:].rearrange("p a b c -> p (a b c)"), c_dbias, "dbias", cds)
            ld("sp", mbias[:].rearrange("p a b -> p (a b)"), c_mbias, "mbias", cds)
            S.group_end()
            kds = [S.dsem("kk0"), S.dsem("kk1")]
            vds = [S.dsem("vv0"), S.dsem("vv1")]

            def load_kv(hh):
                ld("sp", KTh_l[hh % 2][:], KT[hh], f"KTh{hh % 2}", kds[hh % 2])
                ld("sp", VPh_l[hh % 2][:], VP[hh], f"VPh{hh % 2}", vds[hh % 2])

            load_kv(0)
            qds = [S.dsem("qb0"), S.dsem("qb1")]
            ods_s = [S.dsem("od0"), S.dsem("od1")]
            uctr = [0, 0, 0, 0]

            LA = 3
            for h in range(4):
                a = SLOPES[h]
                if h == 1 and phases >= 3:
                    for kc in range(8):
                        for hh in range(2):
                            S.op("act", ("activation", dict(out=wup[:, kc, hh * DFF:(hh + 1) * DFF], in_=wup[:, kc, hh * DFF:(hh + 1) * DFF],
                                                           func=AF.Copy, scale=nw2c[:, kc:kc + 1])),
                                 reads=["nw2c"], writes=[f"wup{kc}_{hh}"])
                KTh, VPh = KTh_l[h % 2], VPh_l[h % 2]
                KTk, VPk = f"KTh{h % 2}", f"VPh{h % 2}"
                if h + 1 < 4:
                    load_kv(h + 1)
                blks = []
                for I in [-1] + list(range(NB)):
                    bi = uctr[1] % 2
                    uctr[1] += 1
                    units = []
                    if I < 0:
                        units.append((0, 16, "mdiag", 0.0))
                    else:
                        cm = -a * (256 * I + 1)
                        if cm >= SKIP_C:
                            units.append((0, 16, "past", cm))
                        for Tk in range(1, 2 * I + 1):
                            c = -a * (256 * I - 128 * (Tk - 1) - 127)
                            if c < SKIP_C:
                                continue
                            units.append((Tk, 128, "past", c))
                        units.append((2 * I + 1, 128, "diag0", 0.0))
                        units.append((2 * I + 2, 128, "diag1", 0.0))
                    blks.append(dict(I=I, bi=bi, pob=bi * 2, nq=(16 if I < 0 else 256), nqt=(1 if I < 0 else 2),
                                     c0=(0 if I < 0 else 128 + 256 * I), units=units))
                flat = []
                for bidx, bl in enumerate(blks):
                    for ui, un in enumerate(bl["units"]):
                        flat.append((bidx, ui, un))

                def load_q(bl):
                    bi, c0, nq = bl["bi"], bl["c0"], bl["nq"]
                    S.op("sp", ("dma_start", dict(out=qb[bi][:, :, 0:nq], in_=QT[h, :, :, c0:c0 + nq].rearrange("m p t -> p m t"))),
                         writes=[f"qb{bi}"], dsem=qds[bi])

                def st_qk(fi):
                    bidx, ui, (Tk, nk, kind, c) = flat[fi]
                    bl = blks[bidx]
                    if ui == 0 and bidx + 1 < len(blks):
                        load_q(blks[bidx + 1])
                    u = uctr[0] + fi
                    ps_, psk = p_s[u % 4], f"p_s{u % 4}"
                    bi, nq = bl["bi"], bl["nq"]
                    if nq == 256:
                        S.op("pe", ("matmul", dict(out=ps_[0:nk, 0:512], lhsT=KTh[:, Tk * 128:Tk * 128 + nk],
                                                  rhs=qb[bi][:].rearrange("p m t -> p (m t)"), start=True, stop=True)),
                             reads=[KTk, f"qb{bi}"], writes=[psk])
                    else:
                        for m in range(2):
                            S.op("pe", ("matmul", dict(out=ps_[0:nk, m * 256:m * 256 + nq], lhsT=KTh[:, Tk * 128:Tk * 128 + nk], rhs=qb[bi][:, m, 0:nq],
                                                      start=True, stop=True)),
                                 reads=[KTk, f"qb{bi}"], writes=[psk], sig=(m == 1))

                def st_exp_pv(fi):
                    bidx, ui, (Tk, nk, kind, c) = flat[fi]
                    bl = blks[bidx]
                    I, bi, pob, nq, nqt = bl["I"], bl["bi"], bl["pob"], bl["nq"], bl["nqt"]
                    u = uctr[0] + fi
                    ps_, psk = p_s[u % 4], f"p_s{u % 4}"
                    ptt, ptk = pt[u % NPT], f"pt{u % NPT}"
                    if kind == "past":
                        S.op("act", ("activation", dict(out=ptt[0:nk, :], in_=ps_[0:nk, :], func=AF.Exp, bias=float(c), scale=0.125)),
                             reads=[psk], writes=[ptk])
                    elif kind == "mdiag":
                        sbt, sbk = sbias[u % 2], f"sbias{u % 2}"
                        for m in range(2):
                            S.op("dve", ("scalar_tensor_tensor", dict(
                                out=sbt[0:16, m * 256:m * 256 + 16], in0=ps_[0:16, m * 256:m * 256 + 16], scalar=0.125, in1=mbias[0:16, h, :],
                                op0=ALU.mult, op1=ALU.add)), reads=[psk, "mbias"], writes=[sbk])
                        for m in range(2):
                            S.op("act", ("activation", dict(out=ptt[0:16, m * 256:m * 256 + 16], in_=sbt[0:16, m * 256:m * 256 + 16], func=AF.Exp)),
                                 reads=[sbk], writes=[ptk])
                    else:
                        dt_ = 0 if kind == "diag0" else 1
                        sbt, sbk = sbias[u % 2], f"sbias{u % 2}"
                        for m in range(2):
                            S.op("dve", ("scalar_tensor_tensor", dict(
                                out=sbt[:, m * 256:(m + 1) * 256], in0=ps_[:, m * 256:(m + 1) * 256], scalar=0.125, in1=dbias[:, h, dt_, :],
                                op0=ALU.mult, op1=ALU.add)), reads=[psk, "dbias"], writes=[sbk])
                        S.op("act", ("activation", dict(out=ptt[:], in_=sbt[:], func=AF.Exp)), reads=[sbk], writes=[ptk])
                    first = (ui == 0)
                    last = (ui == len(bl["units"]) - 1)
                    mq = 16 if I < 0 else 128
                    for qt in range(nqt):
                        if kind == "diag1" and qt == 0:
                            continue
                        lastq = last or (kind == "diag0" and qt == 0)
                        for m in range(2):
                            S.op("pe", ("matmul", dict(
                                out=p_o[pob + qt][0:mq, m, :], lhsT=ptt[0:nk, m * 256 + qt * 128: m * 256 + qt * 128 + mq], rhs=VPh[0:nk, Tk, :],
                                start=(first and m == 0), stop=lastq, skip_group_check=True)),
                                reads=[ptk, VPk], writes=[f"p_o{pob + qt}"], sig=(m == 1))
                    if not last:
                        return None

                    def epilogue(I=I, pob=pob, nqt=nqt, mq=mq):
                        for qt in range(nqt):
                            po = p_o[pob + qt]
                            pok = f"p_o{pob + qt}"
                            S.op("dve", ("reciprocal", dict(out=zr[0:mq, qt, 0:2], in_=po[0:mq, :, 128])), reads=[pok], writes=["zr"])
                            S.op("dve", ("tensor_tensor", dict(out=zr[0:mq, qt, 2:3], in0=zr[0:mq, qt, 1:2], in1=nlam[0:mq, :], op=ALU.mult)),
                                 reads=["zr"], writes=["zr"])
                            S.op("dve", ("tensor_scalar", dict(out=otmp[0:mq, :], in0=po[0:mq, 1, 0:128], scalar1=zr[0:mq, qt, 2:3], scalar2=None, op0=ALU.mult)),
                                 reads=[pok, "zr"], writes=["otmp"])
                            S.op("dve", ("scalar_tensor_tensor", dict(out=ofin[0:mq, qt, :], in0=po[0:mq, 0, 0:128], scalar=zr[0:mq, qt, 0:1], in1=otmp[0:mq, :],
                                                                     op0=ALU.mult, op1=ALU.add)),
                                 reads=[pok, "zr", "otmp"], writes=[f"ofin{qt}"])
                            S.op("dve", ("scalar_tensor_tensor", dict(out=otmp[0:mq, :], in0=ofin[0:mq, qt, :], scalar=1.0, in1=ofin[0:mq, qt, :],
                                                                     op0=ALU.mult, op1=ALU.mult, accum_out=oss[0:mq, qt:qt + 1])),
                                 reads=[f"ofin{qt}"], writes=["otmp", "oss"])
                        rstd_ops(oss[0:mq, 0:nqt], 128, "oss")
                        for qt in range(nqt):
                            oi = uctr[2] % 2
                            uctr[2] += 1
                            S.op("dve", ("scalar_tensor_tensor", dict(out=ods[oi][0:mq, :], in0=ofin[0:mq, qt, :], scalar=oss[0:mq, qt:qt + 1], in1=subw_t[0:mq, :],
                                                                     op0=ALU.mult, op1=ALU.mult)),
                                 reads=[f"ofin{qt}", "oss", "subw"], writes=[f"ods{oi}"])
                            r0 = 0 if I < 0 else 128 + 256 * I + 128 * qt
                            S.op("sp", ("dma_start", dict(out=OD[r0:r0 + mq, h * 128:(h + 1) * 128], in_=ods[oi][0:mq, :])),
                                 reads=[f"ods{oi}"], writes=[("OD", h, r0)], dsem=ods_s[oi])
                    return epilogue

                load_q(blks[0])
                n = len(flat)
                pending = []
                for idx in range(n + LA):
                    if idx < n:
                        st_qk(idx)
                    if idx - LA >= 0:
                        fb, fu, _ = flat[idx - LA]
                        if fu == 0:
                            keep = []
                            for pe_ in pending:
                                if pe_[2] == blks[fb]["pob"]:
                                    pe_[1]()
                                else:
                                    keep.append(pe_)
                            pending[:] = keep
                        ep = st_exp_pv(idx - LA)
                        if ep is not None:
                            pending.append((idx + 5, ep, blks[fb]["pob"]))
                    while pending and pending[0][0] <= idx:
                        pending.pop(0)[1]()
                for pe_ in pending:
                    pe_[1]()
                uctr[0] += n
            S.barrier()

        with ExitStack() as st:
            st.enter_context(suppress(_Skip))
            if phases < 3:
                raise _Skip()
            wdn = sb("wdn", [128, 22, D], BF16, st)
            fnw_t = sb("fnw_t", [128, D], F32, st)
            cv = sb("cv", [128, NCH, 4], F32, st)
            mix = sb("mix", [128, D], BF16, st)
            mixT = sb("mixT", [128, 8, 128], BF16, st)
            h1 = [[sb(f"h1_{p}_{t}", [128, D], F32, st) for t in range(2)] for p in range(2)]
            xn2 = sb("xn2", [128, D], BF16, st)
            x2Ts = [sb(f"x2T{j}", [128, 8, 258], BF16, st) for j in range(2)]
            yv = [sb(f"yv{i}", [128, 256], F32, st) for i in range(3)]
            yg = [sb(f"yg{i}", [128, 256], F32, st) for i in range(3)]
            actT = sb("actT", [128, 22, 256], BF16, st)
            ss2 = sb("ss2", [128, 2], F32, st)
            p_t = pst("p_t", [128, 8, 128], BF16, st)
            p_op = pst("p_op", [128, 512], F32, st)
            p_u = [pst(f"p_u{i}", [128, 512], F32, st) for i in range(6)]
            p_d = [p_u[4], p_u[5]]

            w3s = S.dsem("w3")
            S.group_begin()
            ld("sp", fnw_t[:], fnw, "fnw", cds)
            ld("sp", cv[:].rearrange("p a b -> p (a b)"), convt, "cv", cds)
            for c in range(22):
                ld("pool", wdn[:, c, :], w_down[c * 128:(c + 1) * 128, :], f"wdn{c}", w3s)
            S.group_end()
            WOUT = [f"wout{kc}" for kc in range(8)]
            WUP = [f"wup{kc}_{hh}" for kc in range(8) for hh in range(2)]
            WDN = [f"wdn{c}" for c in range(22)]
            S.op("dve", ("memset", dict(ap=x2Ts[0][:, :, 0:2], constant=0.0)), writes=["x2T0", "x2Tb0"])
            S.op("pool", ("memset", dict(ap=h1[0][0][:], constant=0.0)), writes=["h1_0_0_0", "h1_0_0_1"])
            S.op("pool", ("memset", dict(ap=mix[:], constant=0.0)), writes=["mix", "mixb"])

            mds = S.dsem("mix")
            hds = [[S.dsem(f"h1{p}{t}") for t in range(2)] for p in range(2)]
            ods2 = [[S.dsem(f"o{p}{t}") for t in range(2)] for p in range(2)]
            blocks2 = [[0]] + [[2 * B + 1, 2 * B + 2] for B in range(NB)]

            def front_hist(bidx):
                par = bidx % 2
                if bidx > 0:
                    pn = 16 if bidx == 1 else 256
                    S.op("pool", ("tensor_copy", dict(out=x2Ts[par][:, :, 0:2], in_=x2Ts[1 - par][:, :, pn:pn + 2])),
                         reads=[f"x2T{1 - par}", f"x2Tb{1 - par}"], writes=[f"x2T{par}", f"x2Tb{par}"])

            def front_tile(bidx, ti, part):
                tiles = blocks2[bidx]
                par = bidx % 2
                x2T = x2Ts[par]
                T = tiles[ti]
                nt = 16 if T == 0 else 128
                hk = [f"h1_{par}_{ti}_0", f"h1_{par}_{ti}_1"]
                hb = h1[par][ti]
                if part == "A0":
                    S.group_begin()
                    S.op("sp", ("dma_start", dict(out=mix[0:nt, 0:512], in_=OD[T * 128:T * 128 + nt, :])), writes=["mix"], dsem=mds)
                    S.op("sp", ("dma_start", dict(out=mix[0:nt, 512:1024], in_=OG[T * 128:T * 128 + nt, :])), writes=["mixb"], dsem=mds)
                    S.group_end()
                    if T == 0:
                        S.op("sp", ("dma_start", dict(out=hb[0:16, :], in_=meta)), writes=hk, dsem=hds[par][ti])
                    else:
                        S.op("sp", ("dma_start", dict(out=hb[:], in_=x[(T - 1) * 128:T * 128, :])), writes=hk, dsem=hds[par][ti])
                elif part == "A":
                    for kc in range(8):
                        S.op("pe", ("transpose", dict(out=p_t[:, kc, :], in_=mix[:, kc * 128:(kc + 1) * 128], identity=ident[:])),
                             reads=["mix", "mixb", "ident"], writes=["p_t"], sig=(kc == 7))
                    S.op("act", ("copy", dict(out=mixT[:], in_=p_t[:])), reads=["p_t"], writes=["mixTa", "mixTb"])
                elif part in ("B0", "B1"):
                    half = 0 if part == "B0" else 1
                    for kc in range(8):
                        S.op("pe", ("matmul", dict(out=p_op[:], lhsT=mixT[:, kc, :], rhs=wout[:, kc, half * 512:(half + 1) * 512],
                                                  start=(kc == 0), stop=(kc == 7))),
                             reads=["mixTa", "mixTb"] + WOUT, writes=["p_op"], sig=(kc == 7))
                    S.op("dve", ("tensor_tensor", dict(out=hb[:, half * 512:(half + 1) * 512], in0=p_op[:],
                                                      in1=hb[:, half * 512:(half + 1) * 512], op=ALU.add)),
                         reads=["p_op", hk[half]], writes=[hk[half]])
                    if half == 1:
                        S.op("act", ("activation", dict(out=xn2[:], in_=hb[:], func=AF.Square, accum_out=ss2[:, 0:1])),
                             reads=hk, writes=["xn2", "ss2"])
                        rstd_ops(ss2[:, 0:1], D, "ss2")
                        S.op("dve", ("tensor_scalar", dict(out=xn2[:], in0=hb[:], scalar1=ss2[:, 0:1], scalar2=None, op0=ALU.mult)),
                             reads=hk + ["ss2"], writes=["xn2"])
                else:
                    for kc in range(8):
                        S.op("pe", ("transpose", dict(out=p_t[:, kc, :], in_=xn2[:, kc * 128:(kc + 1) * 128], identity=ident[:])),
                             reads=["xn2", "ident"], writes=["p_t"], sig=(kc == 7))
                    S.op("dve", ("tensor_copy", dict(out=x2T[:, :, 2 + ti * 128:2 + ti * 128 + nt], in_=p_t[:, :, 0:nt])),
                         reads=["p_t"], writes=[f"x2T{par}", f"x2Tb{par}"])

            def ffn(bidx):
                ntok = 16 if bidx == 0 else 256
                x2T = x2Ts[bidx % 2]
                XB = [f"x2T{bidx % 2}", f"x2Tb{bidx % 2}"]

                def stage1(c):
                    pr = c % 3
                    pb = (c % 2) if c < 4 else (c - 2) % 3
                    pv_, pg_ = p_u[2 * pb], p_u[2 * pb + 1]
                    pvk, pgk = f"p_u{2 * pb}", f"p_u{2 * pb + 1}"
                    for (pp, pk, ch) in ((pv_, pvk, c), (pg_, pgk, c + 22)):
                        for kc in range(8):
                            S.op("pe", ("matmul", dict(out=pp[:, 0:ntok + 2], lhsT=wup[:, kc, ch * 128:(ch + 1) * 128], rhs=x2T[:, kc, 0:ntok + 2],
                                                      start=(kc == 0), stop=(kc == 7))),
                                 reads=XB + WUP, writes=[pk], sig=(kc == 7))
                    for (pp, pk, ch, y_, yk) in ((pv_, pvk, c, yv[pr], f"yv{pr}"), (pg_, pgk, c + 22, yg[pr], f"yg{pr}")):
                        S.op("act", ("activation", dict(out=y_[:, 0:ntok], in_=pp[:, 2:ntok + 2], func=AF.Identity,
                                                       bias=cv[:, ch, 3:4], scale=cv[:, ch, 2:3])),
                             reads=[pk, "cv"], writes=[yk])
                        S.op("dve", ("scalar_tensor_tensor", dict(out=y_[:, 0:ntok], in0=pp[:, 1:ntok + 1], scalar=cv[:, ch, 1:2], in1=y_[:, 0:ntok],
                                                                 op0=ALU.mult, op1=ALU.add)),
                             reads=[pk, "cv", yk], writes=[yk])
                        S.op("dve", ("scalar_tensor_tensor", dict(out=y_[:, 0:ntok], in0=pp[:, 0:ntok], scalar=cv[:, ch, 0:1], in1=y_[:, 0:ntok],
                                                                 op0=ALU.mult, op1=ALU.add)),
                             reads=[pk, "cv", yk], writes=[yk])

                def stage2(c):
                    pr = c % 3
                    S.op("act", ("activation", dict(out=yg[pr][:, 0:ntok], in_=yg[pr][:, 0:ntok], func=AF.Silu)),
                         reads=[f"yg{pr}"], writes=[f"yg{pr}"])
                    S.op("pool", ("tensor_tensor", dict(out=actT[:, c, 0:ntok], in0=yg[pr][:, 0:ntok], in1=yv[pr][:, 0:ntok], op=ALU.mult)),
                         reads=[f"yg{pr}", f"yv{pr}"], writes=[f"actT{c}"])

                nxt = bidx + 1 if bidx + 1 < len(blocks2) else None
                for c in range(22):
                    stage1(c)
                    if c >= 2:
                        stage2(c - 2)
                    if nxt is not None:
                        for ti_ in range(len(blocks2[nxt])):
                            c0 = 2 + 9 * ti_
                            if c == c0 - 2:
                                front_tile(nxt, ti_, "A0")
                            elif c == c0:
                                if ti_ == 0:
                                    front_hist(nxt)
                                front_tile(nxt, ti_, "A")
                            elif c == c0 + 2:
                                front_tile(nxt, ti_, "B0")
                            elif c == c0 + 4:
                                front_tile(nxt, ti_, "B1")
                            elif c == c0 + 7:
                                front_tile(nxt, ti_, "C")
                stage2(20)
                stage2(21)

            def down(bidx, ti):
                T = blocks2[bidx][ti]
                if T == 0:
                    return
                par = bidx % 2
                hb = h1[par][ti]
                hk = [f"h1_{par}_{ti}_0", f"h1_{par}_{ti}_1"]
                for half in range(2):
                    for c in range(22):
                        S.op("pe", ("matmul", dict(out=p_d[half][:], lhsT=actT[:, c, ti * 128:(ti + 1) * 128], rhs=wdn[:, c, half * 512:(half + 1) * 512],
                                                  start=(c == 0), stop=(c == 21))),
                             reads=[f"actT{c}"] + WDN, writes=[f"p_u{4 + half}"], sig=(c == 21))
                    S.op("dve", ("tensor_tensor", dict(out=hb[:, half * 512:(half + 1) * 512], in0=p_d[half][:],
                                                      in1=hb[:, half * 512:(half + 1) * 512], op=ALU.add)),
                         reads=[f"p_u{4 + half}", hk[half]], writes=[hk[half]])
                S.op("act", ("activation", dict(out=xn2[:], in_=hb[:], func=AF.Square, accum_out=ss2[:, 1:2])),
                     reads=hk, writes=["xn2", "ss2b"])
                rstd_ops(ss2[:, 1:2], D, "ss2b")
                S.op("dve", ("scalar_tensor_tensor", dict(out=hb[:], in0=hb[:], scalar=ss2[:, 1:2], in1=fnw_t[:], op0=ALU.mult, op1=ALU.mult)),
                     reads=hk + ["ss2b", "fnw"], writes=hk)
                ev = S.op("sp", ("dma_start", dict(out=out[(T - 1) * 128:T * 128, :], in_=hb[:])),
                          reads=hk, writes=[("out", T)], dsem=ods2[par][ti])
                S.out_events.append(ev)

            front_hist(0)
            for part_ in ("A0", "A", "B0", "B1", "C"):
                front_tile(0, 0, part_)
            for bidx in range(len(blocks2)):
                ffn(bidx)
                down(bidx, 0)
                if len(blocks2[bidx]) > 1:
                    down(bidx, 1)
            S.finish()
        S.run()
    return nc


_NC_CACHE = {}


def make_in_maps(inputs, NT, ncores):
    f = lambda a: np.ascontiguousarray(np.asarray(a, dtype=np.float32))
    t = host_tables()
    rep = lambda v: np.ascontiguousarray(np.broadcast_to(f(v).reshape(1, -1), (128, f(v).size)))
    lamv = np.concatenate([f(inputs["lambda_q1"])[0], f(inputs["lambda_k1"])[0], f(inputs["lambda_q2"])[0], f(inputs["lambda_k2"])[0]])
    cw = f(inputs["conv_w"])[0]
    cb = f(inputs["conv_b"])[0]
    convt = np.concatenate([cw, cb[None, :]], 0)
    convt = convt.reshape(4, NCH, 128).transpose(2, 1, 0)
    common = {
        "meta": f(inputs["meta_tokens"]),
        "w_in": f(inputs["w_in"])[0],
        "w_out": f(inputs["w_out"])[0],
        "w_up": f(inputs["w_up"])[0],
        "w_down": f(inputs["w_down"])[0],
        "nw1": rep(inputs["norm1_w"]),
        "nw2": np.ascontiguousarray(f(inputs["norm2_w"]).reshape(8, 128).T),
        "fnw": rep(inputs["final_norm_w"]),
        "lamv": rep(lamv),
        "subw": rep(inputs["da_subln_w"]),
        "gnw": rep(inputs["gla_norm_w"]),
        "gatew": np.ascontiguousarray(np.concatenate([f(inputs["gla_gate_w"])[0], f(inputs["gla_gate_b"])], 0)),
        "convt": np.ascontiguousarray(convt.reshape(128, NCH * 4)),
    }
    common.update(t)
    xx = f(inputs["x"])
    maps = []
    for b in range(ncores):
        m = dict(common)
        m["x"] = np.ascontiguousarray(xx[b])
        maps.append(m)
    return maps


def kernel(**inputs):
    x = np.asarray(inputs["x"])
    B, SEQ, _ = x.shape
    NT = SEQ // 128
    if NT not in _NC_CACHE:
        _NC_CACHE[NT] = build_nc(NT)
    nc = _NC_CACHE[NT]
    maps = make_in_maps(inputs, NT, B)
    res = run_bass_kernel_spmd(nc, maps, core_ids=list(range(B)))
    return np.stack([np.asarray(r["out"]).reshape(SEQ, D) for r in res.results], 0).astype(np.float32)
```

```python
import math
from contextlib import ExitStack, suppress

import numpy as np
import concourse.bass as bass
import concourse.mybir as mybir
from concourse.bass_utils import run_bass_kernel_spmd

F32 = mybir.dt.float32
BF16 = mybir.dt.bfloat16
AF = mybir.ActivationFunctionType
ALU = mybir.AluOpType

D = 1024
NMETA = 16
INC = 3088
DFF = 2816
NCH = 44
EPS = 1e-6
SLOPES = [2.0 ** (-2.0 * (h + 1)) for h in range(4)]
LAM_INIT = 0.8 - 0.6 * math.exp(0.0)
NEG = -30000.0
SKIP_C = -100.0


class _Skip(Exception):
    pass


class DSem:
    def __init__(self, sem):
        self.sem = sem
        self.cnt = 0


class Sched:
    ENG = ("pe", "act", "dve", "pool", "sp")

    def __init__(self, nc, es):
        self.nc, self.es = nc, es
        self.q = {e: [] for e in self.ENG}
        self.sem, self.cnt = {}, {}
        self.nsem = 0
        for e in self.ENG:
            self.new_sem(e)
        self.waited = {e: {} for e in self.ENG}
        self.last_w = {}
        self.readers = {}
        self.sems = {}
        self.out_events = []
        self._grp = None
        self.dead = False
        self.kind = {}

    def group_begin(self):
        self._grp = []

    def group_end(self):
        for ev, ds in self._grp:
            ev[1] = ds.cnt
        self._grp = None

    def new_sem(self, e):
        self.nsem += 1
        self.sem[e] = self.es.enter_context(self.nc.semaphore(f"s{self.nsem}_{e}"))
        self.cnt[e] = 0

    def dsem(self, name):
        self.nsem += 1
        return DSem(self.es.enter_context(self.nc.semaphore(f"d{self.nsem}_{name}")))

    def _wait(self, eng, evs):
        need = {}
        for ev in evs:
            if ev is None:
                continue
            s, v, e = ev
            if e == "pe" and eng == "pe":
                continue
            k = id(s)
            if k not in need or need[k][1] < v:
                need[k] = (s, v)
        for k, (s, v) in need.items():
            if self.waited[eng].get(k, 0) >= v:
                continue
            self.waited[eng][k] = v
            self.q[eng].append(lambda E, s=s, v=v: E.wait_ge(s, v))

    def op(self, eng, fn, reads=(), writes=(), sig=True, dsem=None):
        if self.dead:
            return None
        preads = [k for k in reads if isinstance(k, str) and k.startswith("p_")]
        reads = [k for k in reads if not (isinstance(k, str) and k.startswith("p_"))]
        deps = []
        for k in reads:
            deps.append(self.last_w.get(k))
        for k in writes:
            deps.append(self.last_w.get(k))
            deps.extend(self.readers.get(k, {}).values())
        for k in preads:
            ev0 = self.last_w.get(k)
            if ev0 is not None and not (self.kind.get(k) == "r" and ev0[2] == eng):
                deps.append(ev0)
        self._wait(eng, deps)
        if dsem is not None:
            dsem.cnt += 16
            ev = [dsem.sem, dsem.cnt, "dma"]
            if self._grp is not None:
                self._grp.append((ev, dsem))
            self.q[eng].append(lambda E, fn=fn, s=dsem.sem: getattr(E, fn[0])(**fn[1]).then_inc(s, 16))
        elif sig:
            self.cnt[eng] += 1
            ev = [self.sem[eng], self.cnt[eng], eng]
            self.q[eng].append(lambda E, fn=fn, s=self.sem[eng]: getattr(E, fn[0])(**fn[1]).then_inc(s, 1))
        else:
            ev = [self.sem[eng], self.cnt[eng] + 1, eng]
            self.q[eng].append(lambda E, fn=fn: getattr(E, fn[0])(**fn[1]))
        for k in reads:
            self.readers.setdefault(k, {})[id(ev[0])] = ev
        for k in writes:
            self.last_w[k] = ev
            self.readers[k] = {}
            self.kind[k] = "w"
        for k in preads:
            self.last_w[k] = ev
            self.readers[k] = {}
            self.kind[k] = "r"
        return ev

    def barrier(self):
        self.dead = False
        evs = []
        for k, ev in self.last_w.items():
            evs.append(ev)
        for k, d in self.readers.items():
            evs.extend(d.values())
        for e in self.ENG:
            self._wait(e, evs)
        self.last_w = {}
        self.readers = {}

    def finish(self):
        self._wait("sp", self.out_events)

    def run(self):
        block = self.es.enter_context(self.nc.Block())
        q = self.q

        @block.tensor
        def _(E):
            for f in q["pe"]:
                f(E)

        @block.scalar
        def _(E):
            for f in q["act"]:
                f(E)

        @block.vector
        def _(E):
            for f in q["dve"]:
                f(E)

        @block.gpsimd
        def _(E):
            for f in q["pool"]:
                f(E)

        @block.sync
        def _(E):
            for f in q["sp"]:
                f(E)


def host_tables():
    t = {}
    t["c_ident"] = np.eye(128, dtype=np.float32)
    s = np.arange(128)[:, None]
    c = np.arange(128)[None, :]
    same = (s // 128) == (c // 128)
    t["c_tri"] = ((s <= c) & same).astype(np.float32)
    t["c_tri3"] = ((s > c) & same).astype(np.float32)
    db = np.zeros((128, 4, 2, 256), np.float32)
    k = np.arange(128)[:, None]
    q = np.arange(256)[None, :]
    for h in range(4):
        a = SLOPES[h]
        for dt_ in range(2):
            kb = 128 * dt_ + k
            vis = (kb // 64) <= (q // 64)
            val = -a * np.abs(q - kb) + a * q + a * (127 - k)
            db[:, h, dt_, :] = np.where(vis, val, NEG)
    t["c_dbias"] = db.reshape(128, 4 * 2 * 256)
    mb = np.full((128, 4, 16), NEG, np.float32)
    k16 = np.arange(16)[:, None]
    q16 = np.arange(16)[None, :]
    for h in range(4):
        a = SLOPES[h]
        mb[:16, h, :] = -a * np.abs(q16 - k16) + a * (15 - k16)
    t["c_mbias"] = mb.reshape(128, 64)
    dt = np.zeros((128, 8), np.float32)
    j = np.arange(128)
    for h in range(4):
        a = SLOPES[h]
        dt[:, h] = np.exp(-a * (127 - j))
        dt[:16, 4 + h] = np.exp(-a * (15 - j[:16]))
    t["c_dtab"] = dt
    return t


def build_nc(NT, debug=False, phases=3):
    TT = NT + 1
    LT = TT * 128
    NB = NT // 2
    nc = bass.Bass("TRN2", target_bir_lowering=False)

    def din(name, shape, dt=F32):
        return nc.dram_tensor(name, list(shape), dt, kind="ExternalInput").ap()

    x = din("x", [NT * 128, D])
    meta = din("meta", [NMETA, D])
    w_in = din("w_in", [D, INC])
    w_out = din("w_out", [D, D])
    w_up = din("w_up", [D, 2 * DFF])
    w_down = din("w_down", [DFF, D])
    nw1 = din("nw1", [128, D])
    nw2 = din("nw2", [128, 8])
    fnw = din("fnw", [128, D])
    lamv = din("lamv", [128, 256])
    subw = din("subw", [128, 128])
    gnw = din("gnw", [128, 128])
    gatew = din("gatew", [17, 256])
    convt = din("convt", [128, NCH * 4])
    c_ident = din("c_ident", [128, 128])
    c_tri = din("c_tri", [128, 128])
    c_tri3 = din("c_tri3", [128, 128])
    c_dbias = din("c_dbias", [128, 2048])
    c_mbias = din("c_mbias", [128, 64])
    c_dtab = din("c_dtab", [128, 8])
    out = nc.dram_tensor("out", [NT * 128, D], F32, kind="ExternalOutput").ap()

    okind = "ExternalOutput" if debug else "Internal"
    QT = nc.dram_tensor("QT", [4, 2, 128, LT], BF16, kind=okind).ap()
    KT = nc.dram_tensor("KT", [4, 128, LT], BF16, kind=okind).ap()
    VP = nc.dram_tensor("VP", [4, 128, TT, 129], BF16, kind=okind).ap()
    OG = nc.dram_tensor("OG", [LT, 512], BF16, kind=okind).ap()
    OD = nc.dram_tensor("OD", [LT, 512], BF16, kind=okind).ap()

    with ExitStack() as es:
        S = Sched(nc, es)

        def sb(name, shape, dt, st=es):
            return st.enter_context(nc.sbuf_tensor(name, list(shape), dt))

        def pst(name, shape, dt, st):
            return st.enter_context(nc.psum_tensor(name, list(shape), dt))

        ident = sb("ident", [128, 128], BF16)
        lam_t = sb("lam_t", [128, 8], F32)
        subw_t = sb("subw_t", [128, 128], F32)
        eps_t = sb("eps_t", [128, 1], F32)
        cds = S.dsem("const")

        def ld(eng, dst, src, key, ds):
            return S.op(eng, ("dma_start", dict(out=dst, in_=src)), writes=[key], dsem=ds)

        ld("pool", ident[:], c_ident, "ident", S.dsem("ident"))
        S.group_begin()
        ld("sp", subw_t[:], subw, "subw", cds)
        with ExitStack() as st0:
            lv = sb("lv", [128, 256], F32, st0)
            lj = sb("lj", [128, 128], F32, st0)
            ld("sp", lv[:], lamv, "lv", cds)
            S.group_end()
            S.op("dve", ("memset", dict(ap=eps_t[:], constant=EPS)), writes=["eps"])
            S.op("dve", ("memset", dict(ap=lam_t[:], constant=0.0)), writes=["lam"])
            S.op("dve", ("tensor_tensor", dict(out=lj[:, 0:64], in0=lv[:, 0:64], in1=lv[:, 64:128], op=ALU.mult)),
                 reads=["lv"], writes=["lj"])
            S.op("dve", ("tensor_tensor", dict(out=lj[:, 64:128], in0=lv[:, 128:192], in1=lv[:, 192:256], op=ALU.mult)),
                 reads=["lv"], writes=["lj2"])
            S.op("dve", ("reduce_sum", dict(out=lam_t[:, 0:1], in_=lj[:, 0:64], axis=mybir.AxisListType.X)),
                 reads=["lj"], writes=["lam"])
            S.op("dve", ("reduce_sum", dict(out=lam_t[:, 1:2], in_=lj[:, 64:128], axis=mybir.AxisListType.X)),
                 reads=["lj2"], writes=["lam"])
            S.op("act", ("activation", dict(out=lam_t[:, 2:4], in_=lam_t[:, 0:2], func=AF.Exp)), reads=["lam"], writes=["lam"])
            S.op("dve", ("tensor_tensor", dict(out=lam_t[:, 5:6], in0=lam_t[:, 3:4], in1=lam_t[:, 2:3], op=ALU.subtract)),
                 reads=["lam"], writes=["lam"])
            S.op("dve", ("tensor_scalar", dict(out=lam_t[:, 4:5], in0=lam_t[:, 5:6], scalar1=-LAM_INIT, scalar2=None, op0=ALU.add)),
                 reads=["lam"], writes=["lam"])
            S.op("dve", ("tensor_scalar", dict(out=subw_t[:], in0=subw_t[:], scalar1=1.0 - LAM_INIT, scalar2=None, op0=ALU.mult)),
                 reads=["subw"], writes=["subw"])
            S.barrier()
        nlam = lam_t[:, 4:5]

        def rstd_ops(ss_ap, n, key):
            S.op("act", ("activation", dict(out=ss_ap, in_=ss_ap, func=AF.Ln, bias=eps_t[0:ss_ap.shape[0], 0:1], scale=1.0 / n)),
                 reads=[key, "eps"], writes=[key])
            S.op("act", ("activation", dict(out=ss_ap, in_=ss_ap, func=AF.Exp, scale=-0.5)), reads=[key], writes=[key])

        import os as _os
        _stop = int(_os.environ.get("K_STOP", "0"))

        def ck(n):
            if _stop == n:
                S.dead = True

        with ExitStack() as st:
            st.enter_context(suppress(_Skip))
            win = sb("win", [128, 8, INC], BF16, st)
            nw1_t = sb("nw1_t", [128, D], F32, st)
            gnw_t = sb("gnw_t", [128, 128], F32, st)
            gw_t = sb("gw_t", [32, 256], F32, st)
            tri_t = sb("tri_t", [128, 128], F32, st)
            tri3_t = sb("tri3_t", [128, 128], F32, st)
            dtab = sb("dtab", [128, 8], F32, st)
            xs = [sb(f"xs{i}", [128, D], F32, st) for i in range(2)]
            junk = sb("junk", [128, D], BF16, st)
            ssn = sb("ssn", [128, 2], F32, st)
            xn = sb("xn", [128, D], BF16, st)
            xnTs = [sb(f"xnT{j}", [128, 8, 128], BF16, st) for j in range(3)]
            gjunk = sb("gjunk", [128, 512], BF16, st)
            qst = [sb(f"qst{i}", [128, 4, 2, 128], BF16, st) for i in range(2)]
            kst = [sb(f"kst{i}", [128, 4, 128], BF16, st) for i in range(2)]
            vst = [sb(f"vst{i}", [128, 4, 129], BF16, st) for i in range(2)]
            ogs = [sb(f"ogs{i}", [128, 512], BF16, st) for i in range(2)]
            glrT = sb("glrT", [32, 128], F32, st)
            la_l = [sb(f"la{j}", [128, 256], F32, st) for j in range(2)]
            e1T_l = [sb(f"e1T{j}", [128, 2, 128], F32, st) for j in range(2)]
            e2T_l = [sb(f"e2T{j}", [128, 2, 128], F32, st) for j in range(2)]
            ke_l = [sb(f"ke{j}", [128, 256], F32, st) for j in range(2)]
            kdT_l = [sb(f"kdT{j}", [128, 2, 128], BF16, st) for j in range(2)]
            qdh_l = [sb(f"qdh{j}", [128, 4, 128], BF16, st) for j in range(2)]
            qdc_l = [sb(f"qdc{j}", [128, 2, 2, 128], BF16, st) for j in range(2)]
            kec_l = [sb(f"kec{j}", [128, 2, 256], BF16, st) for j in range(2)]
            gvb_l = [sb(f"gvb{j}", [128, 512], BF16, st) for j in range(2)]
            aT_l = [sb(f"aT{j}", [128, 4, 128], BF16, st) for j in range(2)]
            Sf = sb("Sf", [128, 2, 128], F32, st)
            Sb = sb("Sb", [128, 2, 256], BF16, st)
            gss = sb("gss", [128, 4], F32, st)
            gon = sb("gon", [128, 512], F32, st)
            ger = sb("ger", [128, 512], F32, st)

            p_tr = pst("p_tr", [128, 8, 128], BF16, st)
            p_fm = [pst(f"p_fm{i}", [128, 4, 128], F32, st) for i in range(3)]
            p_tm = [pst(f"p_tm{i}", [128, 512], F32, st) for i in range(2)]
            p_g = [pst(f"p_g{i}", [128, 512], F32, st) for i in range(2)]

            wds = S.dsem("win")
            S.group_begin()
            WG = [(0, 512), (512, 1024), (1024, 1536), (1536, 2048), (2048, 2560), (2560, INC)]
            w_in_v = w_in.rearrange("(kc p) n -> p kc n", p=128)
            for gi, (c0_, c1_) in enumerate(WG):
                ld("pool", win[:, :, c0_:c1_], w_in_v[:, :, c0_:c1_], f"win{gi}", S.dsem(f"win{gi}"))

            def wkey(col0):
                for gi, (c0_, c1_) in enumerate(WG):
                    if c0_ <= col0 < c1_:
                        return [f"win{gi}"]
                raise ValueError(col0)
            ld("sp", nw1_t[:], nw1, "nw1", cds)
            ld("sp", gnw_t[:], gnw, "gnw", cds)
            ld("sp", gw_t[0:17, :], gatew, "gw", cds)
            ld("sp", tri_t[:], c_tri, "tri", cds)
            ld("sp", tri3_t[:], c_tri3, "tri3", cds)
            ld("sp", dtab[:], c_dtab, "dtab", cds)
            S.group_end()
            S.op("dve", ("memset", dict(ap=glrT[:], constant=1.0)), writes=["glrT"])
            for i in range(2):
                S.op("pool", ("memset", dict(ap=qst[i][:], constant=0.0)), writes=[f"qst{i}"])
            for j in range(2):
                S.op("pool", ("memset", dict(ap=qdh_l[j][:], constant=0.0)), writes=[f"qdh{j}"])
                S.op("pool", ("memset", dict(ap=qdc_l[j][:], constant=0.0)), writes=[f"qdc{j}"])
                S.op("pool", ("memset", dict(ap=kec_l[j][:], constant=0.0)), writes=[f"kec{j}"])
            S.op("dve", ("memset", dict(ap=Sf[:], constant=0.0)), writes=["Sf"])
            S.op("dve", ("memset", dict(ap=Sb[:], constant=0.0)), writes=["Sb"])
            S.op("pool", ("memset", dict(ap=xs[0][:], constant=0.0)), writes=["xs0"])

            xds = [S.dsem("xs0"), S.dsem("xs1")]
            sdq = [S.dsem("sq0"), S.dsem("sq1")]
            sdk = [S.dsem("sk0"), S.dsem("sk1")]
            sdv = [S.dsem("sv0"), S.dsem("sv1")]
            sdg = [S.dsem("sg0"), S.dsem("sg1")]

            def load_x(T):
                i = T % 2
                if T == 0:
                    ld("sp", xs[0][0:16, :], meta, "xs0", xds[0])
                else:
                    ld("sp", xs[i][:], x[(T - 1) * 128:T * 128, :], f"xs{i}", xds[i])

            load_x(0)
            def mk_groups(xnT, XT):
                def fm_group(pt, pk, col0, nchunk, m=128):
                    for c in range(nchunk):
                        for kc in range(8):
                            S.op("pe", ("matmul", dict(
                                out=pt[0:m, c, :], lhsT=win[:, kc, col0 + c * 128: col0 + c * 128 + m], rhs=xnT[:, kc, :],
                                start=(kc == 0), stop=(kc == 7))),
                                reads=XT + wkey(col0), writes=[pk], sig=(kc == 7 and c == nchunk - 1))
                def tm_group(pt, pk, col0, n, off=0):
                    for kc in range(8):
                        S.op("pe", ("matmul", dict(out=pt[:, off:off + n], lhsT=xnT[:, kc, :], rhs=win[:, kc, col0:col0 + n],
                                                            start=(kc == 0), stop=(kc == 7))),
                             reads=XT + wkey(col0), writes=[pk], sig=(kc == 7))
                return fm_group, tm_group

            def seg_F1(T, part="ab"):
                i = T % 2
                xk = f"xs{i}"
                xnT = xnTs[T % 3]
                XT = [f"xnTa{T % 3}", f"xnTb{T % 3}"]
                fm_group, tm_group = mk_groups(xnT, XT)
                if "a" in part:
                    if T + 1 < TT:
                        load_x(T + 1)
                    S.op("act", ("activation", dict(out=junk[:], in_=xs[i][:], func=AF.Square, accum_out=ssn[:, 0:1])),
                         reads=[xk], writes=["junk", "ssn"])
                    rstd_ops(ssn[:, 0:1], D, "ssn")
                    S.op("dve", ("scalar_tensor_tensor", dict(out=xn[:], in0=xs[i][:], scalar=ssn[:, 0:1], in1=nw1_t[:],
                                                                     op0=ALU.mult, op1=ALU.mult)),
                         reads=[xk, "ssn", "nw1"], writes=["xn"])
                if "b" not in part:
                    return
                for kc in range(8):
                    S.op("pe", ("transpose", dict(out=p_tr[:, kc, :], in_=xn[:, kc * 128:(kc + 1) * 128], identity=ident[:])),
                         reads=["xn", "ident"], writes=["p_tr"], sig=(kc == 7))
                S.op("act", ("copy", dict(out=xnT[:], in_=p_tr[:])), reads=["p_tr"], writes=XT)

            def seg_F2(T, part="qk"):
                i = T % 2
                xk = f"xs{i}"
                xnT = xnTs[T % 3]
                XT = [f"xnTa{T % 3}", f"xnTb{T % 3}"]
                fm_group, tm_group = mk_groups(xnT, XT)

                if "q" in part:
                    fm_group(p_fm[0], "p_fm0", 0, 4)
                    for m in range(2):
                        S.op("act" if m == 0 else "dve",
                             (("copy", dict(out=qst[i][m * 64:(m + 1) * 64, :, m, :], in_=p_fm[0][m * 64:(m + 1) * 64, :, :]))) if m == 0 else
                             (("tensor_copy", dict(out=qst[i][m * 64:(m + 1) * 64, :, m, :], in_=p_fm[0][m * 64:(m + 1) * 64, :, :]))),
                             reads=["p_fm0"], writes=[f"qst{i}"])
                    S.op("sp", ("dma_start", dict(out=QT[:, :, :, T * 128:(T + 1) * 128].rearrange("h m p t -> p h m t"), in_=qst[i][:])),
                         reads=[f"qst{i}"], writes=[("QT", T)], dsem=sdq[i])
                if "k" not in part:
                    return
                fm_group(p_fm[0], "p_fm0", 512, 4)
                S.op("act", ("copy", dict(out=kst[i][:], in_=p_fm[0][:])), reads=["p_fm0"], writes=[f"kst{i}"])
                S.op("sp", ("dma_start", dict(out=KT[:, :, T * 128:(T + 1) * 128].rearrange("h p t -> p h t"), in_=kst[i][:])),
                     reads=[f"kst{i}"], writes=[("KT", T)], dsem=sdk[i])

            def seg_F3(T):
                i = T % 2
                xk = f"xs{i}"
                xnT = xnTs[T % 3]
                XT = [f"xnTa{T % 3}", f"xnTb{T % 3}"]
                fm_group, tm_group = mk_groups(xnT, XT)
                pv0 = p_fm[0][:].rearrange("p a t -> p (a t)")
                tm_group(pv0, "p_fm0", 1024, 512)
                dcol = 4 if T == 0 else 0
                for h in range(4):
                    S.op("dve", ("tensor_scalar", dict(out=vst[i][:, h, 0:128], in0=pv0[:, h * 128:(h + 1) * 128],
                                                              scalar1=dtab[:, dcol + h:dcol + h + 1], scalar2=None, op0=ALU.mult)),
                         reads=["p_fm0", "dtab"], writes=[f"vst{i}"])
                S.op("dve", ("tensor_copy", dict(out=vst[i][:, :, 128], in_=dtab[:, dcol:dcol + 4])),
                     reads=["dtab"], writes=[f"vst{i}"])
                S.op("sp", ("dma_start", dict(out=VP[:, :, T, :].rearrange("h p c -> p h c"), in_=vst[i][:])),
                     reads=[f"vst{i}"], writes=[("VP", T)], dsem=sdv[i])


            def seg_G1(T, part="all"):
                i = T % 2
                xk = f"xs{i}"
                xnT = xnTs[T % 3]
                XT = [f"xnTa{T % 3}", f"xnTb{T % 3}"]
                fm_group, tm_group = mk_groups(xnT, XT)
                pd = p_fm[1][:].rearrange("p a t -> p (a t)")
                la = la_l[i]
                e1T = e1T_l[i]
                e2T = e2T_l[i]
                ke = ke_l[i]
                kdT = kdT_l[i]
                qdh = qdh_l[i]
                qdc = qdc_l[i]
                kec = kec_l[i]
                gvb = gvb_l[i]
                aT = aT_l[i]
                if part != "rest":
                    for kc in range(8):
                        S.op("pe", ("matmul", dict(out=p_g[0][0:16, 0:128], lhsT=win[:, kc, 3072:3088], rhs=xnT[:, kc, :],
                                                            start=(kc == 0), stop=(kc == 7))),
                             reads=XT + wkey(3072), writes=["p_g0"], sig=(kc == 7))
                    S.op("act", ("copy", dict(out=glrT[0:16, :], in_=p_g[0][0:16, 0:128])), reads=["p_g0"], writes=["glrT"])
                if part == "glr":
                    return
                fm_group(p_fm[2], "p_fm2", 1536, 4)
                S.op("pe", ("matmul", dict(out=p_g[1][:, 0:256], lhsT=glrT[0:17, :], rhs=gw_t[0:17, :], start=True, stop=True)),
                     reads=["glrT", "gw"], writes=["p_g1"])
                S.op("act", ("activation", dict(out=la[:], in_=p_g[1][:, 0:256], func=AF.Exp, scale=-1.0)),
                     reads=["p_g1"], writes=[f"la{i}"])
                S.op("act", ("activation", dict(out=la[:], in_=la[:], func=AF.Ln, bias=1.0)), reads=[f"la{i}"], writes=[f"la{i}"])

            def seg_G2(T, part="ab"):
                i = T % 2
                xk = f"xs{i}"
                xnT = xnTs[T % 3]
                XT = [f"xnTa{T % 3}", f"xnTb{T % 3}"]
                fm_group, tm_group = mk_groups(xnT, XT)
                pd = p_fm[1][:].rearrange("p a t -> p (a t)")
                la = la_l[i]
                e1T = e1T_l[i]
                e2T = e2T_l[i]
                ke = ke_l[i]
                kdT = kdT_l[i]
                qdh = qdh_l[i]
                qdc = qdc_l[i]
                kec = kec_l[i]
                gvb = gvb_l[i]
                aT = aT_l[i]
                if "a" in part:
                    tm_group(p_tm[0], "p_tm0", 2048, 512)
                    S.op("act", ("copy", dict(out=gvb[:], in_=p_tm[0][:])), reads=["p_tm0"], writes=[f"gvb{i}"])
                    kr = 16 if T == 0 else 128
                    for hp in range(2):
                        S.op("pe", ("matmul", dict(out=p_g[0][:, hp * 128:(hp + 1) * 128], lhsT=la[0:kr, hp * 128:(hp + 1) * 128],
                                                            rhs=tri_t[0:kr, :], start=True, stop=True)),
                             reads=[f"la{i}", "tri"], writes=["p_g0"], sig=(hp == 1))
                    S.op("pe", ("matmul", dict(out=p_g[1][:, 256:512], lhsT=tri3_t[0:kr, :], rhs=la[0:kr, :], start=True, stop=True)),
                         reads=[f"la{i}", "tri3"], writes=["p_g1"])
                    p_g0v = p_g[0][:, 0:256].rearrange("p (a t) -> p a t", a=2)
                    S.op("act", ("activation", dict(out=e1T[:], in_=p_g0v, func=AF.Exp, scale=-1.0 / 16)), reads=["p_g0"], writes=[f"e1T{i}"])
                    S.op("act", ("activation", dict(out=e2T[:], in_=p_g0v, func=AF.Exp, scale=1.0 / 16)), reads=["p_g0"], writes=[f"e2T{i}"])
                    S.op("act", ("activation", dict(out=ke[:], in_=p_g[1][:, 256:512], func=AF.Exp, scale=-1.0 / 16)),
                         reads=["p_g1"], writes=[f"ke{i}"])
                    tm_group(p_tm[0], "p_tm0", 1792, 256)
                    S.op("dve", ("tensor_tensor", dict(out=kec[:, 0, :], in0=p_tm[0][:, 0:256], in1=ke[:], op=ALU.mult)),
                         reads=["p_tm0", f"ke{i}"], writes=[f"kec{i}"])
                    S.op("dve", ("tensor_tensor", dict(out=kdT[:], in0=p_fm[2][:, 2:4, :], in1=e2T[:], op=ALU.mult)),
                         reads=["p_fm2", f"e2T{i}"], writes=[f"kdT{i}"])
                    for h in range(4):
                        hp, hl = h // 2, h % 2
                        S.op("dve", ("scalar_tensor_tensor", dict(
                            out=qdh[hl * 64:(hl + 1) * 64, h, :], in0=p_fm[2][hl * 64:(hl + 1) * 64, hp, :], scalar=0.125,
                            in1=e1T[hl * 64:(hl + 1) * 64, hp, :], op0=ALU.mult, op1=ALU.mult)),
                            reads=["p_fm2", f"e1T{i}"], writes=[f"qdh{i}"])
                    S.op("dve", ("scalar_tensor_tensor", dict(
                        out=qdc[:, 0, :, :], in0=p_fm[2][:, 0:2, :], scalar=0.125, in1=e1T[:], op0=ALU.mult, op1=ALU.mult)),
                        reads=["p_fm2", f"e1T{i}"], writes=[f"qdc{i}"])
                if "b" not in part:
                    return
                p_g1v = p_g[1][:].rearrange("p (a t) -> p a t", a=4)
                for h in range(4):
                    S.op("pe", ("matmul", dict(out=p_g1v[:, h, :], lhsT=kdT[:, h // 2, :], rhs=qdh[:, h, :], start=True, stop=True)),
                         reads=[f"kdT{i}", f"qdh{i}"], writes=["p_g1"], sig=(h == 3))
                for h in range(4):
                    S.op("dve", ("tensor_tensor", dict(out=aT[:, h, :], in0=p_g1v[:, h, :], in1=tri_t[:], op=ALU.mult)),
                         reads=["p_g1", "tri"], writes=[f"aT{i}"])

            def seg_G3(T):
                i = T % 2
                xk = f"xs{i}"
                xnT = xnTs[T % 3]
                XT = [f"xnTa{T % 3}", f"xnTb{T % 3}"]
                fm_group, tm_group = mk_groups(xnT, XT)
                pd = p_fm[1][:].rearrange("p a t -> p (a t)")
                la = la_l[i]
                e1T = e1T_l[i]
                e2T = e2T_l[i]
                ke = ke_l[i]
                kdT = kdT_l[i]
                qdh = qdh_l[i]
                qdc = qdc_l[i]
                kec = kec_l[i]
                gvb = gvb_l[i]
                aT = aT_l[i]
                nch = 1
                for hp in range(2):
                    for hl in range(2):
                        h = 2 * hp + hl
                        S.op("pe", ("matmul", dict(out=p_tm[1][:, h * 128:(h + 1) * 128], lhsT=aT[:, h, :], rhs=gvb[:, h * 128:(h + 1) * 128],
                                                          start=(h == 0), stop=(T == 0), skip_group_check=True)),
                             reads=[f"aT{i}", f"gvb{i}"], writes=["p_tm1"], sig=False)
                for cc in range(nch):
                    if T > 0:
                        for hp in range(2):
                            S.op("pe", ("matmul", dict(out=p_tm[1][:, hp * 256:(hp + 1) * 256], lhsT=qdc[:, cc, hp, :], rhs=Sb[:, hp, :],
                                                                       start=False, stop=True, skip_group_check=True)),
                                 reads=[f"qdc{i}", "Sb"], writes=["p_tm1"], sig=(hp == 1))
                    for hp in range(2):
                        S.op("pe", ("matmul", dict(out=pd[:, hp * 256:(hp + 1) * 256], lhsT=kec[:, cc, hp * 128:(hp + 1) * 128],
                                                                   rhs=gvb[:, hp * 256:(hp + 1) * 256], start=True, stop=True)),
                             reads=[f"kec{i}", f"gvb{i}"], writes=["p_fm1"], sig=(hp == 1))
                    dc = 15 if T == 0 else 127
                    for hp in range(2):
                        for hl in range(2):
                            S.op("dve", ("scalar_tensor_tensor", dict(
                                out=Sb[hl * 64:(hl + 1) * 64, hp, hl * 128:(hl + 1) * 128], in0=Sf[hl * 64:(hl + 1) * 64, hp, :],
                                scalar=e1T[hl * 64:(hl + 1) * 64, hp, dc:dc + 1],
                                in1=pd[hl * 64:(hl + 1) * 64, hp * 256 + hl * 128: hp * 256 + (hl + 1) * 128],
                                op0=ALU.mult, op1=ALU.add)),
                                reads=["Sf", f"e1T{i}", "p_fm1"], writes=["Sb"])
                            S.op("dve", ("scalar_tensor_tensor", dict(
                                out=Sf[hl * 64:(hl + 1) * 64, hp, :], in0=Sf[hl * 64:(hl + 1) * 64, hp, :],
                                scalar=e1T[hl * 64:(hl + 1) * 64, hp, dc:dc + 1],
                                in1=pd[hl * 64:(hl + 1) * 64, hp * 256 + hl * 128: hp * 256 + (hl + 1) * 128],
                                op0=ALU.mult, op1=ALU.add)),
                                reads=["Sf", f"e1T{i}", "p_fm1"], writes=["Sf"])

            def seg_G4(T):
                i = T % 2
                xk = f"xs{i}"
                xnT = xnTs[T % 3]
                XT = [f"xnTa{T % 3}", f"xnTb{T % 3}"]
                fm_group, tm_group = mk_groups(xnT, XT)
                pd = p_fm[1][:].rearrange("p a t -> p (a t)")
                la = la_l[i]
                e1T = e1T_l[i]
                e2T = e2T_l[i]
                ke = ke_l[i]
                kdT = kdT_l[i]
                qdh = qdh_l[i]
                qdc = qdc_l[i]
                kec = kec_l[i]
                gvb = gvb_l[i]
                aT = aT_l[i]
                for h in range(4):
                    S.op("act", ("activation", dict(out=gjunk[:, h * 128:(h + 1) * 128], in_=p_tm[1][:, h * 128:(h + 1) * 128],
                                                           func=AF.Square, accum_out=gss[:, h:h + 1])),
                         reads=["p_tm1"], writes=["gjunk", "gss"])
                rstd_ops(gss[:, 0:4], 128, "gss")
                for h in range(4):
                    S.op("dve", ("scalar_tensor_tensor", dict(out=gon[:, h * 128:(h + 1) * 128], in0=p_tm[1][:, h * 128:(h + 1) * 128],
                                                                     scalar=gss[:, h:h + 1], in1=gnw_t[:], op0=ALU.mult, op1=ALU.mult)),
                         reads=["p_tm1", "gss", "gnw"], writes=["gon"])
                tm_group(pd, "p_fm1", 2560, 512)
                S.op("act", ("activation", dict(out=ger[:], in_=pd, func=AF.Exp, scale=-1.0)), reads=["p_fm1"], writes=["ger"])
                S.op("act", ("activation", dict(out=ger[:], in_=ger[:], func=AF.Ln, bias=1.0)), reads=["ger"], writes=["ger"])
                S.op("act", ("activation", dict(out=ger[:], in_=ger[:], func=AF.Exp, scale=-1.0)), reads=["ger"], writes=["ger"])
                S.op("dve", ("tensor_tensor", dict(out=ger[:], in0=ger[:], in1=pd, op=ALU.mult)),
                     reads=["ger", "p_fm1"], writes=["ger"])
                S.op("pool", ("tensor_tensor", dict(out=ogs[i][:], in0=gon[:], in1=ger[:], op=ALU.mult)),
                     reads=["gon", "ger"], writes=[f"ogs{i}"])
                S.op("sp", ("dma_start", dict(out=OG[T * 128:(T + 1) * 128, :], in_=ogs[i][:])),
                     reads=[f"ogs{i}"], writes=[("OG", T)], dsem=sdg[i])

            seg_F1(0)
            seg_F2(0)
            seg_F3(0)
            seg_G1(0)
            seg_G2(0)
            if TT > 1:
                seg_F1(1)
                seg_F2(1)
                seg_F3(1)
                seg_G1(1, "glr")
            for T in range(TT):
                if T + 2 < TT:
                    seg_F1(T + 2, "a")
                if T + 1 < TT:
                    seg_G1(T + 1, "rest")
                seg_G3(T)
                if T + 2 < TT:
                    seg_F1(T + 2, "b")
                if T + 1 < TT:
                    seg_G2(T + 1, "a")
                if T + 2 < TT:
                    seg_F2(T + 2, "q")
                if T + 1 < TT:
                    seg_G2(T + 1, "b")
                if T + 2 < TT:
                    seg_F2(T + 2, "k")
                seg_G4(T)
                if T + 2 < TT:
                    seg_F3(T + 2)
                    seg_G1(T + 2, "glr")
            S.barrier()

        wout = sb("wout", [128, 8, D], BF16)
        wup = sb("wup", [128, 8, 2 * DFF], BF16)
        nw2c = sb("nw2c", [128, 8], F32)
        if phases >= 3:
            w2s = S.dsem("w2")
            S.group_begin()
            ld("sp", nw2c[:], nw2, "nw2c", S.dsem("nw2c"))
            for kc in range(8):
                ld("pool", wout[:, kc, :], w_out[kc * 128:(kc + 1) * 128, :], f"wout{kc}", w2s)
            for kc in range(8):
                for hh in range(2):
                    ld("pool", wup[:, kc, hh * DFF:(hh + 1) * DFF], w_up[kc * 128:(kc + 1) * 128, hh * DFF:(hh + 1) * DFF], f"wup{kc}_{hh}", w2s)
            S.group_end()

        with ExitStack() as st:
            st.enter_context(suppress(_Skip))
            if phases < 2:
                raise _Skip()
            KTh_l = [sb(f"KTh{j}", [128, LT], BF16, st) for j in range(2)]
            VPh_l = [sb(f"VPh{j}", [128, TT, 129], BF16, st) for j in range(2)]
            dbias = sb("dbias", [128, 4, 2, 256], F32, st)
            mbias = sb("mbias", [128, 4, 16], F32, st)
            qb = [sb(f"qb{i}", [128, 2, 256], BF16, st) for i in range(2)]
            NPT = 4
            pt = [sb(f"pt{i}", [128, 512], BF16, st) for i in range(NPT)]
            sbias = [sb(f"sbias{i}", [128, 512], F32, st) for i in range(2)]
            otmp = sb("otmp", [128, 128], F32, st)
            ofin = sb("ofin", [128, 2, 128], F32, st)
            zr = sb("zr", [128, 2, 4], F32, st)
            oss = sb("oss", [128, 2], F32, st)
            ods = [sb(f"ods{i}", [128, 128], BF16, st) for i in range(2)]
            p_s = [pst(f"p_s{i}", [128, 512], F32, st) for i in range(4)]
            p_o_full = [pst(f"p_o{i}", [128, 512], F32, st) for i in range(4)]
            p_o = [t[:, 0:258].rearrange("p (a c) -> p a c", a=2) for t in p_o_full]

            S.group_begin()
            ld("sp", dbias[:].rearrange("p a b c -> p (a b c)"), c_dbias, "dbias", cds)
            ld("sp", mbias[:].rearrange("p a b -> p (a b)"), c_mbias, "mbias", cds)
            S.group_end()
            kds = [S.dsem("kk0"), S.dsem("kk1")]
            vds = [S.dsem("vv0"), S.dsem("vv1")]

            def load_kv(hh):
                ld("sp", KTh_l[hh % 2][:], KT[hh], f"KTh{hh % 2}", kds[hh % 2])
                ld("sp", VPh_l[hh % 2][:], VP[hh], f"VPh{hh % 2}", vds[hh % 2])

            load_kv(0)
            qds = [S.dsem("qb0"), S.dsem("qb1")]
            ods_s = [S.dsem("od0"), S.dsem("od1")]
            uctr = [0, 0, 0, 0]

            LA = 3
            for h in range(4):
                a = SLOPES[h]
                if h == 1 and phases >= 3:
                    for kc in range(8):
                        for hh in range(2):
                            S.op("act", ("activation", dict(out=wup[:, kc, hh * DFF:(hh + 1) * DFF], in_=wup[:, kc, hh * DFF:(hh + 1) * DFF],
                                                           func=AF.Copy, scale=nw2c[:, kc:kc + 1])),
                                 reads=["nw2c"], writes=[f"wup{kc}_{hh}"])
                KTh, VPh = KTh_l[h % 2], VPh_l[h % 2]
                KTk, VPk = f"KTh{h % 2}", f"VPh{h % 2}"
                if h + 1 < 4:
                    load_kv(h + 1)
                blks = []
                for I in [-1] + list(range(NB)):
                    bi = uctr[1] % 2
                    uctr[1] += 1
                    units = []
                    if I < 0:
                        units.append((0, 16, "mdiag", 0.0))
                    else:
                        cm = -a * (256 * I + 1)
                        if cm >= SKIP_C:
                            units.append((0, 16, "past", cm))
                        for Tk in range(1, 2 * I + 1):
                            c = -a * (256 * I - 128 * (Tk - 1) - 127)
                            if c < SKIP_C:
                                continue
                            units.append((Tk, 128, "past", c))
                        units.append((2 * I + 1, 128, "diag0", 0.0))
                        units.append((2 * I + 2, 128, "diag1", 0.0))
                    blks.append(dict(I=I, bi=bi, pob=bi * 2, nq=(16 if I < 0 else 256), nqt=(1 if I < 0 else 2),
                                     c0=(0 if I < 0 else 128 + 256 * I), units=units))
                flat = []
                for bidx, bl in enumerate(blks):
                    for ui, un in enumerate(bl["units"]):
                        flat.append((bidx, ui, un))

                def load_q(bl):
                    bi, c0, nq = bl["bi"], bl["c0"], bl["nq"]
                    S.op("sp", ("dma_start", dict(out=qb[bi][:, :, 0:nq], in_=QT[h, :, :, c0:c0 + nq].rearrange("m p t -> p m t"))),
                         writes=[f"qb{bi}"], dsem=qds[bi])

                def st_qk(fi):
                    bidx, ui, (Tk, nk, kind, c) = flat[fi]
                    bl = blks[bidx]
                    if ui == 0 and bidx + 1 < len(blks):
                        load_q(blks[bidx + 1])
                    u = uctr[0] + fi
                    ps_, psk = p_s[u % 4], f"p_s{u % 4}"
                    bi, nq = bl["bi"], bl["nq"]
                    if nq == 256:
                        S.op("pe", ("matmul", dict(out=ps_[0:nk, 0:512], lhsT=KTh[:, Tk * 128:Tk * 128 + nk],
                                                  rhs=qb[bi][:].rearrange("p m t -> p (m t)"), start=True, stop=True)),
                             reads=[KTk, f"qb{bi}"], writes=[psk])
                    else:
                        for m in range(2):
                            S.op("pe", ("matmul", dict(out=ps_[0:nk, m * 256:m * 256 + nq], lhsT=KTh[:, Tk * 128:Tk * 128 + nk], rhs=qb[bi][:, m, 0:nq],
                                                      start=True, stop=True)),
                                 reads=[KTk, f"qb{bi}"], writes=[psk], sig=(m == 1))

                def st_exp_pv(fi):
                    bidx, ui, (Tk, nk, kind, c) = flat[fi]
                    bl = blks[bidx]
                    I, bi, pob, nq, nqt = bl["I"], bl["bi"], bl["pob"], bl["nq"], bl["nqt"]
                    u = uctr[0] + fi
                    ps_, psk = p_s[u % 4], f"p_s{u % 4}"
                    ptt, ptk = pt[u % NPT], f"pt{u % NPT}"
                    if kind == "past":
                        S.op("act", ("activation", dict(out=ptt[0:nk, :], in_=ps_[0:nk, :], func=AF.Exp, bias=float(c), scale=0.125)),
                             reads=[psk], writes=[ptk])
                    elif kind == "mdiag":
                        sbt, sbk = sbias[u % 2], f"sbias{u % 2}"
                        for m in range(2):
                            S.op("dve", ("scalar_tensor_tensor", dict(
                                out=sbt[0:16, m * 256:m * 256 + 16], in0=ps_[0:16, m * 256:m * 256 + 16], scalar=0.125, in1=mbias[0:16, h, :],
                                op0=ALU.mult, op1=ALU.add)), reads=[psk, "mbias"], writes=[sbk])
                        for m in range(2):
                            S.op("act", ("activation", dict(out=ptt[0:16, m * 256:m * 256 + 16], in_=sbt[0:16, m * 256:m * 256 + 16], func=AF.Exp)),
                                 reads=[sbk], writes=[ptk])
                    else:
                        dt_ = 0 if kind == "diag0" else 1
                        sbt, sbk = sbias[u % 2], f"sbias{u % 2}"
                        for m in range(2):
                            S.op("dve", ("scalar_tensor_tensor", dict(
                                out=sbt[:, m * 256:(m + 1) * 256], in0=ps_[:, m * 256:(m + 1) * 256], scalar=0.125, in1=dbias[:, h, dt_, :],
                                op0=ALU.mult, op1=ALU.add)), reads=[psk, "dbias"], writes=[sbk])
                        S.op("act", ("activation", dict(out=ptt[:], in_=sbt[:], func=AF.Exp)), reads=[sbk], writes=[ptk])
                    first = (ui == 0)
                    last = (ui == len(bl["units"]) - 1)
                    mq = 16 if I < 0 else 128
                    for qt in range(nqt):
                        if kind == "diag1" and qt == 0:
                            continue
                        lastq = last or (kind == "diag0" and qt == 0)
                        for m in range(2):
                            S.op("pe", ("matmul", dict(
                                out=p_o[pob + qt][0:mq, m, :], lhsT=ptt[0:nk, m * 256 + qt * 128: m * 256 + qt * 128 + mq], rhs=VPh[0:nk, Tk, :],
                                start=(first and m == 0), stop=lastq, skip_group_check=True)),
                                reads=[ptk, VPk], writes=[f"p_o{pob + qt}"], sig=(m == 1))
                    if not last:
                        return None

                    def epilogue(I=I, pob=pob, nqt=nqt, mq=mq):
                        for qt in range(nqt):
                            po = p_o[pob + qt]
                            pok = f"p_o{pob + qt}"
                            S.op("dve", ("reciprocal", dict(out=zr[0:mq, qt, 0:2], in_=po[0:mq, :, 128])), reads=[pok], writes=["zr"])
                            S.op("dve", ("tensor_tensor", dict(out=zr[0:mq, qt, 2:3], in0=zr[0:mq, qt, 1:2], in1=nlam[0:mq, :], op=ALU.mult)),
                                 reads=["zr"], writes=["zr"])
                            S.op("dve", ("tensor_scalar", dict(out=otmp[0:mq, :], in0=po[0:mq, 1, 0:128], scalar1=zr[0:mq, qt, 2:3], scalar2=None, op0=ALU.mult)),
                                 reads=[pok, "zr"], writes=["otmp"])
                            S.op("dve", ("scalar_tensor_tensor", dict(out=ofin[0:mq, qt, :], in0=po[0:mq, 0, 0:128], scalar=zr[0:mq, qt, 0:1], in1=otmp[0:mq, :],
                                                                     op0=ALU.mult, op1=ALU.add)),
                                 reads=[pok, "zr", "otmp"], writes=[f"ofin{qt}"])
                            S.op("dve", ("scalar_tensor_tensor", dict(out=otmp[0:mq, :], in0=ofin[0:mq, qt, :], scalar=1.0, in1=ofin[0:mq, qt, :],
                                                                     op0=ALU.mult, op1=ALU.mult, accum_out=oss[0:mq, qt:qt + 1])),
                                 reads=[f"ofin{qt}"], writes=["otmp", "oss"])
                        rstd_ops(oss[0:mq, 0:nqt], 128, "oss")
                        for qt in range(nqt):
                            oi = uctr[2] % 2
                            uctr[2] += 1
                            S.op("dve", ("scalar_tensor_tensor", dict(out=ods[oi][0:mq, :], in0=ofin[0:mq, qt, :], scalar=oss[0:mq, qt:qt + 1], in1=subw_t[0:mq, :],
                                                                     op0=ALU.mult, op1=ALU.mult)),
                                 reads=[f"ofin{qt}", "oss", "subw"], writes=[f"ods{oi}"])
                            r0 = 0 if I < 0 else 128 + 256 * I + 128 * qt
                            S.op("sp", ("dma_start", dict(out=OD[r0:r0 + mq, h * 128:(h + 1) * 128], in_=ods[oi][0:mq, :])),
                                 reads=[f"ods{oi}"], writes=[("OD", h, r0)], dsem=ods_s[oi])
                    return epilogue

                load_q(blks[0])
                n = len(flat)
                pending = []
                for idx in range(n + LA):
                    if idx < n:
                        st_qk(idx)
                    if idx - LA >= 0:
                        fb, fu, _ = flat[idx - LA]
                        if fu == 0:
                            keep = []
                            for pe_ in pending:
                                if pe_[2] == blks[fb]["pob"]:
                                    pe_[1]()
                                else:
                                    keep.append(pe_)
                            pending[:] = keep
                        ep = st_exp_pv(idx - LA)
                        if ep is not None:
                            pending.append((idx + 9, ep, blks[fb]["pob"]))
                    while pending and pending[0][0] <= idx:
                        pending.pop(0)[1]()
                for pe_ in pending:
                    pe_[1]()
                uctr[0] += n
            S.barrier()

        with ExitStack() as st:
            st.enter_context(suppress(_Skip))
            if phases < 3:
                raise _Skip()
            wdn = sb("wdn", [128, 22, D], BF16, st)
            fnw_t = sb("fnw_t", [128, D], F32, st)
            cv = sb("cv", [128, NCH, 4], F32, st)
            mix = sb("mix", [128, D], BF16, st)
            mixT = sb("mixT", [128, 8, 128], BF16, st)
            h1 = [[sb(f"h1_{p}_{t}", [128, D], F32, st) for t in range(2)] for p in range(2)]
            xn2 = sb("xn2", [128, D], BF16, st)
            x2Ts = [sb(f"x2T{j}", [128, 8, 258], BF16, st) for j in range(2)]
            yv = [sb(f"yv{i}", [128, 256], F32, st) for i in range(3)]
            yg = [sb(f"yg{i}", [128, 256], F32, st) for i in range(3)]
            actT = sb("actT", [128, 22, 256], BF16, st)
            ss2 = sb("ss2", [128, 2], F32, st)
            p_t = pst("p_t", [128, 8, 128], BF16, st)
            p_op = pst("p_op", [128, 512], F32, st)
            p_u = [pst(f"p_u{i}", [128, 512], F32, st) for i in range(6)]
            p_d = [p_u[4], p_u[5]]

            w3s = S.dsem("w3")
            S.group_begin()
            ld("sp", fnw_t[:], fnw, "fnw", cds)
            ld("sp", cv[:].rearrange("p a b -> p (a b)"), convt, "cv", cds)
            for c in range(22):
                ld("pool", wdn[:, c, :], w_down[c * 128:(c + 1) * 128, :], f"wdn{c}", w3s)
            S.group_end()
            WOUT = [f"wout{kc}" for kc in range(8)]
            WUP = [f"wup{kc}_{hh}" for kc in range(8) for hh in range(2)]
            WDN = [f"wdn{c}" for c in range(22)]
            S.op("dve", ("memset", dict(ap=x2Ts[0][:, :, 0:2], constant=0.0)), writes=["x2T0", "x2Tb0"])
            S.op("pool", ("memset", dict(ap=h1[0][0][:], constant=0.0)), writes=["h1_0_0_0", "h1_0_0_1"])
            S.op("pool", ("memset", dict(ap=mix[:], constant=0.0)), writes=["mix", "mixb"])

            mds = S.dsem("mix")
            hds = [[S.dsem(f"h1{p}{t}") for t in range(2)] for p in range(2)]
            ods2 = [[S.dsem(f"o{p}{t}") for t in range(2)] for p in range(2)]
            blocks2 = [[0]] + [[2 * B + 1, 2 * B + 2] for B in range(NB)]

            def front_hist(bidx):
                par = bidx % 2
                if bidx > 0:
                    pn = 16 if bidx == 1 else 256
                    S.op("pool", ("tensor_copy", dict(out=x2Ts[par][:, :, 0:2], in_=x2Ts[1 - par][:, :, pn:pn + 2])),
                         reads=[f"x2T{1 - par}", f"x2Tb{1 - par}"], writes=[f"x2T{par}", f"x2Tb{par}"])

            def front_tile(bidx, ti, part):
                tiles = blocks2[bidx]
                par = bidx % 2
                x2T = x2Ts[par]
                T = tiles[ti]
                nt = 16 if T == 0 else 128
                hk = [f"h1_{par}_{ti}_0", f"h1_{par}_{ti}_1"]
                hb = h1[par][ti]
                if part == "A0":
                    S.group_begin()
                    S.op("sp", ("dma_start", dict(out=mix[0:nt, 0:512], in_=OD[T * 128:T * 128 + nt, :])), writes=["mix"], dsem=mds)
                    S.op("sp", ("dma_start", dict(out=mix[0:nt, 512:1024], in_=OG[T * 128:T * 128 + nt, :])), writes=["mixb"], dsem=mds)
                    S.group_end()
                    if T == 0:
                        S.op("sp", ("dma_start", dict(out=hb[0:16, :], in_=meta)), writes=hk, dsem=hds[par][ti])
                    else:
                        S.op("sp", ("dma_start", dict(out=hb[:], in_=x[(T - 1) * 128:T * 128, :])), writes=hk, dsem=hds[par][ti])
                elif part == "A":
                    for kc in range(8):
                        S.op("pe", ("transpose", dict(out=p_t[:, kc, :], in_=mix[:, kc * 128:(kc + 1) * 128], identity=ident[:])),
                             reads=["mix", "mixb", "ident"], writes=["p_t"], sig=(kc == 7))
                    S.op("act", ("copy", dict(out=mixT[:], in_=p_t[:])), reads=["p_t"], writes=["mixTa", "mixTb"])
                elif part in ("B0", "B1"):
                    half = 0 if part == "B0" else 1
                    for kc in range(8):
                        S.op("pe", ("matmul", dict(out=p_op[:], lhsT=mixT[:, kc, :], rhs=wout[:, kc, half * 512:(half + 1) * 512],
                                                  start=(kc == 0), stop=(kc == 7))),
                             reads=["mixTa", "mixTb"] + WOUT, writes=["p_op"], sig=(kc == 7))
                    S.op("dve", ("tensor_tensor", dict(out=hb[:, half * 512:(half + 1) * 512], in0=p_op[:],
                                                      in1=hb[:, half * 512:(half + 1) * 512], op=ALU.add)),
                         reads=["p_op", hk[half]], writes=[hk[half]])
                    if half == 1:
                        S.op("act", ("activation", dict(out=xn2[:], in_=hb[:], func=AF.Square, accum_out=ss2[:, 0:1])),
                             reads=hk, writes=["xn2", "ss2"])
                        rstd_ops(ss2[:, 0:1], D, "ss2")
                        S.op("dve", ("tensor_scalar", dict(out=xn2[:], in0=hb[:], scalar1=ss2[:, 0:1], scalar2=None, op0=ALU.mult)),
                             reads=hk + ["ss2"], writes=["xn2"])
                else:
                    for kc in range(8):
                        S.op("pe", ("transpose", dict(out=p_t[:, kc, :], in_=xn2[:, kc * 128:(kc + 1) * 128], identity=ident[:])),
                             reads=["xn2", "ident"], writes=["p_t"], sig=(kc == 7))
                    S.op("dve", ("tensor_copy", dict(out=x2T[:, :, 2 + ti * 128:2 + ti * 128 + nt], in_=p_t[:, :, 0:nt])),
                         reads=["p_t"], writes=[f"x2T{par}", f"x2Tb{par}"])

            def ffn(bidx):
                ntok = 16 if bidx == 0 else 256
                x2T = x2Ts[bidx % 2]
                XB = [f"x2T{bidx % 2}", f"x2Tb{bidx % 2}"]

                def stage1(c):
                    pr = c % 3
                    pb = (c % 2) if c < 4 else (c - 2) % 3
                    pv_, pg_ = p_u[2 * pb], p_u[2 * pb + 1]
                    pvk, pgk = f"p_u{2 * pb}", f"p_u{2 * pb + 1}"
                    for (pp, pk, ch) in ((pv_, pvk, c), (pg_, pgk, c + 22)):
                        for kc in range(8):
                            S.op("pe", ("matmul", dict(out=pp[:, 0:ntok + 2], lhsT=wup[:, kc, ch * 128:(ch + 1) * 128], rhs=x2T[:, kc, 0:ntok + 2],
                                                      start=(kc == 0), stop=(kc == 7))),
                                 reads=XB + WUP, writes=[pk], sig=(kc == 7))
                    for (pp, pk, ch, y_, yk) in ((pv_, pvk, c, yv[pr], f"yv{pr}"), (pg_, pgk, c + 22, yg[pr], f"yg{pr}")):
                        S.op("act", ("activation", dict(out=y_[:, 0:ntok], in_=pp[:, 2:ntok + 2], func=AF.Identity,
                                                       bias=cv[:, ch, 3:4], scale=cv[:, ch, 2:3])),
                             reads=[pk, "cv"], writes=[yk])
                        S.op("dve", ("scalar_tensor_tensor", dict(out=y_[:, 0:ntok], in0=pp[:, 1:ntok + 1], scalar=cv[:, ch, 1:2], in1=y_[:, 0:ntok],
                                                                 op0=ALU.mult, op1=ALU.add)),
                             reads=[pk, "cv", yk], writes=[yk])
                        S.op("dve", ("scalar_tensor_tensor", dict(out=y_[:, 0:ntok], in0=pp[:, 0:ntok], scalar=cv[:, ch, 0:1], in1=y_[:, 0:ntok],
                                                                 op0=ALU.mult, op1=ALU.add)),
                             reads=[pk, "cv", yk], writes=[yk])

                def stage2(c):
                    pr = c % 3
                    S.op("act", ("activation", dict(out=yg[pr][:, 0:ntok], in_=yg[pr][:, 0:ntok], func=AF.Silu)),
                         reads=[f"yg{pr}"], writes=[f"yg{pr}"])
                    S.op("pool", ("tensor_tensor", dict(out=actT[:, c, 0:ntok], in0=yg[pr][:, 0:ntok], in1=yv[pr][:, 0:ntok], op=ALU.mult)),
                         reads=[f"yg{pr}", f"yv{pr}"], writes=[f"actT{c}"])

                nxt = bidx + 1 if bidx + 1 < len(blocks2) else None
                for c in range(22):
                    stage1(c)
                    if c >= 2:
                        stage2(c - 2)
                    if nxt is not None:
                        for ti_ in range(len(blocks2[nxt])):
                            c0 = 2 + 9 * ti_
                            if c == c0 - 2:
                                front_tile(nxt, ti_, "A0")
                            elif c == c0:
                                if ti_ == 0:
                                    front_hist(nxt)
                                front_tile(nxt, ti_, "A")
                            elif c == c0 + 2:
                                front_tile(nxt, ti_, "B0")
                            elif c == c0 + 4:
                                front_tile(nxt, ti_, "B1")
                            elif c == c0 + 7:
                                front_tile(nxt, ti_, "C")
                stage2(20)
                stage2(21)

            def down(bidx, ti):
                T = blocks2[bidx][ti]
                if T == 0:
                    return
                par = bidx % 2
                hb = h1[par][ti]
                hk = [f"h1_{par}_{ti}_0", f"h1_{par}_{ti}_1"]
                for half in range(2):
                    for c in range(22):
                        S.op("pe", ("matmul", dict(out=p_d[half][:], lhsT=actT[:, c, ti * 128:(ti + 1) * 128], rhs=wdn[:, c, half * 512:(half + 1) * 512],
                                                  start=(c == 0), stop=(c == 21))),
                             reads=[f"actT{c}"] + WDN, writes=[f"p_u{4 + half}"], sig=(c == 21))
                    S.op("dve", ("tensor_tensor", dict(out=hb[:, half * 512:(half + 1) * 512], in0=p_d[half][:],
                                                      in1=hb[:, half * 512:(half + 1) * 512], op=ALU.add)),
                         reads=[f"p_u{4 + half}", hk[half]], writes=[hk[half]])
                S.op("act", ("activation", dict(out=xn2[:], in_=hb[:], func=AF.Square, accum_out=ss2[:, 1:2])),
                     reads=hk, writes=["xn2", "ss2b"])
                rstd_ops(ss2[:, 1:2], D, "ss2b")
                S.op("dve", ("scalar_tensor_tensor", dict(out=hb[:], in0=hb[:], scalar=ss2[:, 1:2], in1=fnw_t[:], op0=ALU.mult, op1=ALU.mult)),
                     reads=hk + ["ss2b", "fnw"], writes=hk)
                ev = S.op("sp", ("dma_start", dict(out=out[(T - 1) * 128:T * 128, :], in_=hb[:])),
                          reads=hk, writes=[("out", T)], dsem=ods2[par][ti])
                S.out_events.append(ev)

            front_hist(0)
            for part_ in ("A0", "A", "B0", "B1", "C"):
                front_tile(0, 0, part_)
            for bidx in range(len(blocks2)):
                ffn(bidx)
                down(bidx, 0)
                if len(blocks2[bidx]) > 1:
                    down(bidx, 1)
            S.finish()
        S.run()
    return nc


_NC_CACHE = {}


def make_in_maps(inputs, NT, ncores):
    f = lambda a: np.ascontiguousarray(np.asarray(a, dtype=np.float32))
    t = host_tables()
    rep = lambda v: np.ascontiguousarray(np.broadcast_to(f(v).reshape(1, -1), (128, f(v).size)))
    lamv = np.concatenate([f(inputs["lambda_q1"])[0], f(inputs["lambda_k1"])[0], f(inputs["lambda_q2"])[0], f(inputs["lambda_k2"])[0]])
    cw = f(inputs["conv_w"])[0]
    cb = f(inputs["conv_b"])[0]
    convt = np.concatenate([cw, cb[None, :]], 0)
    convt = convt.reshape(4, NCH, 128).transpose(2, 1, 0)
    common = {
        "meta": f(inputs["meta_tokens"]),
        "w_in": f(inputs["w_in"])[0],
        "w_out": f(inputs["w_out"])[0],
        "w_up": f(inputs["w_up"])[0],
        "w_down": f(inputs["w_down"])[0],
        "nw1": rep(inputs["norm1_w"]),
        "nw2": np.ascontiguousarray(f(inputs["norm2_w"]).reshape(8, 128).T),
        "fnw": rep(inputs["final_norm_w"]),
        "lamv": rep(lamv),
        "subw": rep(inputs["da_subln_w"]),
        "gnw": rep(inputs["gla_norm_w"]),
        "gatew": np.ascontiguousarray(np.concatenate([f(inputs["gla_gate_w"])[0], f(inputs["gla_gate_b"])], 0)),
        "convt": np.ascontiguousarray(convt.reshape(128, NCH * 4)),
    }
    common.update(t)
    xx = f(inputs["x"])
    maps = []
    for b in range(ncores):
        m = dict(common)
        m["x"] = np.ascontiguousarray(xx[b])
        maps.append(m)
    return maps


def kernel(**inputs):
    x = np.asarray(inputs["x"])
    B, SEQ, _ = x.shape
    NT = SEQ // 128
    if NT not in _NC_CACHE:
        _NC_CACHE[NT] = build_nc(NT)
    nc = _NC_CACHE[NT]
    maps = make_in_maps(inputs, NT, B)
    res = run_bass_kernel_spmd(nc, maps, core_ids=list(range(B)))
    return np.stack([np.asarray(r["out"]).reshape(SEQ, D) for r in res.results], 0).astype(np.float32)
```

```python
import math
from contextlib import ExitStack, suppress

import numpy as np
import concourse.bass as bass
import concourse.mybir as mybir
from concourse.bass_utils import run_bass_kernel_spmd

F32 = mybir.dt.float32
BF16 = mybir.dt.bfloat16
AF = mybir.ActivationFunctionType
ALU = mybir.AluOpType

D = 1024
NMETA = 16
INC = 3088
DFF = 2816
NCH = 44
EPS = 1e-6
SLOPES = [2.0 ** (-2.0 * (h + 1)) for h in range(4)]
LAM_INIT = 0.8 - 0.6 * math.exp(0.0)
NEG = -30000.0
SKIP_C = -100.0


class _Skip(Exception):
    pass


class DSem:
    def __init__(self, sem):
        self.sem = sem
        self.cnt = 0


class Sched:
    ENG = ("pe", "act", "dve", "pool", "sp")

    def __init__(self, nc, es):
        self.nc, self.es = nc, es
        self.q = {e: [] for e in self.ENG}
        self.sem, self.cnt = {}, {}
        self.nsem = 0
        for e in self.ENG:
            self.new_sem(e)
        self.waited = {e: {} for e in self.ENG}
        self.last_w = {}
        self.readers = {}
        self.sems = {}
        self.out_events = []
        self._grp = None
        self.dead = False
        self.kind = {}

    def group_begin(self):
        self._grp = []

    def group_end(self):
        for ev, ds in self._grp:
            ev[1] = ds.cnt
        self._grp = None

    def new_sem(self, e):
        self.nsem += 1
        self.sem[e] = self.es.enter_context(self.nc.semaphore(f"s{self.nsem}_{e}"))
        self.cnt[e] = 0

    def dsem(self, name):
        self.nsem += 1
        return DSem(self.es.enter_context(self.nc.semaphore(f"d{self.nsem}_{name}")))

    def _wait(self, eng, evs):
        need = {}
        for ev in evs:
            if ev is None:
                continue
            s, v, e = ev
            if e == "pe" and eng == "pe":
                continue
            k = id(s)
            if k not in need or need[k][1] < v:
                need[k] = (s, v)
        for k, (s, v) in need.items():
            if self.waited[eng].get(k, 0) >= v:
                continue
            self.waited[eng][k] = v
            self.q[eng].append(lambda E, s=s, v=v: E.wait_ge(s, v))

    def op(self, eng, fn, reads=(), writes=(), sig=True, dsem=None):
        if self.dead:
            return None
        preads = [k for k in reads if isinstance(k, str) and k.startswith("p_")]
        reads = [k for k in reads if not (isinstance(k, str) and k.startswith("p_"))]
        deps = []
        for k in reads:
            deps.append(self.last_w.get(k))
        for k in writes:
            deps.append(self.last_w.get(k))
            deps.extend(self.readers.get(k, {}).values())
        for k in preads:
            ev0 = self.last_w.get(k)
            if ev0 is not None and not (self.kind.get(k) == "r" and ev0[2] == eng):
                deps.append(ev0)
        self._wait(eng, deps)
        if dsem is not None:
            dsem.cnt += 16
            ev = [dsem.sem, dsem.cnt, "dma"]
            if self._grp is not None:
                self._grp.append((ev, dsem))
            self.q[eng].append(lambda E, fn=fn, s=dsem.sem: getattr(E, fn[0])(**fn[1]).then_inc(s, 16))
        elif sig:
            self.cnt[eng] += 1
            ev = [self.sem[eng], self.cnt[eng], eng]
            self.q[eng].append(lambda E, fn=fn, s=self.sem[eng]: getattr(E, fn[0])(**fn[1]).then_inc(s, 1))
        else:
            ev = [self.sem[eng], self.cnt[eng] + 1, eng]
            self.q[eng].append(lambda E, fn=fn: getattr(E, fn[0])(**fn[1]))
        for k in reads:
            self.readers.setdefault(k, {})[id(ev[0])] = ev
        for k in writes:
            self.last_w[k] = ev
            self.readers[k] = {}
            self.kind[k] = "w"
        for k in preads:
            self.last_w[k] = ev
            self.readers[k] = {}
            self.kind[k] = "r"
        return ev

    def barrier(self):
        self.dead = False
        evs = []
        for k, ev in self.last_w.items():
            evs.append(ev)
        for k, d in self.readers.items():
            evs.extend(d.values())
        for e in self.ENG:
            self._wait(e, evs)
        self.last_w = {}
        self.readers = {}

    def finish(self):
        self._wait("sp", self.out_events)

    def run(self):
        block = self.es.enter_context(self.nc.Block())
        q = self.q

        @block.tensor
        def _(E):
            for f in q["pe"]:
                f(E)

        @block.scalar
        def _(E):
            for f in q["act"]:
                f(E)

        @block.vector
        def _(E):
            for f in q["dve"]:
                f(E)

        @block.gpsimd
        def _(E):
            for f in q["pool"]:
                f(E)

        @block.sync
        def _(E):
            for f in q["sp"]:
                f(E)


def host_tables():
    t = {}
    t["c_ident"] = np.eye(128, dtype=np.float32)
    s = np.arange(128)[:, None]
    c = np.arange(128)[None, :]
    same = (s // 128) == (c // 128)
    t["c_tri"] = ((s <= c) & same).astype(np.float32)
    t["c_tri3"] = ((s > c) & same).astype(np.float32)
    db = np.zeros((128, 4, 2, 256), np.float32)
    k = np.arange(128)[:, None]
    q = np.arange(256)[None, :]
    for h in range(4):
        a = SLOPES[h]
        for dt_ in range(2):
            kb = 128 * dt_ + k
            vis = (kb // 64) <= (q // 64)
            val = -a * np.abs(q - kb) + a * q + a * (127 - k)
            db[:, h, dt_, :] = np.where(vis, val, NEG)
    t["c_dbias"] = db.reshape(128, 4 * 2 * 256)
    mb = np.full((128, 4, 16), NEG, np.float32)
    k16 = np.arange(16)[:, None]
    q16 = np.arange(16)[None, :]
    for h in range(4):
        a = SLOPES[h]
        mb[:16, h, :] = -a * np.abs(q16 - k16) + a * (15 - k16)
    t["c_mbias"] = mb.reshape(128, 64)
    dt = np.zeros((128, 8), np.float32)
    j = np.arange(128)
    for h in range(4):
        a = SLOPES[h]
        dt[:, h] = np.exp(-a * (127 - j))
        dt[:16, 4 + h] = np.exp(-a * (15 - j[:16]))
    t["c_dtab"] = dt
    return t


def build_nc(NT, debug=False, phases=3):
    TT = NT + 1
    LT = TT * 128
    NB = NT // 2
    nc = bass.Bass("TRN2", target_bir_lowering=False)

    def din(name, shape, dt=F32):
        return nc.dram_tensor(name, list(shape), dt, kind="ExternalInput").ap()

    x = din("x", [NT * 128, D])
    meta = din("meta", [NMETA, D])
    w_in = din("w_in", [D, INC])
    w_out = din("w_out", [D, D])
    w_up = din("w_up", [D, 2 * DFF])
    w_down = din("w_down", [DFF, D])
    nw1 = din("nw1", [128, D])
    nw2 = din("nw2", [128, 8])
    fnw = din("fnw", [128, D])
    lamv = din("lamv", [128, 256])
    subw = din("subw", [128, 128])
    gnw = din("gnw", [128, 128])
    gatew = din("gatew", [17, 256])
    convt = din("convt", [128, NCH * 4])
    c_ident = din("c_ident", [128, 128])
    c_tri = din("c_tri", [128, 128])
    c_tri3 = din("c_tri3", [128, 128])
    c_dbias = din("c_dbias", [128, 2048])
    c_mbias = din("c_mbias", [128, 64])
    c_dtab = din("c_dtab", [128, 8])
    out = nc.dram_tensor("out", [NT * 128, D], F32, kind="ExternalOutput").ap()

    okind = "ExternalOutput" if debug else "Internal"
    QT = nc.dram_tensor("QT", [4, 2, 128, LT], BF16, kind=okind).ap()
    KT = nc.dram_tensor("KT", [4, 128, LT], BF16, kind=okind).ap()
    VP = nc.dram_tensor("VP", [4, 128, TT, 129], BF16, kind=okind).ap()
    OG = nc.dram_tensor("OG", [LT, 512], BF16, kind=okind).ap()
    OD = nc.dram_tensor("OD", [LT, 512], BF16, kind=okind).ap()

    with ExitStack() as es:
        S = Sched(nc, es)

        def sb(name, shape, dt, st=es):
            return st.enter_context(nc.sbuf_tensor(name, list(shape), dt))

        def pst(name, shape, dt, st):
            return st.enter_context(nc.psum_tensor(name, list(shape), dt))

        ident = sb("ident", [128, 128], BF16)
        lam_t = sb("lam_t", [128, 8], F32)
        subw_t = sb("subw_t", [128, 128], F32)
        eps_t = sb("eps_t", [128, 1], F32)
        cds = S.dsem("const")

        def ld(eng, dst, src, key, ds):
            return S.op(eng, ("dma_start", dict(out=dst, in_=src)), writes=[key], dsem=ds)

        ld("pool", ident[:], c_ident, "ident", S.dsem("ident"))
        S.group_begin()
        ld("sp", subw_t[:], subw, "subw", cds)
        with ExitStack() as st0:
            lv = sb("lv", [128, 256], F32, st0)
            lj = sb("lj", [128, 128], F32, st0)
            ld("sp", lv[:], lamv, "lv", cds)
            S.group_end()
            S.op("dve", ("memset", dict(ap=eps_t[:], constant=EPS)), writes=["eps"])
            S.op("dve", ("memset", dict(ap=lam_t[:], constant=0.0)), writes=["lam"])
            S.op("dve", ("tensor_tensor", dict(out=lj[:, 0:64], in0=lv[:, 0:64], in1=lv[:, 64:128], op=ALU.mult)),
                 reads=["lv"], writes=["lj"])
            S.op("dve", ("tensor_tensor", dict(out=lj[:, 64:128], in0=lv[:, 128:192], in1=lv[:, 192:256], op=ALU.mult)),
                 reads=["lv"], writes=["lj2"])
            S.op("dve", ("reduce_sum", dict(out=lam_t[:, 0:1], in_=lj[:, 0:64], axis=mybir.AxisListType.X)),
                 reads=["lj"], writes=["lam"])
            S.op("dve", ("reduce_sum", dict(out=lam_t[:, 1:2], in_=lj[:, 64:128], axis=mybir.AxisListType.X)),
                 reads=["lj2"], writes=["lam"])
            S.op("act", ("activation", dict(out=lam_t[:, 2:4], in_=lam_t[:, 0:2], func=AF.Exp)), reads=["lam"], writes=["lam"])
            S.op("dve", ("tensor_tensor", dict(out=lam_t[:, 5:6], in0=lam_t[:, 3:4], in1=lam_t[:, 2:3], op=ALU.subtract)),
                 reads=["lam"], writes=["lam"])
            S.op("dve", ("tensor_scalar", dict(out=lam_t[:, 4:5], in0=lam_t[:, 5:6], scalar1=-LAM_INIT, scalar2=None, op0=ALU.add)),
                 reads=["lam"], writes=["lam"])
            S.op("dve", ("tensor_scalar", dict(out=subw_t[:], in0=subw_t[:], scalar1=1.0 - LAM_INIT, scalar2=None, op0=ALU.mult)),
                 reads=["subw"], writes=["subw"])
            S.barrier()
        nlam = lam_t[:, 4:5]

        def rstd_ops(ss_ap, n, key):
            S.op("act", ("activation", dict(out=ss_ap, in_=ss_ap, func=AF.Ln, bias=eps_t[0:ss_ap.shape[0], 0:1], scale=1.0 / n)),
                 reads=[key, "eps"], writes=[key])
            S.op("act", ("activation", dict(out=ss_ap, in_=ss_ap, func=AF.Exp, scale=-0.5)), reads=[key], writes=[key])

        import os as _os
        _stop = int(_os.environ.get("K_STOP", "0"))

        def ck(n):
            if _stop == n:
                S.dead = True

        with ExitStack() as st:
            st.enter_context(suppress(_Skip))
            win = sb("win", [128, 8, INC], BF16, st)
            nw1_t = sb("nw1_t", [128, D], F32, st)
            gnw_t = sb("gnw_t", [128, 128], F32, st)
            gw_t = sb("gw_t", [32, 256], F32, st)
            tri_t = sb("tri_t", [128, 128], F32, st)
            tri3_t = sb("tri3_t", [128, 128], F32, st)
            dtab = sb("dtab", [128, 8], F32, st)
            xs = [sb(f"xs{i}", [128, D], F32, st) for i in range(2)]
            junk = sb("junk", [128, D], BF16, st)
            ssn = sb("ssn", [128, 2], F32, st)
            xn = sb("xn", [128, D], BF16, st)
            xnTs = [sb(f"xnT{j}", [128, 8, 128], BF16, st) for j in range(3)]
            gjunk = sb("gjunk", [128, 512], BF16, st)
            qst = [sb(f"qst{i}", [128, 4, 2, 128], BF16, st) for i in range(2)]
            kst = [sb(f"kst{i}", [128, 4, 128], BF16, st) for i in range(2)]
            vst = [sb(f"vst{i}", [128, 4, 129], BF16, st) for i in range(2)]
            ogs = [sb(f"ogs{i}", [128, 512], BF16, st) for i in range(2)]
            glrT = sb("glrT", [32, 128], F32, st)
            la_l = [sb(f"la{j}", [128, 256], F32, st) for j in range(2)]
            e1T_l = [sb(f"e1T{j}", [128, 2, 128], F32, st) for j in range(2)]
            e2T_l = [sb(f"e2T{j}", [128, 2, 128], F32, st) for j in range(2)]
            ke_l = [sb(f"ke{j}", [128, 256], F32, st) for j in range(2)]
            kdT_l = [sb(f"kdT{j}", [128, 2, 128], BF16, st) for j in range(2)]
            qdh_l = [sb(f"qdh{j}", [128, 4, 128], BF16, st) for j in range(2)]
            qdc_l = [sb(f"qdc{j}", [128, 2, 2, 128], BF16, st) for j in range(2)]
            kec_l = [sb(f"kec{j}", [128, 2, 256], BF16, st) for j in range(2)]
            gvb_l = [sb(f"gvb{j}", [128, 512], BF16, st) for j in range(2)]
            aT_l = [sb(f"aT{j}", [128, 4, 128], BF16, st) for j in range(2)]
            Sf = sb("Sf", [128, 2, 128], F32, st)
            Sb = sb("Sb", [128, 2, 256], BF16, st)
            gss = sb("gss", [128, 4], F32, st)
            gon = sb("gon", [128, 512], F32, st)
            ger = sb("ger", [128, 512], F32, st)

            p_tr = pst("p_tr", [128, 8, 128], BF16, st)
            p_fm = [pst(f"p_fm{i}", [128, 4, 128], F32, st) for i in range(3)]
            p_tm = [pst(f"p_tm{i}", [128, 512], F32, st) for i in range(2)]
            p_g = [pst(f"p_g{i}", [128, 512], F32, st) for i in range(2)]

            wds = S.dsem("win")
            S.group_begin()
            WG = [(0, 512), (512, 1024), (1024, 1536), (1536, 2048), (2048, 2560), (2560, INC)]
            w_in_v = w_in.rearrange("(kc p) n -> p kc n", p=128)
            for gi, (c0_, c1_) in enumerate(WG):
                ld("pool", win[:, :, c0_:c1_], w_in_v[:, :, c0_:c1_], f"win{gi}", S.dsem(f"win{gi}"))

            def wkey(col0):
                for gi, (c0_, c1_) in enumerate(WG):
                    if c0_ <= col0 < c1_:
                        return [f"win{gi}"]
                raise ValueError(col0)
            ld("sp", nw1_t[:], nw1, "nw1", cds)
            ld("sp", gnw_t[:], gnw, "gnw", cds)
            ld("sp", gw_t[0:17, :], gatew, "gw", cds)
            ld("sp", tri_t[:], c_tri, "tri", cds)
            ld("sp", tri3_t[:], c_tri3, "tri3", cds)
            ld("sp", dtab[:], c_dtab, "dtab", cds)
            S.group_end()
            S.op("dve", ("memset", dict(ap=glrT[:], constant=1.0)), writes=["glrT"])
            for i in range(2):
                S.op("pool", ("memset", dict(ap=qst[i][:], constant=0.0)), writes=[f"qst{i}"])
            for j in range(2):
                S.op("pool", ("memset", dict(ap=qdh_l[j][:], constant=0.0)), writes=[f"qdh{j}"])
                S.op("pool", ("memset", dict(ap=qdc_l[j][:], constant=0.0)), writes=[f"qdc{j}"])
                S.op("pool", ("memset", dict(ap=kec_l[j][:], constant=0.0)), writes=[f"kec{j}"])
            S.op("dve", ("memset", dict(ap=Sf[:], constant=0.0)), writes=["Sf"])
            S.op("dve", ("memset", dict(ap=Sb[:], constant=0.0)), writes=["Sb"])
            S.op("pool", ("memset", dict(ap=xs[0][:], constant=0.0)), writes=["xs0"])

            xds = [S.dsem("xs0"), S.dsem("xs1")]
            sdq = [S.dsem("sq0"), S.dsem("sq1")]
            sdk = [S.dsem("sk0"), S.dsem("sk1")]
            sdv = [S.dsem("sv0"), S.dsem("sv1")]
            sdg = [S.dsem("sg0"), S.dsem("sg1")]

            def load_x(T):
                i = T % 2
                if T == 0:
                    ld("sp", xs[0][0:16, :], meta, "xs0", xds[0])
                else:
                    ld("sp", xs[i][:], x[(T - 1) * 128:T * 128, :], f"xs{i}", xds[i])

            load_x(0)
            def mk_groups(xnT, XT):
                def fm_group(pt, pk, col0, nchunk, m=128):
                    for c in range(nchunk):
                        for kc in range(8):
                            S.op("pe", ("matmul", dict(
                                out=pt[0:m, c, :], lhsT=win[:, kc, col0 + c * 128: col0 + c * 128 + m], rhs=xnT[:, kc, :],
                                start=(kc == 0), stop=(kc == 7))),
                                reads=XT + wkey(col0), writes=[pk], sig=(kc == 7 and c == nchunk - 1))
                def tm_group(pt, pk, col0, n, off=0):
                    for kc in range(8):
                        S.op("pe", ("matmul", dict(out=pt[:, off:off + n], lhsT=xnT[:, kc, :], rhs=win[:, kc, col0:col0 + n],
                                                            start=(kc == 0), stop=(kc == 7))),
                             reads=XT + wkey(col0), writes=[pk], sig=(kc == 7))
                return fm_group, tm_group

            def seg_F1(T, part="ab"):
                i = T % 2
                xk = f"xs{i}"
                xnT = xnTs[T % 3]
                XT = [f"xnTa{T % 3}", f"xnTb{T % 3}"]
                fm_group, tm_group = mk_groups(xnT, XT)
                if "a" in part:
                    if T + 1 < TT:
                        load_x(T + 1)
                    S.op("act", ("activation", dict(out=junk[:], in_=xs[i][:], func=AF.Square, accum_out=ssn[:, 0:1])),
                         reads=[xk], writes=["junk", "ssn"])
                    rstd_ops(ssn[:, 0:1], D, "ssn")
                    S.op("dve", ("scalar_tensor_tensor", dict(out=xn[:], in0=xs[i][:], scalar=ssn[:, 0:1], in1=nw1_t[:],
                                                                     op0=ALU.mult, op1=ALU.mult)),
                         reads=[xk, "ssn", "nw1"], writes=["xn"])
                if "b" not in part:
                    return
                for kc in range(8):
                    S.op("pe", ("transpose", dict(out=p_tr[:, kc, :], in_=xn[:, kc * 128:(kc + 1) * 128], identity=ident[:])),
                         reads=["xn", "ident"], writes=["p_tr"], sig=(kc == 7))
                S.op("act", ("copy", dict(out=xnT[:], in_=p_tr[:])), reads=["p_tr"], writes=XT)

            def seg_F2(T, part="qk"):
                i = T % 2
                xk = f"xs{i}"
                xnT = xnTs[T % 3]
                XT = [f"xnTa{T % 3}", f"xnTb{T % 3}"]
                fm_group, tm_group = mk_groups(xnT, XT)

                if "q" in part:
                    fm_group(p_fm[0], "p_fm0", 0, 4)
                    for m in range(2):
                        S.op("act" if m == 0 else "dve",
                             (("copy", dict(out=qst[i][m * 64:(m + 1) * 64, :, m, :], in_=p_fm[0][m * 64:(m + 1) * 64, :, :]))) if m == 0 else
                             (("tensor_copy", dict(out=qst[i][m * 64:(m + 1) * 64, :, m, :], in_=p_fm[0][m * 64:(m + 1) * 64, :, :]))),
                             reads=["p_fm0"], writes=[f"qst{i}"])
                    S.op("sp", ("dma_start", dict(out=QT[:, :, :, T * 128:(T + 1) * 128].rearrange("h m p t -> p h m t"), in_=qst[i][:])),
                         reads=[f"qst{i}"], writes=[("QT", T)], dsem=sdq[i])
                if "k" not in part:
                    return
                fm_group(p_fm[0], "p_fm0", 512, 4)
                S.op("act", ("copy", dict(out=kst[i][:], in_=p_fm[0][:])), reads=["p_fm0"], writes=[f"kst{i}"])
                S.op("sp", ("dma_start", dict(out=KT[:, :, T * 128:(T + 1) * 128].rearrange("h p t -> p h t"), in_=kst[i][:])),
                     reads=[f"kst{i}"], writes=[("KT", T)], dsem=sdk[i])

            def seg_F3(T):
                i = T % 2
                xk = f"xs{i}"
                xnT = xnTs[T % 3]
                XT = [f"xnTa{T % 3}", f"xnTb{T % 3}"]
                fm_group, tm_group = mk_groups(xnT, XT)
                pv0 = p_fm[0][:].rearrange("p a t -> p (a t)")
                tm_group(pv0, "p_fm0", 1024, 512)
                dcol = 4 if T == 0 else 0
                for h in range(4):
                    S.op("dve", ("tensor_scalar", dict(out=vst[i][:, h, 0:128], in0=pv0[:, h * 128:(h + 1) * 128],
                                                              scalar1=dtab[:, dcol + h:dcol + h + 1], scalar2=None, op0=ALU.mult)),
                         reads=["p_fm0", "dtab"], writes=[f"vst{i}"])
                S.op("dve", ("tensor_copy", dict(out=vst[i][:, :, 128], in_=dtab[:, dcol:dcol + 4])),
                     reads=["dtab"], writes=[f"vst{i}"])
                S.op("sp", ("dma_start", dict(out=VP[:, :, T, :].rearrange("h p c -> p h c"), in_=vst[i][:])),
                     reads=[f"vst{i}"], writes=[("VP", T)], dsem=sdv[i])


            def seg_G1(T, part="all"):
                i = T % 2
                xk = f"xs{i}"
                xnT = xnTs[T % 3]
                XT = [f"xnTa{T % 3}", f"xnTb{T % 3}"]
                fm_group, tm_group = mk_groups(xnT, XT)
                pd = p_fm[1][:].rearrange("p a t -> p (a t)")
                la = la_l[i]
                e1T = e1T_l[i]
                e2T = e2T_l[i]
                ke = ke_l[i]
                kdT = kdT_l[i]
                qdh = qdh_l[i]
                qdc = qdc_l[i]
                kec = kec_l[i]
                gvb = gvb_l[i]
                aT = aT_l[i]
                if part != "rest":
                    for kc in range(8):
                        S.op("pe", ("matmul", dict(out=p_g[0][0:16, 0:128], lhsT=win[:, kc, 3072:3088], rhs=xnT[:, kc, :],
                                                            start=(kc == 0), stop=(kc == 7))),
                             reads=XT + wkey(3072), writes=["p_g0"], sig=(kc == 7))
                    S.op("act", ("copy", dict(out=glrT[0:16, :], in_=p_g[0][0:16, 0:128])), reads=["p_g0"], writes=["glrT"])
                if part == "glr":
                    return
                fm_group(p_fm[2], "p_fm2", 1536, 4)
                S.op("pe", ("matmul", dict(out=p_g[1][:, 0:256], lhsT=glrT[0:17, :], rhs=gw_t[0:17, :], start=True, stop=True)),
                     reads=["glrT", "gw"], writes=["p_g1"])
                S.op("act", ("activation", dict(out=la[:], in_=p_g[1][:, 0:256], func=AF.Exp, scale=-1.0)),
                     reads=["p_g1"], writes=[f"la{i}"])
                S.op("act", ("activation", dict(out=la[:], in_=la[:], func=AF.Ln, bias=1.0)), reads=[f"la{i}"], writes=[f"la{i}"])

            def seg_G2(T, part="ab"):
                i = T % 2
                xk = f"xs{i}"
                xnT = xnTs[T % 3]
                XT = [f"xnTa{T % 3}", f"xnTb{T % 3}"]
                fm_group, tm_group = mk_groups(xnT, XT)
                pd = p_fm[1][:].rearrange("p a t -> p (a t)")
                la = la_l[i]
                e1T = e1T_l[i]
                e2T = e2T_l[i]
                ke = ke_l[i]
                kdT = kdT_l[i]
                qdh = qdh_l[i]
                qdc = qdc_l[i]
                kec = kec_l[i]
                gvb = gvb_l[i]
                aT = aT_l[i]
                if "a" in part:
                    tm_group(p_tm[0], "p_tm0", 2048, 512)
                    S.op("act", ("copy", dict(out=gvb[:], in_=p_tm[0][:])), reads=["p_tm0"], writes=[f"gvb{i}"])
                    kr = 16 if T == 0 else 128
                    for hp in range(2):
                        S.op("pe", ("matmul", dict(out=p_g[0][:, hp * 128:(hp + 1) * 128], lhsT=la[0:kr, hp * 128:(hp + 1) * 128],
                                                            rhs=tri_t[0:kr, :], start=True, stop=True)),
                             reads=[f"la{i}", "tri"], writes=["p_g0"], sig=(hp == 1))
                    S.op("pe", ("matmul", dict(out=p_g[1][:, 256:512], lhsT=tri3_t[0:kr, :], rhs=la[0:kr, :], start=True, stop=True)),
                         reads=[f"la{i}", "tri3"], writes=["p_g1"])
                    p_g0v = p_g[0][:, 0:256].rearrange("p (a t) -> p a t", a=2)
                    S.op("act", ("activation", dict(out=e1T[:], in_=p_g0v, func=AF.Exp, scale=-1.0 / 16)), reads=["p_g0"], writes=[f"e1T{i}"])
                    S.op("act", ("activation", dict(out=e2T[:], in_=p_g0v, func=AF.Exp, scale=1.0 / 16)), reads=["p_g0"], writes=[f"e2T{i}"])
                    S.op("act", ("activation", dict(out=ke[:], in_=p_g[1][:, 256:512], func=AF.Exp, scale=-1.0 / 16)),
                         reads=["p_g1"], writes=[f"ke{i}"])
                    tm_group(p_tm[0], "p_tm0", 1792, 256)
                    S.op("dve", ("tensor_tensor", dict(out=kec[:, 0, :], in0=p_tm[0][:, 0:256], in1=ke[:], op=ALU.mult)),
                         reads=["p_tm0", f"ke{i}"], writes=[f"kec{i}"])
                    S.op("dve", ("tensor_tensor", dict(out=kdT[:], in0=p_fm[2][:, 2:4, :], in1=e2T[:], op=ALU.mult)),
                         reads=["p_fm2", f"e2T{i}"], writes=[f"kdT{i}"])
                    for h in range(4):
                        hp, hl = h // 2, h % 2
                        S.op("dve", ("scalar_tensor_tensor", dict(
                            out=qdh[hl * 64:(hl + 1) * 64, h, :], in0=p_fm[2][hl * 64:(hl + 1) * 64, hp, :], scalar=0.125,
                            in1=e1T[hl * 64:(hl + 1) * 64, hp, :], op0=ALU.mult, op1=ALU.mult)),
                            reads=["p_fm2", f"e1T{i}"], writes=[f"qdh{i}"])
                    S.op("dve", ("scalar_tensor_tensor", dict(
                        out=qdc[:, 0, :, :], in0=p_fm[2][:, 0:2, :], scalar=0.125, in1=e1T[:], op0=ALU.mult, op1=ALU.mult)),
                        reads=["p_fm2", f"e1T{i}"], writes=[f"qdc{i}"])
                if "b" not in part:
                    return
                p_g1v = p_g[1][:].rearrange("p (a t) -> p a t", a=4)
                for h in range(4):
                    S.op("pe", ("matmul", dict(out=p_g1v[:, h, :], lhsT=kdT[:, h // 2, :], rhs=qdh[:, h, :], start=True, stop=True)),
                         reads=[f"kdT{i}", f"qdh{i}"], writes=["p_g1"], sig=(h == 3))
                for h in range(4):
                    S.op("dve", ("tensor_tensor", dict(out=aT[:, h, :], in0=p_g1v[:, h, :], in1=tri_t[:], op=ALU.mult)),
                         reads=["p_g1", "tri"], writes=[f"aT{i}"])

            def seg_G3(T):
                i = T % 2
                xk = f"xs{i}"
                xnT = xnTs[T % 3]
                XT = [f"xnTa{T % 3}", f"xnTb{T % 3}"]
                fm_group, tm_group = mk_groups(xnT, XT)
                pd = p_fm[1][:].rearrange("p a t -> p (a t)")
                la = la_l[i]
                e1T = e1T_l[i]
                e2T = e2T_l[i]
                ke = ke_l[i]
                kdT = kdT_l[i]
                qdh = qdh_l[i]
                qdc = qdc_l[i]
                kec = kec_l[i]
                gvb = gvb_l[i]
                aT = aT_l[i]
                nch = 1
                for hp in range(2):
                    for hl in range(2):
                        h = 2 * hp + hl
                        S.op("pe", ("matmul", dict(out=p_tm[1][:, h * 128:(h + 1) * 128], lhsT=aT[:, h, :], rhs=gvb[:, h * 128:(h + 1) * 128],
                                                          start=(h == 0), stop=(T == 0), skip_group_check=True)),
                             reads=[f"aT{i}", f"gvb{i}"], writes=["p_tm1"], sig=False)
                for cc in range(nch):
                    if T > 0:
                        for hp in range(2):
                            S.op("pe", ("matmul", dict(out=p_tm[1][:, hp * 256:(hp + 1) * 256], lhsT=qdc[:, cc, hp, :], rhs=Sb[:, hp, :],
                                                                       start=False, stop=True, skip_group_check=True)),
                                 reads=[f"qdc{i}", "Sb"], writes=["p_tm1"], sig=(hp == 1))
                    for hp in range(2):
                        S.op("pe", ("matmul", dict(out=pd[:, hp * 256:(hp + 1) * 256], lhsT=kec[:, cc, hp * 128:(hp + 1) * 128],
                                                                   rhs=gvb[:, hp * 256:(hp + 1) * 256], start=True, stop=True)),
                             reads=[f"kec{i}", f"gvb{i}"], writes=["p_fm1"], sig=(hp == 1))
                    dc = 15 if T == 0 else 127
                    for hp in range(2):
                        for hl in range(2):
                            S.op("dve", ("scalar_tensor_tensor", dict(
                                out=Sb[hl * 64:(hl + 1) * 64, hp, hl * 128:(hl + 1) * 128], in0=Sf[hl * 64:(hl + 1) * 64, hp, :],
                                scalar=e1T[hl * 64:(hl + 1) * 64, hp, dc:dc + 1],
                                in1=pd[hl * 64:(hl + 1) * 64, hp * 256 + hl * 128: hp * 256 + (hl + 1) * 128],
                                op0=ALU.mult, op1=ALU.add)),
                                reads=["Sf", f"e1T{i}", "p_fm1"], writes=["Sb"])
                            S.op("dve", ("scalar_tensor_tensor", dict(
                                out=Sf[hl * 64:(hl + 1) * 64, hp, :], in0=Sf[hl * 64:(hl + 1) * 64, hp, :],
                                scalar=e1T[hl * 64:(hl + 1) * 64, hp, dc:dc + 1],
                                in1=pd[hl * 64:(hl + 1) * 64, hp * 256 + hl * 128: hp * 256 + (hl + 1) * 128],
                                op0=ALU.mult, op1=ALU.add)),
                                reads=["Sf", f"e1T{i}", "p_fm1"], writes=["Sf"])

            def seg_G4(T):
                i = T % 2
                xk = f"xs{i}"
                xnT = xnTs[T % 3]
                XT = [f"xnTa{T % 3}", f"xnTb{T % 3}"]
                fm_group, tm_group = mk_groups(xnT, XT)
                pd = p_fm[1][:].rearrange("p a t -> p (a t)")
                la = la_l[i]
                e1T = e1T_l[i]
                e2T = e2T_l[i]
                ke = ke_l[i]
                kdT = kdT_l[i]
                qdh = qdh_l[i]
                qdc = qdc_l[i]
                kec = kec_l[i]
                gvb = gvb_l[i]
                aT = aT_l[i]
                for h in range(4):
                    S.op("act", ("activation", dict(out=gjunk[:, h * 128:(h + 1) * 128], in_=p_tm[1][:, h * 128:(h + 1) * 128],
                                                           func=AF.Square, accum_out=gss[:, h:h + 1])),
                         reads=["p_tm1"], writes=["gjunk", "gss"])
                rstd_ops(gss[:, 0:4], 128, "gss")
                for h in range(4):
                    S.op("dve", ("scalar_tensor_tensor", dict(out=gon[:, h * 128:(h + 1) * 128], in0=p_tm[1][:, h * 128:(h + 1) * 128],
                                                                     scalar=gss[:, h:h + 1], in1=gnw_t[:], op0=ALU.mult, op1=ALU.mult)),
                         reads=["p_tm1", "gss", "gnw"], writes=["gon"])
                tm_group(pd, "p_fm1", 2560, 512)
                S.op("act", ("activation", dict(out=ger[:], in_=pd, func=AF.Exp, scale=-1.0)), reads=["p_fm1"], writes=["ger"])
                S.op("act", ("activation", dict(out=ger[:], in_=ger[:], func=AF.Ln, bias=1.0)), reads=["ger"], writes=["ger"])
                S.op("act", ("activation", dict(out=ger[:], in_=ger[:], func=AF.Exp, scale=-1.0)), reads=["ger"], writes=["ger"])
                S.op("dve", ("tensor_tensor", dict(out=ger[:], in0=ger[:], in1=pd, op=ALU.mult)),
                     reads=["ger", "p_fm1"], writes=["ger"])
                S.op("pool", ("tensor_tensor", dict(out=ogs[i][:], in0=gon[:], in1=ger[:], op=ALU.mult)),
                     reads=["gon", "ger"], writes=[f"ogs{i}"])
                S.op("sp", ("dma_start", dict(out=OG[T * 128:(T + 1) * 128, :], in_=ogs[i][:])),
                     reads=[f"ogs{i}"], writes=[("OG", T)], dsem=sdg[i])

            seg_F1(0)
            seg_F2(0)
            seg_F3(0)
            seg_G1(0)
            seg_G2(0)
            if TT > 1:
                seg_F1(1)
                seg_F2(1)
                seg_F3(1)
                seg_G1(1, "glr")
            for T in range(TT):
                if T + 2 < TT:
                    seg_F1(T + 2, "a")
                if T + 1 < TT:
                    seg_G1(T + 1, "rest")
                seg_G3(T)
                if T + 2 < TT:
                    seg_F1(T + 2, "b")
                if T + 1 < TT:
                    seg_G2(T + 1, "a")
                if T + 2 < TT:
                    seg_F2(T + 2, "q")
                if T + 1 < TT:
                    seg_G2(T + 1, "b")
                if T + 2 < TT:
                    seg_F2(T + 2, "k")
                seg_G4(T)
                if T + 2 < TT:
                    seg_F3(T + 2)
                    seg_G1(T + 2, "glr")
            S.barrier()

        wout = sb("wout", [128, 8, D], BF16)
        wup = sb("wup", [128, 8, 2 * DFF], BF16)
        nw2c = sb("nw2c", [128, 8], F32)
        if phases >= 3:
            w2s = S.dsem("w2")
            S.group_begin()
            ld("sp", nw2c[:], nw2, "nw2c", S.dsem("nw2c"))
            for kc in range(8):
                ld("pool", wout[:, kc, :], w_out[kc * 128:(kc + 1) * 128, :], f"wout{kc}", w2s)
            for kc in range(8):
                for hh in range(2):
                    ld("pool", wup[:, kc, hh * DFF:(hh + 1) * DFF], w_up[kc * 128:(kc + 1) * 128, hh * DFF:(hh + 1) * DFF], f"wup{kc}_{hh}", w2s)
            S.group_end()

        with ExitStack() as st:
            st.enter_context(suppress(_Skip))
            if phases < 2:
                raise _Skip()
            KTh_l = [sb(f"KTh{j}", [128, LT], BF16, st) for j in range(2)]
            VPh_l = [sb(f"VPh{j}", [128, TT, 129], BF16, st) for j in range(2)]
            dbias = sb("dbias", [128, 4, 2, 256], F32, st)
            mbias = sb("mbias", [128, 4, 16], F32, st)
            qb = [sb(f"qb{i}", [128, 2, 256], BF16, st) for i in range(2)]
            NPT = 4
            pt = [sb(f"pt{i}", [128, 512], BF16, st) for i in range(NPT)]
            sbias = [sb(f"sbias{i}", [128, 512], F32, st) for i in range(2)]
            otmp = sb("otmp", [128, 128], F32, st)
            ofin = sb("ofin", [128, 2, 128], F32, st)
            zr = sb("zr", [128, 2, 4], F32, st)
            oss = sb("oss", [128, 2], F32, st)
            ods = [sb(f"ods{i}", [128, 128], BF16, st) for i in range(2)]
            p_s = [pst(f"p_s{i}", [128, 512], F32, st) for i in range(4)]
            p_o_full = [pst(f"p_o{i}", [128, 512], F32, st) for i in range(4)]
            p_o = [t[:, 0:258].rearrange("p (a c) -> p a c", a=2) for t in p_o_full]

            S.group_begin()
            ld("sp", dbias[:].rearrange("p a b c -> p (a b c)"), c_dbias, "dbias", cds)
            ld("sp", mbias[:].rearrange("p a b -> p (a b)"), c_mbias, "mbias", cds)
            S.group_end()
            kds = [S.dsem("kk0"), S.dsem("kk1")]
            vds = [S.dsem("vv0"), S.dsem("vv1")]

            def load_kv(hh):
                ld("sp", KTh_l[hh % 2][:], KT[hh], f"KTh{hh % 2}", kds[hh % 2])
                ld("sp", VPh_l[hh % 2][:], VP[hh], f"VPh{hh % 2}", vds[hh % 2])

            load_kv(0)
            qds = [S.dsem("qb0"), S.dsem("qb1")]
            ods_s = [S.dsem("od0"), S.dsem("od1")]
            uctr = [0, 0, 0, 0]

            LA = 3
            for h in range(4):
                a = SLOPES[h]
                if h == 1 and phases >= 3:
                    for kc in range(8):
                        for hh in range(2):
                            S.op("act", ("activation", dict(out=wup[:, kc, hh * DFF:(hh + 1) * DFF], in_=wup[:, kc, hh * DFF:(hh + 1) * DFF],
                                                           func=AF.Copy, scale=nw2c[:, kc:kc + 1])),
                                 reads=["nw2c"], writes=[f"wup{kc}_{hh}"])
                KTh, VPh = KTh_l[h % 2], VPh_l[h % 2]
                KTk, VPk = f"KTh{h % 2}", f"VPh{h % 2}"
                if h + 1 < 4:
                    load_kv(h + 1)
                blks = []
                for I in [-1] + list(range(NB)):
                    bi = uctr[1] % 2
                    uctr[1] += 1
                    units = []
                    if I < 0:
                        units.append((0, 16, "mdiag", 0.0))
                    else:
                        cm = -a * (256 * I + 1)
                        if cm >= SKIP_C:
                            units.append((0, 16, "past", cm))
                        for Tk in range(1, 2 * I + 1):
                            c = -a * (256 * I - 128 * (Tk - 1) - 127)
                            if c < SKIP_C:
                                continue
                            units.append((Tk, 128, "past", c))
                        units.append((2 * I + 1, 128, "diag0", 0.0))
                        units.append((2 * I + 2, 128, "diag1", 0.0))
                    blks.append(dict(I=I, bi=bi, pob=bi * 2, nq=(16 if I < 0 else 256), nqt=(1 if I < 0 else 2),
                                     c0=(0 if I < 0 else 128 + 256 * I), units=units))
                flat = []
                for bidx, bl in enumerate(blks):
                    for ui, un in enumerate(bl["units"]):
                        flat.append((bidx, ui, un))

                def load_q(bl):
                    bi, c0, nq = bl["bi"], bl["c0"], bl["nq"]
                    S.op("sp", ("dma_start", dict(out=qb[bi][:, :, 0:nq], in_=QT[h, :, :, c0:c0 + nq].rearrange("m p t -> p m t"))),
                         writes=[f"qb{bi}"], dsem=qds[bi])

                def st_qk(fi):
                    bidx, ui, (Tk, nk, kind, c) = flat[fi]
                    bl = blks[bidx]
                    if ui == 0 and bidx + 1 < len(blks):
                        load_q(blks[bidx + 1])
                    u = uctr[0] + fi
                    ps_, psk = p_s[u % 4], f"p_s{u % 4}"
                    bi, nq = bl["bi"], bl["nq"]
                    if nq == 256:
                        S.op("pe", ("matmul", dict(out=ps_[0:nk, 0:512], lhsT=KTh[:, Tk * 128:Tk * 128 + nk],
                                                  rhs=qb[bi][:].rearrange("p m t -> p (m t)"), start=True, stop=True)),
                             reads=[KTk, f"qb{bi}"], writes=[psk])
                    else:
                        for m in range(2):
                            S.op("pe", ("matmul", dict(out=ps_[0:nk, m * 256:m * 256 + nq], lhsT=KTh[:, Tk * 128:Tk * 128 + nk], rhs=qb[bi][:, m, 0:nq],
                                                      start=True, stop=True)),
                                 reads=[KTk, f"qb{bi}"], writes=[psk], sig=(m == 1))

                def st_exp_pv(fi):
                    bidx, ui, (Tk, nk, kind, c) = flat[fi]
                    bl = blks[bidx]
                    I, bi, pob, nq, nqt = bl["I"], bl["bi"], bl["pob"], bl["nq"], bl["nqt"]
                    u = uctr[0] + fi
                    ps_, psk = p_s[u % 4], f"p_s{u % 4}"
                    ptt, ptk = pt[u % NPT], f"pt{u % NPT}"
                    if kind == "past":
                        S.op("act", ("activation", dict(out=ptt[0:nk, :], in_=ps_[0:nk, :], func=AF.Exp, bias=float(c), scale=0.125)),
                             reads=[psk], writes=[ptk])
                    elif kind == "mdiag":
                        sbt, sbk = sbias[u % 2], f"sbias{u % 2}"
                        for m in range(2):
                            S.op("dve", ("scalar_tensor_tensor", dict(
                                out=sbt[0:16, m * 256:m * 256 + 16], in0=ps_[0:16, m * 256:m * 256 + 16], scalar=0.125, in1=mbias[0:16, h, :],
                                op0=ALU.mult, op1=ALU.add)), reads=[psk, "mbias"], writes=[sbk])
                        for m in range(2):
                            S.op("act", ("activation", dict(out=ptt[0:16, m * 256:m * 256 + 16], in_=sbt[0:16, m * 256:m * 256 + 16], func=AF.Exp)),
                                 reads=[sbk], writes=[ptk])
                    else:
                        dt_ = 0 if kind == "diag0" else 1
                        sbt, sbk = sbias[u % 2], f"sbias{u % 2}"
                        for m in range(2):
                            S.op("dve", ("scalar_tensor_tensor", dict(
                                out=sbt[:, m * 256:(m + 1) * 256], in0=ps_[:, m * 256:(m + 1) * 256], scalar=0.125, in1=dbias[:, h, dt_, :],
                                op0=ALU.mult, op1=ALU.add)), reads=[psk, "dbias"], writes=[sbk])
                        S.op("act", ("activation", dict(out=ptt[:], in_=sbt[:], func=AF.Exp)), reads=[sbk], writes=[ptk])
                    first = (ui == 0)
                    last = (ui == len(bl["units"]) - 1)
                    mq = 16 if I < 0 else 128
                    for qt in range(nqt):
                        if kind == "diag1" and qt == 0:
                            continue
                        lastq = last or (kind == "diag0" and qt == 0)
                        for m in range(2):
                            S.op("pe", ("matmul", dict(
                                out=p_o[pob + qt][0:mq, m, :], lhsT=ptt[0:nk, m * 256 + qt * 128: m * 256 + qt * 128 + mq], rhs=VPh[0:nk, Tk, :],
                                start=(first and m == 0), stop=lastq, skip_group_check=True)),
                                reads=[ptk, VPk], writes=[f"p_o{pob + qt}"], sig=(m == 1))
                    if not last:
                        return None

                    def epilogue(I=I, pob=pob, nqt=nqt, mq=mq):
                        for qt in range(nqt):
                            po = p_o[pob + qt]
                            pok = f"p_o{pob + qt}"
                            S.op("dve", ("reciprocal", dict(out=zr[0:mq, qt, 0:2], in_=po[0:mq, :, 128])), reads=[pok], writes=["zr"])
                            S.op("dve", ("tensor_tensor", dict(out=zr[0:mq, qt, 2:3], in0=zr[0:mq, qt, 1:2], in1=nlam[0:mq, :], op=ALU.mult)),
                                 reads=["zr"], writes=["zr"])
                            S.op("dve", ("tensor_scalar", dict(out=otmp[0:mq, :], in0=po[0:mq, 1, 0:128], scalar1=zr[0:mq, qt, 2:3], scalar2=None, op0=ALU.mult)),
                                 reads=[pok, "zr"], writes=["otmp"])
                            S.op("dve", ("scalar_tensor_tensor", dict(out=ofin[0:mq, qt, :], in0=po[0:mq, 0, 0:128], scalar=zr[0:mq, qt, 0:1], in1=otmp[0:mq, :],
                                                                     op0=ALU.mult, op1=ALU.add)),
                                 reads=[pok, "zr", "otmp"], writes=[f"ofin{qt}"])
                            S.op("dve", ("scalar_tensor_tensor", dict(out=otmp[0:mq, :], in0=ofin[0:mq, qt, :], scalar=1.0, in1=ofin[0:mq, qt, :],
                                                                     op0=ALU.mult, op1=ALU.mult, accum_out=oss[0:mq, qt:qt + 1])),
                                 reads=[f"ofin{qt}"], writes=["otmp", "oss"])
                        rstd_ops(oss[0:mq, 0:nqt], 128, "oss")
                        for qt in range(nqt):
                            oi = uctr[2] % 2
                            uctr[2] += 1
                            S.op("dve", ("scalar_tensor_tensor", dict(out=ods[oi][0:mq, :], in0=ofin[0:mq, qt, :], scalar=oss[0:mq, qt:qt + 1], in1=subw_t[0:mq, :],
                                                                     op0=ALU.mult, op1=ALU.mult)),
                                 reads=[f"ofin{qt}", "oss", "subw"], writes=[f"ods{oi}"])
                            r0 = 0 if I < 0 else 128 + 256 * I + 128 * qt
                            S.op("sp", ("dma_start", dict(out=OD[r0:r0 + mq, h * 128:(h + 1) * 128], in_=ods[oi][0:mq, :])),
                                 reads=[f"ods{oi}"], writes=[("OD", h, r0)], dsem=ods_s[oi])
                    return epilogue

                load_q(blks[0])
                n = len(flat)
                pending = []
                for idx in range(n + LA):
                    if idx < n:
                        st_qk(idx)
                    if idx - LA >= 0:
                        fb, fu, _ = flat[idx - LA]
                        if fu == 0:
                            keep = []
                            for pe_ in pending:
                                if pe_[2] == blks[fb]["pob"]:
                                    pe_[1]()
                                else:
                                    keep.append(pe_)
                            pending[:] = keep
                        ep = st_exp_pv(idx - LA)
                        if ep is not None:
                            nxt_units = len(blks[fb + 1]["units"]) if fb + 1 < len(blks) else 0
                            pending.append((idx + max(3, (2 * nxt_units) // 3), ep, blks[fb]["pob"]))
                    while pending and pending[0][0] <= idx:
                        pending.pop(0)[1]()
                for pe_ in pending:
                    pe_[1]()
                uctr[0] += n
            S.barrier()

        with ExitStack() as st:
            st.enter_context(suppress(_Skip))
            if phases < 3:
                raise _Skip()
            wdn = sb("wdn", [128, 22, D], BF16, st)
            fnw_t = sb("fnw_t", [128, D], F32, st)
            cv = sb("cv", [128, NCH, 4], F32, st)
            mix = sb("mix", [128, D], BF16, st)
            mixT = sb("mixT", [128, 8, 128], BF16, st)
            h1 = [[sb(f"h1_{p}_{t}", [128, D], F32, st) for t in range(2)] for p in range(2)]
            xn2 = sb("xn2", [128, D], BF16, st)
            x2Ts = [sb(f"x2T{j}", [128, 8, 258], BF16, st) for j in range(2)]
            yv = [sb(f"yv{i}", [128, 256], F32, st) for i in range(3)]
            yg = [sb(f"yg{i}", [128, 256], F32, st) for i in range(3)]
            actT = sb("actT", [128, 22, 256], BF16, st)
            ss2 = sb("ss2", [128, 2], F32, st)
            p_t = pst("p_t", [128, 8, 128], BF16, st)
            p_op = pst("p_op", [128, 512], F32, st)
            p_u = [pst(f"p_u{i}", [128, 512], F32, st) for i in range(6)]
            p_d = [p_u[4], p_u[5]]

            w3s = S.dsem("w3")
            S.group_begin()
            ld("sp", fnw_t[:], fnw, "fnw", cds)
            ld("sp", cv[:].rearrange("p a b -> p (a b)"), convt, "cv", cds)
            for c in range(22):
                ld("pool", wdn[:, c, :], w_down[c * 128:(c + 1) * 128, :], f"wdn{c}", w3s)
            S.group_end()
            WOUT = [f"wout{kc}" for kc in range(8)]
            WUP = [f"wup{kc}_{hh}" for kc in range(8) for hh in range(2)]
            WDN = [f"wdn{c}" for c in range(22)]
            S.op("dve", ("memset", dict(ap=x2Ts[0][:, :, 0:2], constant=0.0)), writes=["x2T0", "x2Tb0"])
            S.op("pool", ("memset", dict(ap=h1[0][0][:], constant=0.0)), writes=["h1_0_0_0", "h1_0_0_1"])
            S.op("pool", ("memset", dict(ap=mix[:], constant=0.0)), writes=["mix", "mixb"])

            mds = S.dsem("mix")
            hds = [[S.dsem(f"h1{p}{t}") for t in range(2)] for p in range(2)]
            ods2 = [[S.dsem(f"o{p}{t}") for t in range(2)] for p in range(2)]
            blocks2 = [[0]] + [[2 * B + 1, 2 * B + 2] for B in range(NB)]

            def front_hist(bidx):
                par = bidx % 2
                if bidx > 0:
                    pn = 16 if bidx == 1 else 256
                    S.op("pool", ("tensor_copy", dict(out=x2Ts[par][:, :, 0:2], in_=x2Ts[1 - par][:, :, pn:pn + 2])),
                         reads=[f"x2T{1 - par}", f"x2Tb{1 - par}"], writes=[f"x2T{par}", f"x2Tb{par}"])

            def front_tile(bidx, ti, part):
                tiles = blocks2[bidx]
                par = bidx % 2
                x2T = x2Ts[par]
                T = tiles[ti]
                nt = 16 if T == 0 else 128
                hk = [f"h1_{par}_{ti}_0", f"h1_{par}_{ti}_1"]
                hb = h1[par][ti]
                if part == "A0":
                    S.group_begin()
                    S.op("sp", ("dma_start", dict(out=mix[0:nt, 0:512], in_=OD[T * 128:T * 128 + nt, :])), writes=["mix"], dsem=mds)
                    S.op("sp", ("dma_start", dict(out=mix[0:nt, 512:1024], in_=OG[T * 128:T * 128 + nt, :])), writes=["mixb"], dsem=mds)
                    S.group_end()
                    if T == 0:
                        S.op("sp", ("dma_start", dict(out=hb[0:16, :], in_=meta)), writes=hk, dsem=hds[par][ti])
                    else:
                        S.op("sp", ("dma_start", dict(out=hb[:], in_=x[(T - 1) * 128:T * 128, :])), writes=hk, dsem=hds[par][ti])
                elif part == "A":
                    for kc in range(8):
                        S.op("pe", ("transpose", dict(out=p_t[:, kc, :], in_=mix[:, kc * 128:(kc + 1) * 128], identity=ident[:])),
                             reads=["mix", "mixb", "ident"], writes=["p_t"], sig=(kc == 7))
                    S.op("act", ("copy", dict(out=mixT[:], in_=p_t[:])), reads=["p_t"], writes=["mixTa", "mixTb"])
                elif part in ("B0", "B1"):
                    half = 0 if part == "B0" else 1
                    for kc in range(8):
                        S.op("pe", ("matmul", dict(out=p_op[:], lhsT=mixT[:, kc, :], rhs=wout[:, kc, half * 512:(half + 1) * 512],
                                                  start=(kc == 0), stop=(kc == 7))),
                             reads=["mixTa", "mixTb"] + WOUT, writes=["p_op"], sig=(kc == 7))
                    S.op("dve", ("tensor_tensor", dict(out=hb[:, half * 512:(half + 1) * 512], in0=p_op[:],
                                                      in1=hb[:, half * 512:(half + 1) * 512], op=ALU.add)),
                         reads=["p_op", hk[half]], writes=[hk[half]])
                    if half == 1:
                        S.op("act", ("activation", dict(out=xn2[:], in_=hb[:], func=AF.Square, accum_out=ss2[:, 0:1])),
                             reads=hk, writes=["xn2", "ss2"])
                        rstd_ops(ss2[:, 0:1], D, "ss2")
                        S.op("dve", ("tensor_scalar", dict(out=xn2[:], in0=hb[:], scalar1=ss2[:, 0:1], scalar2=None, op0=ALU.mult)),
                             reads=hk + ["ss2"], writes=["xn2"])
                else:
                    for kc in range(8):
                        S.op("pe", ("transpose", dict(out=p_t[:, kc, :], in_=xn2[:, kc * 128:(kc + 1) * 128], identity=ident[:])),
                             reads=["xn2", "ident"], writes=["p_t"], sig=(kc == 7))
                    S.op("dve", ("tensor_copy", dict(out=x2T[:, :, 2 + ti * 128:2 + ti * 128 + nt], in_=p_t[:, :, 0:nt])),
                         reads=["p_t"], writes=[f"x2T{par}", f"x2Tb{par}"])

            def ffn(bidx):
                ntok = 16 if bidx == 0 else 256
                x2T = x2Ts[bidx % 2]
                XB = [f"x2T{bidx % 2}", f"x2Tb{bidx % 2}"]

                def stage1(c):
                    pr = c % 3
                    pb = (c % 2) if c < 4 else (c - 2) % 3
                    pv_, pg_ = p_u[2 * pb], p_u[2 * pb + 1]
                    pvk, pgk = f"p_u{2 * pb}", f"p_u{2 * pb + 1}"
                    for (pp, pk, ch) in ((pv_, pvk, c), (pg_, pgk, c + 22)):
                        for kc in range(8):
                            S.op("pe", ("matmul", dict(out=pp[:, 0:ntok + 2], lhsT=wup[:, kc, ch * 128:(ch + 1) * 128], rhs=x2T[:, kc, 0:ntok + 2],
                                                      start=(kc == 0), stop=(kc == 7))),
                                 reads=XB + WUP, writes=[pk], sig=(kc == 7))
                    for (pp, pk, ch, y_, yk) in ((pv_, pvk, c, yv[pr], f"yv{pr}"), (pg_, pgk, c + 22, yg[pr], f"yg{pr}")):
                        S.op("act", ("activation", dict(out=y_[:, 0:ntok], in_=pp[:, 2:ntok + 2], func=AF.Identity,
                                                       bias=cv[:, ch, 3:4], scale=cv[:, ch, 2:3])),
                             reads=[pk, "cv"], writes=[yk])
                        S.op("dve", ("scalar_tensor_tensor", dict(out=y_[:, 0:ntok], in0=pp[:, 1:ntok + 1], scalar=cv[:, ch, 1:2], in1=y_[:, 0:ntok],
                                                                 op0=ALU.mult, op1=ALU.add)),
                             reads=[pk, "cv", yk], writes=[yk])
                        S.op("dve", ("scalar_tensor_tensor", dict(out=y_[:, 0:ntok], in0=pp[:, 0:ntok], scalar=cv[:, ch, 0:1], in1=y_[:, 0:ntok],
                                                                 op0=ALU.mult, op1=ALU.add)),
                             reads=[pk, "cv", yk], writes=[yk])

                def stage2(c):
                    pr = c % 3
                    S.op("act", ("activation", dict(out=yg[pr][:, 0:ntok], in_=yg[pr][:, 0:ntok], func=AF.Silu)),
                         reads=[f"yg{pr}"], writes=[f"yg{pr}"])
                    S.op("pool", ("tensor_tensor", dict(out=actT[:, c, 0:ntok], in0=yg[pr][:, 0:ntok], in1=yv[pr][:, 0:ntok], op=ALU.mult)),
                         reads=[f"yg{pr}", f"yv{pr}"], writes=[f"actT{c}"])

                nxt = bidx + 1 if bidx + 1 < len(blocks2) else None
                for c in range(22):
                    stage1(c)
                    if c >= 2:
                        stage2(c - 2)
                    if nxt is not None:
                        for ti_ in range(len(blocks2[nxt])):
                            c0 = 2 + 9 * ti_
                            if c == c0 - 2:
                                front_tile(nxt, ti_, "A0")
                            elif c == c0:
                                if ti_ == 0:
                                    front_hist(nxt)
                                front_tile(nxt, ti_, "A")
                            elif c == c0 + 2:
                                front_tile(nxt, ti_, "B0")
                            elif c == c0 + 4:
                                front_tile(nxt, ti_, "B1")
                            elif c == c0 + 7:
                                front_tile(nxt, ti_, "C")
                stage2(20)
                stage2(21)

            def down(bidx, ti):
                T = blocks2[bidx][ti]
                if T == 0:
                    return
                par = bidx % 2
                hb = h1[par][ti]
                hk = [f"h1_{par}_{ti}_0", f"h1_{par}_{ti}_1"]
                for half in range(2):
                    for c in range(22):
                        S.op("pe", ("matmul", dict(out=p_d[half][:], lhsT=actT[:, c, ti * 128:(ti + 1) * 128], rhs=wdn[:, c, half * 512:(half + 1) * 512],
                                                  start=(c == 0), stop=(c == 21))),
                             reads=[f"actT{c}"] + WDN, writes=[f"p_u{4 + half}"], sig=(c == 21))
                    S.op("dve", ("tensor_tensor", dict(out=hb[:, half * 512:(half + 1) * 512], in0=p_d[half][:],
                                                      in1=hb[:, half * 512:(half + 1) * 512], op=ALU.add)),
                         reads=[f"p_u{4 + half}", hk[half]], writes=[hk[half]])
                S.op("act", ("activation", dict(out=xn2[:], in_=hb[:], func=AF.Square, accum_out=ss2[:, 1:2])),
                     reads=hk, writes=["xn2", "ss2b"])
                rstd_ops(ss2[:, 1:2], D, "ss2b")
                S.op("dve", ("scalar_tensor_tensor", dict(out=hb[:], in0=hb[:], scalar=ss2[:, 1:2], in1=fnw_t[:], op0=ALU.mult, op1=ALU.mult)),
                     reads=hk + ["ss2b", "fnw"], writes=hk)
                ev = S.op("sp", ("dma_start", dict(out=out[(T - 1) * 128:T * 128, :], in_=hb[:])),
                          reads=hk, writes=[("out", T)], dsem=ods2[par][ti])
                S.out_events.append(ev)

            front_hist(0)
            for part_ in ("A0", "A", "B0", "B1", "C"):
                front_tile(0, 0, part_)
            for bidx in range(len(blocks2)):
                ffn(bidx)
                down(bidx, 0)
                if len(blocks2[bidx]) > 1:
                    down(bidx, 1)
            S.finish()
        S.run()
    return nc


_NC_CACHE = {}


def make_in_maps(inputs, NT, ncores):
    f = lambda a: np.ascontiguousarray(np.asarray(a, dtype=np.float32))
    t = host_tables()
    rep = lambda v: np.ascontiguousarray(np.broadcast_to(f(v).reshape(1, -1), (128, f(v).size)))
    lamv = np.concatenate([f(inputs["lambda_q1"])[0], f(inputs["lambda_k1"])[0], f(inputs["lambda_q2"])[0], f(inputs["lambda_k2"])[0]])
    cw = f(inputs["conv_w"])[0]
    cb = f(inputs["conv_b"])[0]
    convt = np.concatenate([cw, cb[None, :]], 0)
    convt = convt.reshape(4, NCH, 128).transpose(2, 1, 0)
    common = {
        "meta": f(inputs["meta_tokens"]),
        "w_in": f(inputs["w_in"])[0],
        "w_out": f(inputs["w_out"])[0],
        "w_up": f(inputs["w_up"])[0],
        "w_down": f(inputs["w_down"])[0],
        "nw1": rep(inputs["norm1_w"]),
        "nw2": np.ascontiguousarray(f(inputs["norm2_w"]).reshape(8, 128).T),
        "fnw": rep(inputs["final_norm_w"]),
        "lamv": rep(lamv),
        "subw": rep(inputs["da_subln_w"]),
        "gnw": rep(inputs["gla_norm_w"]),
        "gatew": np.ascontiguousarray(np.concatenate([f(inputs["gla_gate_w"])[0], f(inputs["gla_gate_b"])], 0)),
        "convt": np.ascontiguousarray(convt.reshape(128, NCH * 4)),
    }
    common.update(t)
    xx = f(inputs["x"])
    maps = []
    for b in range(ncores):
        m = dict(common)
        m["x"] = np.ascontiguousarray(xx[b])
        maps.append(m)
    return maps


def kernel(**inputs):
    x = np.asarray(inputs["x"])
    B, SEQ, _ = x.shape
    NT = SEQ // 128
    if NT not in _NC_CACHE:
        _NC_CACHE[NT] = build_nc(NT)
    nc = _NC_CACHE[NT]
    maps = make_in_maps(inputs, NT, B)
    res = run_bass_kernel_spmd(nc, maps, core_ids=list(range(B)))
    return np.stack([np.asarray(r["out"]).reshape(SEQ, D) for r in res.results], 0).astype(np.float32)
```

```python
import math
from contextlib import ExitStack, suppress

import numpy as np
import concourse.bass as bass
import concourse.mybir as mybir
from concourse.bass_utils import run_bass_kernel_spmd

F32 = mybir.dt.float32
BF16 = mybir.dt.bfloat16
AF = mybir.ActivationFunctionType
ALU = mybir.AluOpType

D = 1024
NMETA = 16
INC = 3088
DFF = 2816
NCH = 44
EPS = 1e-6
SLOPES = [2.0 ** (-2.0 * (h + 1)) for h in range(4)]
LAM_INIT = 0.8 - 0.6 * math.exp(0.0)
NEG = -30000.0
SKIP_C = -100.0


class _Skip(Exception):
    pass


class DSem:
    def __init__(self, sem):
        self.sem = sem
        self.cnt = 0


class Sched:
    ENG = ("pe", "act", "dve", "pool", "sp")

    def __init__(self, nc, es):
        self.nc, self.es = nc, es
        self.q = {e: [] for e in self.ENG}
        self.sem, self.cnt = {}, {}
        self.nsem = 0
        for e in self.ENG:
            self.new_sem(e)
        self.waited = {e: {} for e in self.ENG}
        self.last_w = {}
        self.readers = {}
        self.sems = {}
        self.out_events = []
        self._grp = None
        self.dead = False
        self.kind = {}

    def group_begin(self):
        self._grp = []

    def group_end(self):
        for ev, ds in self._grp:
            ev[1] = ds.cnt
        self._grp = None

    def new_sem(self, e):
        self.nsem += 1
        self.sem[e] = self.es.enter_context(self.nc.semaphore(f"s{self.nsem}_{e}"))
        self.cnt[e] = 0

    def dsem(self, name):
        self.nsem += 1
        return DSem(self.es.enter_context(self.nc.semaphore(f"d{self.nsem}_{name}")))

    def _wait(self, eng, evs):
        need = {}
        for ev in evs:
            if ev is None:
                continue
            s, v, e = ev
            if e == "pe" and eng == "pe":
                continue
            k = id(s)
            if k not in need or need[k][1] < v:
                need[k] = (s, v)
        for k, (s, v) in need.items():
            if self.waited[eng].get(k, 0) >= v:
                continue
            self.waited[eng][k] = v
            self.q[eng].append(lambda E, s=s, v=v: E.wait_ge(s, v))

    def op(self, eng, fn, reads=(), writes=(), sig=True, dsem=None):
        if self.dead:
            return None
        preads = [k for k in reads if isinstance(k, str) and k.startswith("p_")]
        reads = [k for k in reads if not (isinstance(k, str) and k.startswith("p_"))]
        deps = []
        for k in reads:
            deps.append(self.last_w.get(k))
        for k in writes:
            deps.append(self.last_w.get(k))
            deps.extend(self.readers.get(k, {}).values())
        for k in preads:
            ev0 = self.last_w.get(k)
            if ev0 is not None and not (self.kind.get(k) == "r" and ev0[2] == eng):
                deps.append(ev0)
        self._wait(eng, deps)
        if dsem is not None:
            dsem.cnt += 16
            ev = [dsem.sem, dsem.cnt, "dma"]
            if self._grp is not None:
                self._grp.append((ev, dsem))
            self.q[eng].append(lambda E, fn=fn, s=dsem.sem: getattr(E, fn[0])(**fn[1]).then_inc(s, 16))
        elif sig:
            self.cnt[eng] += 1
            ev = [self.sem[eng], self.cnt[eng], eng]
            self.q[eng].append(lambda E, fn=fn, s=self.sem[eng]: getattr(E, fn[0])(**fn[1]).then_inc(s, 1))
        else:
            ev = [self.sem[eng], self.cnt[eng] + 1, eng]
            self.q[eng].append(lambda E, fn=fn: getattr(E, fn[0])(**fn[1]))
        for k in reads:
            self.readers.setdefault(k, {})[id(ev[0])] = ev
        for k in writes:
            self.last_w[k] = ev
            self.readers[k] = {}
            self.kind[k] = "w"
        for k in preads:
            self.last_w[k] = ev
            self.readers[k] = {}
            self.kind[k] = "r"
        return ev

    def barrier(self):
        self.dead = False
        evs = []
        for k, ev in self.last_w.items():
            evs.append(ev)
        for k, d in self.readers.items():
            evs.extend(d.values())
        for e in self.ENG:
            self._wait(e, evs)
        self.last_w = {}
        self.readers = {}

    def finish(self):
        self._wait("sp", self.out_events)

    def run(self):
        block = self.es.enter_context(self.nc.Block())
        q = self.q

        @block.tensor
        def _(E):
            for f in q["pe"]:
                f(E)

        @block.scalar
        def _(E):
            for f in q["act"]:
                f(E)

        @block.vector
        def _(E):
            for f in q["dve"]:
                f(E)

        @block.gpsimd
        def _(E):
            for f in q["pool"]:
                f(E)

        @block.sync
        def _(E):
            for f in q["sp"]:
                f(E)


def host_tables():
    t = {}
    t["c_ident"] = np.eye(128, dtype=np.float32)
    s = np.arange(128)[:, None]
    c = np.arange(128)[None, :]
    same = (s // 128) == (c // 128)
    t["c_tri"] = ((s <= c) & same).astype(np.float32)
    t["c_tri3"] = ((s > c) & same).astype(np.float32)
    db = np.zeros((128, 4, 2, 256), np.float32)
    k = np.arange(128)[:, None]
    q = np.arange(256)[None, :]
    for h in range(4):
        a = SLOPES[h]
        for dt_ in range(2):
            kb = 128 * dt_ + k
            vis = (kb // 64) <= (q // 64)
            val = -a * np.abs(q - kb) + a * q + a * (127 - k)
            db[:, h, dt_, :] = np.where(vis, val, NEG)
    t["c_dbias"] = db.reshape(128, 4 * 2 * 256)
    mb = np.full((128, 4, 16), NEG, np.float32)
    k16 = np.arange(16)[:, None]
    q16 = np.arange(16)[None, :]
    for h in range(4):
        a = SLOPES[h]
        mb[:16, h, :] = -a * np.abs(q16 - k16) + a * (15 - k16)
    t["c_mbias"] = mb.reshape(128, 64)
    dt = np.zeros((128, 8), np.float32)
    j = np.arange(128)
    for h in range(4):
        a = SLOPES[h]
        dt[:, h] = np.exp(-a * (127 - j))
        dt[:16, 4 + h] = np.exp(-a * (15 - j[:16]))
    t["c_dtab"] = dt
    return t


def build_nc(NT, debug=False, phases=3):
    TT = NT + 1
    LT = TT * 128
    NB = NT // 2
    nc = bass.Bass("TRN2", target_bir_lowering=False)

    def din(name, shape, dt=F32):
        return nc.dram_tensor(name, list(shape), dt, kind="ExternalInput").ap()

    x = din("x", [NT * 128, D])
    meta = din("meta", [NMETA, D])
    w_in = din("w_in", [D, INC])
    w_out = din("w_out", [D, D])
    w_up = din("w_up", [D, 2 * DFF])
    w_down = din("w_down", [DFF, D])
    nw1 = din("nw1", [128, D])
    nw2 = din("nw2", [128, 8])
    fnw = din("fnw", [128, D])
    lamv = din("lamv", [128, 256])
    subw = din("subw", [128, 128])
    gnw = din("gnw", [128, 128])
    gatew = din("gatew", [17, 256])
    convt = din("convt", [128, NCH * 4])
    c_ident = din("c_ident", [128, 128])
    c_tri = din("c_tri", [128, 128])
    c_tri3 = din("c_tri3", [128, 128])
    c_dbias = din("c_dbias", [128, 2048])
    c_mbias = din("c_mbias", [128, 64])
    c_dtab = din("c_dtab", [128, 8])
    out = nc.dram_tensor("out", [NT * 128, D], F32, kind="ExternalOutput").ap()

    okind = "ExternalOutput" if debug else "Internal"
    QT = nc.dram_tensor("QT", [4, 2, 128, LT], BF16, kind=okind).ap()
    KT = nc.dram_tensor("KT", [4, 128, LT], BF16, kind=okind).ap()
    VP = nc.dram_tensor("VP", [4, 128, TT, 129], BF16, kind=okind).ap()
    OG = nc.dram_tensor("OG", [LT, 512], BF16, kind=okind).ap()
    OD = nc.dram_tensor("OD", [LT, 512], BF16, kind=okind).ap()

    with ExitStack() as es:
        S = Sched(nc, es)

        def sb(name, shape, dt, st=es):
            return st.enter_context(nc.sbuf_tensor(name, list(shape), dt))

        def pst(name, shape, dt, st):
            return st.enter_context(nc.psum_tensor(name, list(shape), dt))

        ident = sb("ident", [128, 128], BF16)
        lam_t = sb("lam_t", [128, 8], F32)
        subw_t = sb("subw_t", [128, 128], F32)
        eps_t = sb("eps_t", [128, 1], F32)
        cds = S.dsem("const")

        def ld(eng, dst, src, key, ds):
            return S.op(eng, ("dma_start", dict(out=dst, in_=src)), writes=[key], dsem=ds)

        ld("pool", ident[:], c_ident, "ident", S.dsem("ident"))
        S.group_begin()
        ld("sp", subw_t[:], subw, "subw", cds)
        with ExitStack() as st0:
            lv = sb("lv", [128, 256], F32, st0)
            lj = sb("lj", [128, 128], F32, st0)
            ld("sp", lv[:], lamv, "lv", cds)
            S.group_end()
            S.op("dve", ("memset", dict(ap=eps_t[:], constant=EPS)), writes=["eps"])
            S.op("dve", ("memset", dict(ap=lam_t[:], constant=0.0)), writes=["lam"])
            S.op("dve", ("tensor_tensor", dict(out=lj[:, 0:64], in0=lv[:, 0:64], in1=lv[:, 64:128], op=ALU.mult)),
                 reads=["lv"], writes=["lj"])
            S.op("dve", ("tensor_tensor", dict(out=lj[:, 64:128], in0=lv[:, 128:192], in1=lv[:, 192:256], op=ALU.mult)),
                 reads=["lv"], writes=["lj2"])
            S.op("dve", ("reduce_sum", dict(out=lam_t[:, 0:1], in_=lj[:, 0:64], axis=mybir.AxisListType.X)),
                 reads=["lj"], writes=["lam"])
            S.op("dve", ("reduce_sum", dict(out=lam_t[:, 1:2], in_=lj[:, 64:128], axis=mybir.AxisListType.X)),
                 reads=["lj2"], writes=["lam"])
            S.op("act", ("activation", dict(out=lam_t[:, 2:4], in_=lam_t[:, 0:2], func=AF.Exp)), reads=["lam"], writes=["lam"])
            S.op("dve", ("tensor_tensor", dict(out=lam_t[:, 5:6], in0=lam_t[:, 3:4], in1=lam_t[:, 2:3], op=ALU.subtract)),
                 reads=["lam"], writes=["lam"])
            S.op("dve", ("tensor_scalar", dict(out=lam_t[:, 4:5], in0=lam_t[:, 5:6], scalar1=-LAM_INIT, scalar2=None, op0=ALU.add)),
                 reads=["lam"], writes=["lam"])
            S.op("dve", ("tensor_scalar", dict(out=subw_t[:], in0=subw_t[:], scalar1=1.0 - LAM_INIT, scalar2=None, op0=ALU.mult)),
                 reads=["subw"], writes=["subw"])
            S.barrier()
        nlam = lam_t[:, 4:5]

        def rstd_ops(ss_ap, n, key):
            S.op("act", ("activation", dict(out=ss_ap, in_=ss_ap, func=AF.Ln, bias=eps_t[0:ss_ap.shape[0], 0:1], scale=1.0 / n)),
                 reads=[key, "eps"], writes=[key])
            S.op("act", ("activation", dict(out=ss_ap, in_=ss_ap, func=AF.Exp, scale=-0.5)), reads=[key], writes=[key])

        import os as _os
        _stop = int(_os.environ.get("K_STOP", "0"))

        def ck(n):
            if _stop == n:
                S.dead = True

        with ExitStack() as st:
            st.enter_context(suppress(_Skip))
            win = sb("win", [128, 8, INC], BF16, st)
            nw1_t = sb("nw1_t", [128, D], F32, st)
            gnw_t = sb("gnw_t", [128, 128], F32, st)
            gw_t = sb("gw_t", [32, 256], F32, st)
            tri_t = sb("tri_t", [128, 128], F32, st)
            tri3_t = sb("tri3_t", [128, 128], F32, st)
            dtab = sb("dtab", [128, 8], F32, st)
            xs = [sb(f"xs{i}", [128, D], F32, st) for i in range(2)]
            junk = sb("junk", [128, D], BF16, st)
            ssn = sb("ssn", [128, 2], F32, st)
            xn = sb("xn", [128, D], BF16, st)
            xnTs = [sb(f"xnT{j}", [128, 8, 128], BF16, st) for j in range(3)]
            gjunk = sb("gjunk", [128, 512], BF16, st)
            qst = [sb(f"qst{i}", [128, 4, 2, 128], BF16, st) for i in range(2)]
            kst = [sb(f"kst{i}", [128, 4, 128], BF16, st) for i in range(2)]
            vst = [sb(f"vst{i}", [128, 4, 129], BF16, st) for i in range(2)]
            ogs = [sb(f"ogs{i}", [128, 512], BF16, st) for i in range(2)]
            glrT = sb("glrT", [32, 128], F32, st)
            la_l = [sb(f"la{j}", [128, 256], F32, st) for j in range(2)]
            e1T_l = [sb(f"e1T{j}", [128, 2, 128], F32, st) for j in range(2)]
            e2T_l = [sb(f"e2T{j}", [128, 2, 128], F32, st) for j in range(2)]
            ke_l = [sb(f"ke{j}", [128, 256], F32, st) for j in range(2)]
            kdT_l = [sb(f"kdT{j}", [128, 2, 128], BF16, st) for j in range(2)]
            qdh_l = [sb(f"qdh{j}", [128, 4, 128], BF16, st) for j in range(2)]
            qdc_l = [sb(f"qdc{j}", [128, 2, 2, 128], BF16, st) for j in range(2)]
            kec_l = [sb(f"kec{j}", [128, 2, 256], BF16, st) for j in range(2)]
            gvb_l = [sb(f"gvb{j}", [128, 512], BF16, st) for j in range(2)]
            aT_l = [sb(f"aT{j}", [128, 4, 128], BF16, st) for j in range(2)]
            Sf = sb("Sf", [128, 2, 128], F32, st)
            Sb = sb("Sb", [128, 2, 256], BF16, st)
            gss = sb("gss", [128, 4], F32, st)
            gon = sb("gon", [128, 512], F32, st)
            ger = sb("ger", [128, 512], F32, st)

            p_tr = pst("p_tr", [128, 8, 128], BF16, st)
            p_fm = [pst(f"p_fm{i}", [128, 4, 128], F32, st) for i in range(3)]
            p_tm = [pst(f"p_tm{i}", [128, 512], F32, st) for i in range(2)]
            p_g = [pst(f"p_g{i}", [128, 512], F32, st) for i in range(2)]

            wds = S.dsem("win")
            S.group_begin()
            WG = [(0, 512), (512, 1024), (1024, 1536), (1536, 2048), (2048, 2560), (2560, INC)]
            w_in_v = w_in.rearrange("(kc p) n -> p kc n", p=128)
            for gi, (c0_, c1_) in enumerate(WG):
                ld("pool", win[:, :, c0_:c1_], w_in_v[:, :, c0_:c1_], f"win{gi}", S.dsem(f"win{gi}"))

            def wkey(col0):
                for gi, (c0_, c1_) in enumerate(WG):
                    if c0_ <= col0 < c1_:
                        return [f"win{gi}"]
                raise ValueError(col0)
            ld("sp", nw1_t[:], nw1, "nw1", cds)
            ld("sp", gnw_t[:], gnw, "gnw", cds)
            ld("sp", gw_t[0:17, :], gatew, "gw", cds)
            ld("sp", tri_t[:], c_tri, "tri", cds)
            ld("sp", tri3_t[:], c_tri3, "tri3", cds)
            ld("sp", dtab[:], c_dtab, "dtab", cds)
            S.group_end()
            S.op("dve", ("memset", dict(ap=glrT[:], constant=1.0)), writes=["glrT"])
            for i in range(2):
                S.op("pool", ("memset", dict(ap=qst[i][:], constant=0.0)), writes=[f"qst{i}"])
            for j in range(2):
                S.op("pool", ("memset", dict(ap=qdh_l[j][:], constant=0.0)), writes=[f"qdh{j}"])
                S.op("pool", ("memset", dict(ap=qdc_l[j][:], constant=0.0)), writes=[f"qdc{j}"])
                S.op("pool", ("memset", dict(ap=kec_l[j][:], constant=0.0)), writes=[f"kec{j}"])
            S.op("dve", ("memset", dict(ap=Sf[:], constant=0.0)), writes=["Sf"])
            S.op("dve", ("memset", dict(ap=Sb[:], constant=0.0)), writes=["Sb"])
            S.op("pool", ("memset", dict(ap=xs[0][:], constant=0.0)), writes=["xs0"])

            xds = [S.dsem("xs0"), S.dsem("xs1")]
            sdq = [S.dsem("sq0"), S.dsem("sq1")]
            sdk = [S.dsem("sk0"), S.dsem("sk1")]
            sdv = [S.dsem("sv0"), S.dsem("sv1")]
            sdg = [S.dsem("sg0"), S.dsem("sg1")]

            def load_x(T):
                i = T % 2
                if T == 0:
                    ld("sp", xs[0][0:16, :], meta, "xs0", xds[0])
                else:
                    ld("sp", xs[i][:], x[(T - 1) * 128:T * 128, :], f"xs{i}", xds[i])

            load_x(0)
            def mk_groups(xnT, XT):
                def fm_group(pt, pk, col0, nchunk, m=128):
                    for c in range(nchunk):
                        for kc in range(8):
                            S.op("pe", ("matmul", dict(
                                out=pt[0:m, c, :], lhsT=win[:, kc, col0 + c * 128: col0 + c * 128 + m], rhs=xnT[:, kc, :],
                                start=(kc == 0), stop=(kc == 7))),
                                reads=XT + wkey(col0), writes=[pk], sig=(kc == 7 and c == nchunk - 1))
                def tm_group(pt, pk, col0, n, off=0):
                    for kc in range(8):
                        S.op("pe", ("matmul", dict(out=pt[:, off:off + n], lhsT=xnT[:, kc, :], rhs=win[:, kc, col0:col0 + n],
                                                            start=(kc == 0), stop=(kc == 7))),
                             reads=XT + wkey(col0), writes=[pk], sig=(kc == 7))
                return fm_group, tm_group

            def seg_F1(T, part="ab"):
                i = T % 2
                xk = f"xs{i}"
                xnT = xnTs[T % 3]
                XT = [f"xnTa{T % 3}", f"xnTb{T % 3}"]
                fm_group, tm_group = mk_groups(xnT, XT)
                if "a" in part:
                    if T + 1 < TT:
                        load_x(T + 1)
                    S.op("act", ("activation", dict(out=junk[:], in_=xs[i][:], func=AF.Square, accum_out=ssn[:, 0:1])),
                         reads=[xk], writes=["junk", "ssn"])
                    rstd_ops(ssn[:, 0:1], D, "ssn")
                    S.op("dve", ("scalar_tensor_tensor", dict(out=xn[:], in0=xs[i][:], scalar=ssn[:, 0:1], in1=nw1_t[:],
                                                                     op0=ALU.mult, op1=ALU.mult)),
                         reads=[xk, "ssn", "nw1"], writes=["xn"])
                if "b" not in part:
                    return
                for kc in range(8):
                    S.op("pe", ("transpose", dict(out=p_tr[:, kc, :], in_=xn[:, kc * 128:(kc + 1) * 128], identity=ident[:])),
                         reads=["xn", "ident"], writes=["p_tr"], sig=(kc == 7))
                S.op("act", ("copy", dict(out=xnT[:], in_=p_tr[:])), reads=["p_tr"], writes=XT)

            def seg_F2(T, part="qk"):
                i = T % 2
                xk = f"xs{i}"
                xnT = xnTs[T % 3]
                XT = [f"xnTa{T % 3}", f"xnTb{T % 3}"]
                fm_group, tm_group = mk_groups(xnT, XT)

                if "q" in part:
                    fm_group(p_fm[0], "p_fm0", 0, 4)
                    for m in range(2):
                        S.op("act" if m == 0 else "dve",
                             (("copy", dict(out=qst[i][m * 64:(m + 1) * 64, :, m, :], in_=p_fm[0][m * 64:(m + 1) * 64, :, :]))) if m == 0 else
                             (("tensor_copy", dict(out=qst[i][m * 64:(m + 1) * 64, :, m, :], in_=p_fm[0][m * 64:(m + 1) * 64, :, :]))),
                             reads=["p_fm0"], writes=[f"qst{i}"])
                    S.op("sp", ("dma_start", dict(out=QT[:, :, :, T * 128:(T + 1) * 128].rearrange("h m p t -> p h m t"), in_=qst[i][:])),
                         reads=[f"qst{i}"], writes=[("QT", T)], dsem=sdq[i])
                if "k" not in part:
                    return
                fm_group(p_fm[0], "p_fm0", 512, 4)
                S.op("act", ("copy", dict(out=kst[i][:], in_=p_fm[0][:])), reads=["p_fm0"], writes=[f"kst{i}"])
                S.op("sp", ("dma_start", dict(out=KT[:, :, T * 128:(T + 1) * 128].rearrange("h p t -> p h t"), in_=kst[i][:])),
                     reads=[f"kst{i}"], writes=[("KT", T)], dsem=sdk[i])

            def seg_F3(T):
                i = T % 2
                xk = f"xs{i}"
                xnT = xnTs[T % 3]
                XT = [f"xnTa{T % 3}", f"xnTb{T % 3}"]
                fm_group, tm_group = mk_groups(xnT, XT)
                pv0 = p_fm[0][:].rearrange("p a t -> p (a t)")
                tm_group(pv0, "p_fm0", 1024, 512)
                dcol = 4 if T == 0 else 0
                for h in range(4):
                    S.op("dve", ("tensor_scalar", dict(out=vst[i][:, h, 0:128], in0=pv0[:, h * 128:(h + 1) * 128],
                                                              scalar1=dtab[:, dcol + h:dcol + h + 1], scalar2=None, op0=ALU.mult)),
                         reads=["p_fm0", "dtab"], writes=[f"vst{i}"])
                S.op("dve", ("tensor_copy", dict(out=vst[i][:, :, 128], in_=dtab[:, dcol:dcol + 4])),
                     reads=["dtab"], writes=[f"vst{i}"])
                S.op("sp", ("dma_start", dict(out=VP[:, :, T, :].rearrange("h p c -> p h c"), in_=vst[i][:])),
                     reads=[f"vst{i}"], writes=[("VP", T)], dsem=sdv[i])


            def seg_G1(T, part="all"):
                i = T % 2
                xk = f"xs{i}"
                xnT = xnTs[T % 3]
                XT = [f"xnTa{T % 3}", f"xnTb{T % 3}"]
                fm_group, tm_group = mk_groups(xnT, XT)
                pd = p_fm[1][:].rearrange("p a t -> p (a t)")
                la = la_l[i]
                e1T = e1T_l[i]
                e2T = e2T_l[i]
                ke = ke_l[i]
                kdT = kdT_l[i]
                qdh = qdh_l[i]
                qdc = qdc_l[i]
                kec = kec_l[i]
                gvb = gvb_l[i]
                aT = aT_l[i]
                if part != "rest":
                    for kc in range(8):
                        S.op("pe", ("matmul", dict(out=p_g[0][0:16, 0:128], lhsT=win[:, kc, 3072:3088], rhs=xnT[:, kc, :],
                                                            start=(kc == 0), stop=(kc == 7))),
                             reads=XT + wkey(3072), writes=["p_g0"], sig=(kc == 7))
                    S.op("act", ("copy", dict(out=glrT[0:16, :], in_=p_g[0][0:16, 0:128])), reads=["p_g0"], writes=["glrT"])
                if part == "glr":
                    return
                fm_group(p_fm[2], "p_fm2", 1536, 4)
                S.op("pe", ("matmul", dict(out=p_g[1][:, 0:256], lhsT=glrT[0:17, :], rhs=gw_t[0:17, :], start=True, stop=True)),
                     reads=["glrT", "gw"], writes=["p_g1"])
                S.op("act", ("activation", dict(out=la[:], in_=p_g[1][:, 0:256], func=AF.Exp, scale=-1.0)),
                     reads=["p_g1"], writes=[f"la{i}"])
                S.op("act", ("activation", dict(out=la[:], in_=la[:], func=AF.Ln, bias=1.0)), reads=[f"la{i}"], writes=[f"la{i}"])

            def seg_G2(T, part="ab"):
                i = T % 2
                xk = f"xs{i}"
                xnT = xnTs[T % 3]
                XT = [f"xnTa{T % 3}", f"xnTb{T % 3}"]
                fm_group, tm_group = mk_groups(xnT, XT)
                pd = p_fm[1][:].rearrange("p a t -> p (a t)")
                la = la_l[i]
                e1T = e1T_l[i]
                e2T = e2T_l[i]
                ke = ke_l[i]
                kdT = kdT_l[i]
                qdh = qdh_l[i]
                qdc = qdc_l[i]
                kec = kec_l[i]
                gvb = gvb_l[i]
                aT = aT_l[i]
                if "a" in part:
                    tm_group(p_tm[0], "p_tm0", 2048, 512)
                    S.op("act", ("copy", dict(out=gvb[:], in_=p_tm[0][:])), reads=["p_tm0"], writes=[f"gvb{i}"])
                    kr = 16 if T == 0 else 128
                    for hp in range(2):
                        S.op("pe", ("matmul", dict(out=p_g[0][:, hp * 128:(hp + 1) * 128], lhsT=la[0:kr, hp * 128:(hp + 1) * 128],
                                                            rhs=tri_t[0:kr, :], start=True, stop=True)),
                             reads=[f"la{i}", "tri"], writes=["p_g0"], sig=(hp == 1))
                    S.op("pe", ("matmul", dict(out=p_g[1][:, 256:512], lhsT=tri3_t[0:kr, :], rhs=la[0:kr, :], start=True, stop=True)),
                         reads=[f"la{i}", "tri3"], writes=["p_g1"])
                    p_g0v = p_g[0][:, 0:256].rearrange("p (a t) -> p a t", a=2)
                    S.op("act", ("activation", dict(out=e1T[:], in_=p_g0v, func=AF.Exp, scale=-1.0 / 16)), reads=["p_g0"], writes=[f"e1T{i}"])
                    S.op("act", ("activation", dict(out=e2T[:], in_=p_g0v, func=AF.Exp, scale=1.0 / 16)), reads=["p_g0"], writes=[f"e2T{i}"])
                    S.op("act", ("activation", dict(out=ke[:], in_=p_g[1][:, 256:512], func=AF.Exp, scale=-1.0 / 16)),
                         reads=["p_g1"], writes=[f"ke{i}"])
                    tm_group(p_tm[0], "p_tm0", 1792, 256)
                    S.op("dve", ("tensor_tensor", dict(out=kec[:, 0, :], in0=p_tm[0][:, 0:256], in1=ke[:], op=ALU.mult)),
                         reads=["p_tm0", f"ke{i}"], writes=[f"kec{i}"])
                    S.op("dve", ("tensor_tensor", dict(out=kdT[:], in0=p_fm[2][:, 2:4, :], in1=e2T[:], op=ALU.mult)),
                         reads=["p_fm2", f"e2T{i}"], writes=[f"kdT{i}"])
                    for h in range(4):
                        hp, hl = h // 2, h % 2
                        S.op("dve", ("scalar_tensor_tensor", dict(
                            out=qdh[hl * 64:(hl + 1) * 64, h, :], in0=p_fm[2][hl * 64:(hl + 1) * 64, hp, :], scalar=0.125,
                            in1=e1T[hl * 64:(hl + 1) * 64, hp, :], op0=ALU.mult, op1=ALU.mult)),
                            reads=["p_fm2", f"e1T{i}"], writes=[f"qdh{i}"])
                    S.op("dve", ("scalar_tensor_tensor", dict(
                        out=qdc[:, 0, :, :], in0=p_fm[2][:, 0:2, :], scalar=0.125, in1=e1T[:], op0=ALU.mult, op1=ALU.mult)),
                        reads=["p_fm2", f"e1T{i}"], writes=[f"qdc{i}"])
                if "b" not in part:
                    return
                p_g1v = p_g[1][:].rearrange("p (a t) -> p a t", a=4)
                for h in range(4):
                    S.op("pe", ("matmul", dict(out=p_g1v[:, h, :], lhsT=kdT[:, h // 2, :], rhs=qdh[:, h, :], start=True, stop=True)),
                         reads=[f"kdT{i}", f"qdh{i}"], writes=["p_g1"], sig=(h == 3))
                for h in range(4):
                    S.op("dve", ("tensor_tensor", dict(out=aT[:, h, :], in0=p_g1v[:, h, :], in1=tri_t[:], op=ALU.mult)),
                         reads=["p_g1", "tri"], writes=[f"aT{i}"])

            def seg_G3(T):
                i = T % 2
                xk = f"xs{i}"
                xnT = xnTs[T % 3]
                XT = [f"xnTa{T % 3}", f"xnTb{T % 3}"]
                fm_group, tm_group = mk_groups(xnT, XT)
                pd = p_fm[1][:].rearrange("p a t -> p (a t)")
                la = la_l[i]
                e1T = e1T_l[i]
                e2T = e2T_l[i]
                ke = ke_l[i]
                kdT = kdT_l[i]
                qdh = qdh_l[i]
                qdc = qdc_l[i]
                kec = kec_l[i]
                gvb = gvb_l[i]
                aT = aT_l[i]
                nch = 1
                for hp in range(2):
                    for hl in range(2):
                        h = 2 * hp + hl
                        S.op("pe", ("matmul", dict(out=p_tm[1][:, h * 128:(h + 1) * 128], lhsT=aT[:, h, :], rhs=gvb[:, h * 128:(h + 1) * 128],
                                                          start=(h == 0), stop=(T == 0), skip_group_check=True)),
                             reads=[f"aT{i}", f"gvb{i}"], writes=["p_tm1"], sig=False)
                for cc in range(nch):
                    if T > 0:
                        for hp in range(2):
                            S.op("pe", ("matmul", dict(out=p_tm[1][:, hp * 256:(hp + 1) * 256], lhsT=qdc[:, cc, hp, :], rhs=Sb[:, hp, :],
                                                                       start=False, stop=True, skip_group_check=True)),
                                 reads=[f"qdc{i}", "Sb"], writes=["p_tm1"], sig=(hp == 1))
                    for hp in range(2):
                        S.op("pe", ("matmul", dict(out=pd[:, hp * 256:(hp + 1) * 256], lhsT=kec[:, cc, hp * 128:(hp + 1) * 128],
                                                                   rhs=gvb[:, hp * 256:(hp + 1) * 256], start=True, stop=True)),
                             reads=[f"kec{i}", f"gvb{i}"], writes=["p_fm1"], sig=(hp == 1))
                    dc = 15 if T == 0 else 127
                    for hp in range(2):
                        for hl in range(2):
                            S.op("dve", ("scalar_tensor_tensor", dict(
                                out=Sb[hl * 64:(hl + 1) * 64, hp, hl * 128:(hl + 1) * 128], in0=Sf[hl * 64:(hl + 1) * 64, hp, :],
                                scalar=e1T[hl * 64:(hl + 1) * 64, hp, dc:dc + 1],
                                in1=pd[hl * 64:(hl + 1) * 64, hp * 256 + hl * 128: hp * 256 + (hl + 1) * 128],
                                op0=ALU.mult, op1=ALU.add)),
                                reads=["Sf", f"e1T{i}", "p_fm1"], writes=["Sb"])
                            S.op("dve", ("scalar_tensor_tensor", dict(
                                out=Sf[hl * 64:(hl + 1) * 64, hp, :], in0=Sf[hl * 64:(hl + 1) * 64, hp, :],
                                scalar=e1T[hl * 64:(hl + 1) * 64, hp, dc:dc + 1],
                                in1=pd[hl * 64:(hl + 1) * 64, hp * 256 + hl * 128: hp * 256 + (hl + 1) * 128],
                                op0=ALU.mult, op1=ALU.add)),
                                reads=["Sf", f"e1T{i}", "p_fm1"], writes=["Sf"])

            def seg_G4(T):
                i = T % 2
                xk = f"xs{i}"
                xnT = xnTs[T % 3]
                XT = [f"xnTa{T % 3}", f"xnTb{T % 3}"]
                fm_group, tm_group = mk_groups(xnT, XT)
                pd = p_fm[1][:].rearrange("p a t -> p (a t)")
                la = la_l[i]
                e1T = e1T_l[i]
                e2T = e2T_l[i]
                ke = ke_l[i]
                kdT = kdT_l[i]
                qdh = qdh_l[i]
                qdc = qdc_l[i]
                kec = kec_l[i]
                gvb = gvb_l[i]
                aT = aT_l[i]
                for h in range(4):
                    S.op("act", ("activation", dict(out=gjunk[:, h * 128:(h + 1) * 128], in_=p_tm[1][:, h * 128:(h + 1) * 128],
                                                           func=AF.Square, accum_out=gss[:, h:h + 1])),
                         reads=["p_tm1"], writes=["gjunk", "gss"])
                rstd_ops(gss[:, 0:4], 128, "gss")
                for h in range(4):
                    S.op("dve", ("scalar_tensor_tensor", dict(out=gon[:, h * 128:(h + 1) * 128], in0=p_tm[1][:, h * 128:(h + 1) * 128],
                                                                     scalar=gss[:, h:h + 1], in1=gnw_t[:], op0=ALU.mult, op1=ALU.mult)),
                         reads=["p_tm1", "gss", "gnw"], writes=["gon"])
                tm_group(pd, "p_fm1", 2560, 512)
                S.op("act", ("activation", dict(out=ger[:], in_=pd, func=AF.Exp, scale=-1.0)), reads=["p_fm1"], writes=["ger"])
                S.op("act", ("activation", dict(out=ger[:], in_=ger[:], func=AF.Ln, bias=1.0)), reads=["ger"], writes=["ger"])
                S.op("act", ("activation", dict(out=ger[:], in_=ger[:], func=AF.Exp, scale=-1.0)), reads=["ger"], writes=["ger"])
                S.op("dve", ("tensor_tensor", dict(out=ger[:], in0=ger[:], in1=pd, op=ALU.mult)),
                     reads=["ger", "p_fm1"], writes=["ger"])
                S.op("pool", ("tensor_tensor", dict(out=ogs[i][:], in0=gon[:], in1=ger[:], op=ALU.mult)),
                     reads=["gon", "ger"], writes=[f"ogs{i}"])
                S.op("sp", ("dma_start", dict(out=OG[T * 128:(T + 1) * 128, :], in_=ogs[i][:])),
                     reads=[f"ogs{i}"], writes=[("OG", T)], dsem=sdg[i])

            seg_F1(0)
            seg_F2(0)
            seg_F3(0)
            seg_G1(0)
            seg_G2(0)
            if TT > 1:
                seg_F1(1)
                seg_F2(1)
                seg_F3(1)
                seg_G1(1, "glr")
            for T in range(TT):
                if T + 2 < TT:
                    seg_F1(T + 2, "a")
                if T + 1 < TT:
                    seg_G1(T + 1, "rest")
                seg_G3(T)
                if T + 2 < TT:
                    seg_F1(T + 2, "b")
                if T + 1 < TT:
                    seg_G2(T + 1, "a")
                if T + 2 < TT:
                    seg_F2(T + 2, "q")
                if T + 1 < TT:
                    seg_G2(T + 1, "b")
                if T + 2 < TT:
                    seg_F2(T + 2, "k")
                seg_G4(T)
                if T + 2 < TT:
                    seg_F3(T + 2)
                    seg_G1(T + 2, "glr")
            S.barrier()

        wout = sb("wout", [128, 8, D], BF16)
        wup = sb("wup", [128, 8, 2 * DFF], BF16)
        nw2c = sb("nw2c", [128, 8], F32)
        if phases >= 3:
            w2s = S.dsem("w2")
            S.group_begin()
            ld("sp", nw2c[:], nw2, "nw2c", S.dsem("nw2c"))
            for kc in range(8):
                ld("pool", wout[:, kc, :], w_out[kc * 128:(kc + 1) * 128, :], f"wout{kc}", w2s)
            for kc in range(8):
                for hh in range(2):
                    ld("pool", wup[:, kc, hh * DFF:(hh + 1) * DFF], w_up[kc * 128:(kc + 1) * 128, hh * DFF:(hh + 1) * DFF], f"wup{kc}_{hh}", w2s)
            S.group_end()

        with ExitStack() as st:
            st.enter_context(suppress(_Skip))
            if phases < 2:
                raise _Skip()
            KTh_l = [sb(f"KTh{j}", [128, LT], BF16, st) for j in range(2)]
            VPh_l = [sb(f"VPh{j}", [128, TT, 129], BF16, st) for j in range(2)]
            dbias = sb("dbias", [128, 4, 2, 256], F32, st)
            mbias = sb("mbias", [128, 4, 16], F32, st)
            qb = [sb(f"qb{i}", [128, 2, 256], BF16, st) for i in range(2)]
            NPT = 4
            pt = [sb(f"pt{i}", [128, 512], BF16, st) for i in range(NPT)]
            sbias = [sb(f"sbias{i}", [128, 512], F32, st) for i in range(2)]
            otmp = sb("otmp", [128, 128], F32, st)
            ofin = sb("ofin", [128, 2, 128], F32, st)
            zr = sb("zr", [128, 2, 4], F32, st)
            oss = sb("oss", [128, 2], F32, st)
            ods = [sb(f"ods{i}", [128, 128], BF16, st) for i in range(2)]
            p_s = [pst(f"p_s{i}", [128, 512], F32, st) for i in range(4)]
            p_o_full = [pst(f"p_o{i}", [128, 512], F32, st) for i in range(4)]
            p_o = [t[:, 0:258].rearrange("p (a c) -> p a c", a=2) for t in p_o_full]

            S.group_begin()
            ld("sp", dbias[:].rearrange("p a b c -> p (a b c)"), c_dbias, "dbias", cds)
            ld("sp", mbias[:].rearrange("p a b -> p (a b)"), c_mbias, "mbias", cds)
            S.group_end()
            kds = [S.dsem("kk0"), S.dsem("kk1")]
            vds = [S.dsem("vv0"), S.dsem("vv1")]

            def load_kv(hh):
                ld("sp", KTh_l[hh % 2][:], KT[hh], f"KTh{hh % 2}", kds[hh % 2])
                ld("sp", VPh_l[hh % 2][:], VP[hh], f"VPh{hh % 2}", vds[hh % 2])

            load_kv(0)
            qds = [S.dsem("qb0"), S.dsem("qb1")]
            ods_s = [S.dsem("od0"), S.dsem("od1")]
            uctr = [0, 0, 0, 0]

            LA = 3
            for h in range(4):
                a = SLOPES[h]
                if h == 1 and phases >= 3:
                    for kc in range(8):
                        for hh in range(2):
                            S.op("act", ("activation", dict(out=wup[:, kc, hh * DFF:(hh + 1) * DFF], in_=wup[:, kc, hh * DFF:(hh + 1) * DFF],
                                                           func=AF.Copy, scale=nw2c[:, kc:kc + 1])),
                                 reads=["nw2c"], writes=[f"wup{kc}_{hh}"])
                KTh, VPh = KTh_l[h % 2], VPh_l[h % 2]
                KTk, VPk = f"KTh{h % 2}", f"VPh{h % 2}"
                if h + 1 < 4:
                    load_kv(h + 1)
                blks = []
                for I in [-1] + list(range(NB)):
                    bi = uctr[1] % 2
                    uctr[1] += 1
                    units = []
                    if I < 0:
                        units.append((0, 16, "mdiag", 0.0))
                    else:
                        cm = -a * (256 * I + 1)
                        if cm >= SKIP_C:
                            units.append((0, 16, "past", cm))
                        for Tk in range(1, 2 * I + 1):
                            c = -a * (256 * I - 128 * (Tk - 1) - 127)
                            if c < SKIP_C:
                                continue
                            units.append((Tk, 128, "past", c))
                        units.append((2 * I + 1, 128, "diag0", 0.0))
                        units.append((2 * I + 2, 128, "diag1", 0.0))
                    blks.append(dict(I=I, bi=bi, pob=bi * 2, nq=(16 if I < 0 else 256), nqt=(1 if I < 0 else 2),
                                     c0=(0 if I < 0 else 128 + 256 * I), units=units))
                flat = []
                for bidx, bl in enumerate(blks):
                    for ui, un in enumerate(bl["units"]):
                        flat.append((bidx, ui, un))

                def load_q(bl):
                    bi, c0, nq = bl["bi"], bl["c0"], bl["nq"]
                    S.op("sp", ("dma_start", dict(out=qb[bi][:, :, 0:nq], in_=QT[h, :, :, c0:c0 + nq].rearrange("m p t -> p m t"))),
                         writes=[f"qb{bi}"], dsem=qds[bi])

                def st_qk(fi):
                    bidx, ui, (Tk, nk, kind, c) = flat[fi]
                    bl = blks[bidx]
                    if ui == 0 and bidx + 1 < len(blks):
                        load_q(blks[bidx + 1])
                    u = uctr[0] + fi
                    ps_, psk = p_s[u % 4], f"p_s{u % 4}"
                    bi, nq = bl["bi"], bl["nq"]
                    if nq == 256:
                        S.op("pe", ("matmul", dict(out=ps_[0:nk, 0:512], lhsT=KTh[:, Tk * 128:Tk * 128 + nk],
                                                  rhs=qb[bi][:].rearrange("p m t -> p (m t)"), start=True, stop=True)),
                             reads=[KTk, f"qb{bi}"], writes=[psk])
                    else:
                        for m in range(2):
                            S.op("pe", ("matmul", dict(out=ps_[0:nk, m * 256:m * 256 + nq], lhsT=KTh[:, Tk * 128:Tk * 128 + nk], rhs=qb[bi][:, m, 0:nq],
                                                      start=True, stop=True)),
                                 reads=[KTk, f"qb{bi}"], writes=[psk], sig=(m == 1))

                def st_exp_pv(fi):
                    bidx, ui, (Tk, nk, kind, c) = flat[fi]
                    bl = blks[bidx]
                    I, bi, pob, nq, nqt = bl["I"], bl["bi"], bl["pob"], bl["nq"], bl["nqt"]
                    u = uctr[0] + fi
                    ps_, psk = p_s[u % 4], f"p_s{u % 4}"
                    ptt, ptk = pt[u % NPT], f"pt{u % NPT}"
                    if kind == "past":
                        S.op("act", ("activation", dict(out=ptt[0:nk, :], in_=ps_[0:nk, :], func=AF.Exp, bias=float(c), scale=0.125)),
                             reads=[psk], writes=[ptk])
                    elif kind == "mdiag":
                        sbt, sbk = sbias[u % 2], f"sbias{u % 2}"
                        for m in range(2):
                            S.op("dve", ("scalar_tensor_tensor", dict(
                                out=sbt[0:16, m * 256:m * 256 + 16], in0=ps_[0:16, m * 256:m * 256 + 16], scalar=0.125, in1=mbias[0:16, h, :],
                                op0=ALU.mult, op1=ALU.add)), reads=[psk, "mbias"], writes=[sbk])
                        for m in range(2):
                            S.op("act", ("activation", dict(out=ptt[0:16, m * 256:m * 256 + 16], in_=sbt[0:16, m * 256:m * 256 + 16], func=AF.Exp)),
                                 reads=[sbk], writes=[ptk])
                    else:
                        dt_ = 0 if kind == "diag0" else 1
                        sbt, sbk = sbias[u % 2], f"sbias{u % 2}"
                        for m in range(2):
                            S.op("dve", ("scalar_tensor_tensor", dict(
                                out=sbt[:, m * 256:(m + 1) * 256], in0=ps_[:, m * 256:(m + 1) * 256], scalar=0.125, in1=dbias[:, h, dt_, :],
                                op0=ALU.mult, op1=ALU.add)), reads=[psk, "dbias"], writes=[sbk])
                        S.op("act", ("activation", dict(out=ptt[:], in_=sbt[:], func=AF.Exp)), reads=[sbk], writes=[ptk])
                    first = (ui == 0)
                    last = (ui == len(bl["units"]) - 1)
                    mq = 16 if I < 0 else 128
                    for qt in range(nqt):
                        if kind == "diag1" and qt == 0:
                            continue
                        lastq = last or (kind == "diag0" and qt == 0)
                        for m in range(2):
                            S.op("pe", ("matmul", dict(
                                out=p_o[pob + qt][0:mq, m, :], lhsT=ptt[0:nk, m * 256 + qt * 128: m * 256 + qt * 128 + mq], rhs=VPh[0:nk, Tk, :],
                                start=(first and m == 0), stop=lastq, skip_group_check=True)),
                                reads=[ptk, VPk], writes=[f"p_o{pob + qt}"], sig=(m == 1))
                    if not last:
                        return None

                    def epilogue(I=I, pob=pob, nqt=nqt, mq=mq):
                        for qt in range(nqt):
                            po = p_o[pob + qt]
                            pok = f"p_o{pob + qt}"
                            S.op("dve", ("reciprocal", dict(out=zr[0:mq, qt, 0:2], in_=po[0:mq, :, 128])), reads=[pok], writes=["zr"])
                            S.op("dve", ("tensor_tensor", dict(out=zr[0:mq, qt, 2:3], in0=zr[0:mq, qt, 1:2], in1=nlam[0:mq, :], op=ALU.mult)),
                                 reads=["zr"], writes=["zr"])
                            S.op("dve", ("tensor_scalar", dict(out=otmp[0:mq, :], in0=po[0:mq, 1, 0:128], scalar1=zr[0:mq, qt, 2:3], scalar2=None, op0=ALU.mult)),
                                 reads=[pok, "zr"], writes=["otmp"])
                            S.op("dve", ("scalar_tensor_tensor", dict(out=ofin[0:mq, qt, :], in0=po[0:mq, 0, 0:128], scalar=zr[0:mq, qt, 0:1], in1=otmp[0:mq, :],
                                                                     op0=ALU.mult, op1=ALU.add)),
                                 reads=[pok, "zr", "otmp"], writes=[f"ofin{qt}"])
                            S.op("dve", ("scalar_tensor_tensor", dict(out=otmp[0:mq, :], in0=ofin[0:mq, qt, :], scalar=1.0, in1=ofin[0:mq, qt, :],
                                                                     op0=ALU.mult, op1=ALU.mult, accum_out=oss[0:mq, qt:qt + 1])),
                                 reads=[f"ofin{qt}"], writes=["otmp", "oss"])
                        rstd_ops(oss[0:mq, 0:nqt], 128, "oss")
                        for qt in range(nqt):
                            oi = uctr[2] % 2
                            uctr[2] += 1
                            S.op("dve", ("scalar_tensor_tensor", dict(out=ods[oi][0:mq, :], in0=ofin[0:mq, qt, :], scalar=oss[0:mq, qt:qt + 1], in1=subw_t[0:mq, :],
                                                                     op0=ALU.mult, op1=ALU.mult)),
                                 reads=[f"ofin{qt}", "oss", "subw"], writes=[f"ods{oi}"])
                            r0 = 0 if I < 0 else 128 + 256 * I + 128 * qt
                            S.op("sp", ("dma_start", dict(out=OD[r0:r0 + mq, h * 128:(h + 1) * 128], in_=ods[oi][0:mq, :])),
                                 reads=[f"ods{oi}"], writes=[("OD", h, r0)], dsem=ods_s[oi])
                    return epilogue

                load_q(blks[0])
                n = len(flat)
                pending = []
                for idx in range(n + LA):
                    if idx < n:
                        st_qk(idx)
                    if idx - LA >= 0:
                        fb, fu, _ = flat[idx - LA]
                        if fu == 0:
                            keep = []
                            for pe_ in pending:
                                if pe_[2] == blks[fb]["pob"]:
                                    pe_[1]()
                                else:
                                    keep.append(pe_)
                            pending[:] = keep
                        ep = st_exp_pv(idx - LA)
                        if ep is not None:
                            pending.append((idx + 9, ep, blks[fb]["pob"]))
                    while pending and pending[0][0] <= idx:
                        pending.pop(0)[1]()
                for pe_ in pending:
                    pe_[1]()
                uctr[0] += n
            S.barrier()

        with ExitStack() as st:
            st.enter_context(suppress(_Skip))
            if phases < 3:
                raise _Skip()
            wdn = sb("wdn", [128, 22, D], BF16, st)
            fnw_t = sb("fnw_t", [128, D], F32, st)
            cv = sb("cv", [128, NCH, 4], F32, st)
            mix = sb("mix", [128, D], BF16, st)
            mixT = sb("mixT", [128, 8, 128], BF16, st)
            h1 = [[sb(f"h1_{p}_{t}", [128, D], F32, st) for t in range(2)] for p in range(2)]
            xn2 = sb("xn2", [128, D], BF16, st)
            x2Ts = [sb(f"x2T{j}", [128, 8, 258], BF16, st) for j in range(2)]
            yv = [sb(f"yv{i}", [128, 256], F32, st) for i in range(3)]
            yg = [sb(f"yg{i}", [128, 256], F32, st) for i in range(3)]
            actT = sb("actT", [128, 22, 256], BF16, st)
            ss2 = sb("ss2", [128, 2], F32, st)
            p_t = pst("p_t", [128, 8, 128], BF16, st)
            p_op = pst("p_op", [128, 512], F32, st)
            p_u = [pst(f"p_u{i}", [128, 512], F32, st) for i in range(6)]
            p_d = [p_u[4], p_u[5]]

            w3s = S.dsem("w3")
            S.group_begin()
            ld("sp", fnw_t[:], fnw, "fnw", cds)
            ld("sp", cv[:].rearrange("p a b -> p (a b)"), convt, "cv", cds)
            for c in range(22):
                ld("pool", wdn[:, c, :], w_down[c * 128:(c + 1) * 128, :], f"wdn{c}", w3s)
            S.group_end()
            WOUT = [f"wout{kc}" for kc in range(8)]
            WUP = [f"wup{kc}_{hh}" for kc in range(8) for hh in range(2)]
            WDN = [f"wdn{c}" for c in range(22)]
            S.op("dve", ("memset", dict(ap=x2Ts[0][:, :, 0:2], constant=0.0)), writes=["x2T0", "x2Tb0"])
            S.op("pool", ("memset", dict(ap=h1[0][0][:], constant=0.0)), writes=["h1_0_0_0", "h1_0_0_1"])
            S.op("pool", ("memset", dict(ap=mix[:], constant=0.0)), writes=["mix", "mixb"])

            mds = S.dsem("mix")
            hds = [[S.dsem(f"h1{p}{t}") for t in range(2)] for p in range(2)]
            ods2 = [[S.dsem(f"o{p}{t}") for t in range(2)] for p in range(2)]
            blocks2 = [[0]] + [[2 * B + 1, 2 * B + 2] for B in range(NB)]

            def front_hist(bidx):
                par = bidx % 2
                if bidx > 0:
                    pn = 16 if bidx == 1 else 256
                    S.op("pool", ("tensor_copy", dict(out=x2Ts[par][:, :, 0:2], in_=x2Ts[1 - par][:, :, pn:pn + 2])),
                         reads=[f"x2T{1 - par}", f"x2Tb{1 - par}"], writes=[f"x2T{par}", f"x2Tb{par}"])

            def front_tile(bidx, ti, part):
                tiles = blocks2[bidx]
                par = bidx % 2
                x2T = x2Ts[par]
                T = tiles[ti]
                nt = 16 if T == 0 else 128
                hk = [f"h1_{par}_{ti}_0", f"h1_{par}_{ti}_1"]
                hb = h1[par][ti]
                if part == "A0":
                    S.group_begin()
                    S.op("sp", ("dma_start", dict(out=mix[0:nt, 0:512], in_=OD[T * 128:T * 128 + nt, :])), writes=["mix"], dsem=mds)
                    S.op("sp", ("dma_start", dict(out=mix[0:nt, 512:1024], in_=OG[T * 128:T * 128 + nt, :])), writes=["mixb"], dsem=mds)
                    S.group_end()
                    if T == 0:
                        S.op("sp", ("dma_start", dict(out=hb[0:16, :], in_=meta)), writes=hk, dsem=hds[par][ti])
                    else:
                        S.op("sp", ("dma_start", dict(out=hb[:], in_=x[(T - 1) * 128:T * 128, :])), writes=hk, dsem=hds[par][ti])
                elif part == "A":
                    for kc in range(8):
                        S.op("pe", ("transpose", dict(out=p_t[:, kc, :], in_=mix[:, kc * 128:(kc + 1) * 128], identity=ident[:])),
                             reads=["mix", "mixb", "ident"], writes=["p_t"], sig=(kc == 7))
                    S.op("act", ("copy", dict(out=mixT[:], in_=p_t[:])), reads=["p_t"], writes=["mixTa", "mixTb"])
                elif part in ("B0", "B1"):
                    half = 0 if part == "B0" else 1
                    for kc in range(8):
                        S.op("pe", ("matmul", dict(out=p_op[:], lhsT=mixT[:, kc, :], rhs=wout[:, kc, half * 512:(half + 1) * 512],
                                                  start=(kc == 0), stop=(kc == 7))),
                             reads=["mixTa", "mixTb"] + WOUT, writes=["p_op"], sig=(kc == 7))
                    S.op("dve", ("tensor_tensor", dict(out=hb[:, half * 512:(half + 1) * 512], in0=p_op[:],
                                                      in1=hb[:, half * 512:(half + 1) * 512], op=ALU.add)),
                         reads=["p_op", hk[half]], writes=[hk[half]])
                    if half == 1:
                        S.op("act", ("activation", dict(out=xn2[:], in_=hb[:], func=AF.Square, accum_out=ss2[:, 0:1])),
                             reads=hk, writes=["xn2", "ss2"])
                        rstd_ops(ss2[:, 0:1], D, "ss2")
                        S.op("dve", ("tensor_scalar", dict(out=xn2[:], in0=hb[:], scalar1=ss2[:, 0:1], scalar2=None, op0=ALU.mult)),
                             reads=hk + ["ss2"], writes=["xn2"])
                else:
                    for kc in range(8):
                        S.op("pe", ("transpose", dict(out=p_t[:, kc, :], in_=xn2[:, kc * 128:(kc + 1) * 128], identity=ident[:])),
                             reads=["xn2", "ident"], writes=["p_t"], sig=(kc == 7))
                    S.op("dve", ("tensor_copy", dict(out=x2T[:, :, 2 + ti * 128:2 + ti * 128 + nt], in_=p_t[:, :, 0:nt])),
                         reads=["p_t"], writes=[f"x2T{par}", f"x2Tb{par}"])

            def ffn(bidx):
                ntok = 16 if bidx == 0 else 256
                x2T = x2Ts[bidx % 2]
                XB = [f"x2T{bidx % 2}", f"x2Tb{bidx % 2}"]

                def stage1(c):
                    pr = c % 3
                    pb = (c % 2) if c < 4 else (c - 2) % 3
                    pv_, pg_ = p_u[2 * pb], p_u[2 * pb + 1]
                    pvk, pgk = f"p_u{2 * pb}", f"p_u{2 * pb + 1}"
                    for (pp, pk, ch) in ((pv_, pvk, c), (pg_, pgk, c + 22)):
                        for kc in range(8):
                            S.op("pe", ("matmul", dict(out=pp[:, 0:ntok + 2], lhsT=wup[:, kc, ch * 128:(ch + 1) * 128], rhs=x2T[:, kc, 0:ntok + 2],
                                                      start=(kc == 0), stop=(kc == 7))),
                                 reads=XB + WUP, writes=[pk], sig=(kc == 7))
                    for (pp, pk, ch, y_, yk) in ((pv_, pvk, c, yv[pr], f"yv{pr}"), (pg_, pgk, c + 22, yg[pr], f"yg{pr}")):
                        S.op("act", ("activation", dict(out=y_[:, 0:ntok], in_=pp[:, 2:ntok + 2], func=AF.Identity,
                                                       bias=cv[:, ch, 3:4], scale=cv[:, ch, 2:3])),
                             reads=[pk, "cv"], writes=[yk])
                        S.op("dve", ("scalar_tensor_tensor", dict(out=y_[:, 0:ntok], in0=pp[:, 1:ntok + 1], scalar=cv[:, ch, 1:2], in1=y_[:, 0:ntok],
                                                                 op0=ALU.mult, op1=ALU.add)),
                             reads=[pk, "cv", yk], writes=[yk])
                        S.op("dve", ("scalar_tensor_tensor", dict(out=y_[:, 0:ntok], in0=pp[:, 0:ntok], scalar=cv[:, ch, 0:1], in1=y_[:, 0:ntok],
                                                                 op0=ALU.mult, op1=ALU.add)),
                             reads=[pk, "cv", yk], writes=[yk])

                def stage2(c):
                    pr = c % 3
                    S.op("act", ("activation", dict(out=yg[pr][:, 0:ntok], in_=yg[pr][:, 0:ntok], func=AF.Silu)),
                         reads=[f"yg{pr}"], writes=[f"yg{pr}"])
                    S.op("pool", ("tensor_tensor", dict(out=actT[:, c, 0:ntok], in0=yg[pr][:, 0:ntok], in1=yv[pr][:, 0:ntok], op=ALU.mult)),
                         reads=[f"yg{pr}", f"yv{pr}"], writes=[f"actT{c}"])

                nxt = bidx + 1 if bidx + 1 < len(blocks2) else None
                for c in range(22):
                    if bidx > 0 and c in (0, 2):
                        ti_p = 0 if c == 0 else 1
                        if ti_p < len(blocks2[bidx - 1]):
                            down(bidx - 1, ti_p, "fin")
                    stage1(c)
                    if c >= 2:
                        stage2(c - 2)
                    if nxt is not None:
                        for ti_ in range(len(blocks2[nxt])):
                            c0 = 3 + 9 * ti_
                            if c == c0 - 2:
                                front_tile(nxt, ti_, "A0")
                            elif c == c0:
                                if ti_ == 0:
                                    front_hist(nxt)
                                front_tile(nxt, ti_, "A")
                            elif c == c0 + 2:
                                front_tile(nxt, ti_, "B0")
                            elif c == c0 + 4:
                                front_tile(nxt, ti_, "B1")
                            elif c == c0 + 7:
                                front_tile(nxt, ti_, "C")
                stage2(20)
                stage2(21)

            def down(bidx, ti, part="all"):
                T = blocks2[bidx][ti]
                if T == 0:
                    return
                par = bidx % 2
                hb = h1[par][ti]
                hk = [f"h1_{par}_{ti}_0", f"h1_{par}_{ti}_1"]
                for half in (range(2) if part != "fin" else []):
                    for c in range(22):
                        S.op("pe", ("matmul", dict(out=p_d[half][:], lhsT=actT[:, c, ti * 128:(ti + 1) * 128], rhs=wdn[:, c, half * 512:(half + 1) * 512],
                                                  start=(c == 0), stop=(c == 21))),
                             reads=[f"actT{c}"] + WDN, writes=[f"p_u{4 + half}"], sig=(c == 21))
                    S.op("dve", ("tensor_tensor", dict(out=hb[:, half * 512:(half + 1) * 512], in0=p_d[half][:],
                                                      in1=hb[:, half * 512:(half + 1) * 512], op=ALU.add)),
                         reads=[f"p_u{4 + half}", hk[half]], writes=[hk[half]])
                if part == "mm":
                    return
                S.op("act", ("activation", dict(out=xn2[:], in_=hb[:], func=AF.Square, accum_out=ss2[:, 1:2])),
                     reads=hk, writes=["xn2", "ss2b"])
                rstd_ops(ss2[:, 1:2], D, "ss2b")
                S.op("dve", ("scalar_tensor_tensor", dict(out=hb[:], in0=hb[:], scalar=ss2[:, 1:2], in1=fnw_t[:], op0=ALU.mult, op1=ALU.mult)),
                     reads=hk + ["ss2b", "fnw"], writes=hk)
                ev = S.op("sp", ("dma_start", dict(out=out[(T - 1) * 128:T * 128, :], in_=hb[:])),
                          reads=hk, writes=[("out", T)], dsem=ods2[par][ti])
                S.out_events.append(ev)

            front_hist(0)
            for part_ in ("A0", "A", "B0", "B1", "C"):
                front_tile(0, 0, part_)
            for bidx in range(len(blocks2)):
                ffn(bidx)
                down(bidx, 0, "mm")
                if len(blocks2[bidx]) > 1:
                    down(bidx, 1, "mm")
            lastb = len(blocks2) - 1
            for ti_ in range(len(blocks2[lastb])):
                down(lastb, ti_, "fin")
            S.finish()
        S.run()
    return nc


_NC_CACHE = {}


def make_in_maps(inputs, NT, ncores):
    f = lambda a: np.ascontiguousarray(np.asarray(a, dtype=np.float32))
    t = host_tables()
    rep = lambda v: np.ascontiguousarray(np.broadcast_to(f(v).reshape(1, -1), (128, f(v).size)))
    lamv = np.concatenate([f(inputs["lambda_q1"])[0], f(inputs["lambda_k1"])[0], f(inputs["lambda_q2"])[0], f(inputs["lambda_k2"])[0]])
    cw = f(inputs["conv_w"])[0]
    cb = f(inputs["conv_b"])[0]
    convt = np.concatenate([cw, cb[None, :]], 0)
    convt = convt.reshape(4, NCH, 128).transpose(2, 1, 0)
    common = {
        "meta": f(inputs["meta_tokens"]),
        "w_in": f(inputs["w_in"])[0],
        "w_out": f(inputs["w_out"])[0],
        "w_up": f(inputs["w_up"])[0],
        "w_down": f(inputs["w_down"])[0],
        "nw1": rep(inputs["norm1_w"]),
        "nw2": np.ascontiguousarray(f(inputs["norm2_w"]).reshape(8, 128).T),
        "fnw": rep(inputs["final_norm_w"]),
        "lamv": rep(lamv),
        "subw": rep(inputs["da_subln_w"]),
        "gnw": rep(inputs["gla_norm_w"]),
        "gatew": np.ascontiguousarray(np.concatenate([f(inputs["gla_gate_w"])[0], f(inputs["gla_gate_b"])], 0)),
        "convt": np.ascontiguousarray(convt.reshape(128, NCH * 4)),
    }
    common.update(t)
    xx = f(inputs["x"])
    maps = []
    for b in range(ncores):
        m = dict(common)
        m["x"] = np.ascontiguousarray(xx[b])
        maps.append(m)
    return maps


def kernel(**inputs):
    x = np.asarray(inputs["x"])
    B, SEQ, _ = x.shape
    NT = SEQ // 128
    if NT not in _NC_CACHE:
        _NC_CACHE[NT] = build_nc(NT)
    nc = _NC_CACHE[NT]
    maps = make_in_maps(inputs, NT, B)
    res = run_bass_kernel_spmd(nc, maps, core_ids=list(range(B)))
    return np.stack([np.asarray(r["out"]).reshape(SEQ, D) for r in res.results], 0).astype(np.float32)
```
